# Optimizing a Trainium2 kernel written in Bass

```python
import math
import jax, jax.numpy as jnp
from jax import lax
import numpy as np

D_MODEL = 1024
BATCH = 16
SEQ = 256
DEPTH = 1
DEC_BATCH = 8
DEC_SEQ = 1024
PAST_LEN = 512

GRID_W = 64
D_HYENA = 512
D_RWKV = 512
RWKV_HEAD = 64
RWKV_HEADS = D_RWKV // RWKV_HEAD
LORA_W = 64
LORA_A = 64
LORA_G = 128
C_IN = 3 * D_HYENA + 3 * D_RWKV + LORA_W + LORA_A + LORA_G
FILT_BANDS = 16
FILT_FEAT = 1 + 2 * FILT_BANDS
FILT_HIDDEN = 64
N_FILT = 2 * D_HYENA
HYENA_TARGET = 1e-2
HYENA_FAST_PCT = 0.3
HYENA_SLOW_PCT = 1.5
D_FF = 2816
ALPHA = (2.0 * DEPTH) ** 0.25
BETA = (8.0 * DEPTH) ** -0.25
LN_EPS = 1e-5
GN_EPS = 64e-5
NORM_EPS = 1e-12

kernel_name = 'hymba_hyena_rwkv7_prefix_dit'


def _layernorm(x, g=None, b=None):
    xf = x.astype(jnp.float32)
    mu = xf.mean(-1, keepdims=True)
    var = jnp.square(xf - mu).mean(-1, keepdims=True)
    y = (xf - mu) * lax.rsqrt(var + LN_EPS)
    if g is not None:
        y = y * g.astype(jnp.float32) + b.astype(jnp.float32)
    return y.astype(x.dtype)


def _ada(cond, ada_w, ada_b):
    m = jax.nn.silu(cond) @ ada_w + ada_b
    return jnp.split(m[..., None, :], 6, axis=-1)


def _dwconv1d(x, w):
    return lax.conv_general_dilated(x, w[:, None, :].astype(x.dtype), (1,), ((1, 1),),
                                    dimension_numbers=('NWC', 'WIO', 'NWC'),
                                    feature_group_count=x.shape[-1])


def _dwconv2d_grid(x, w):
    B, L, C = x.shape
    rows = L // GRID_W
    xg = x.reshape(B, rows, GRID_W, C)
    y = lax.conv_general_dilated(xg, w[:, :, None, :].astype(x.dtype), (1, 1), ((1, 1), (1, 1)),
                                 dimension_numbers=('NHWC', 'HWIO', 'NHWC'),
                                 feature_group_count=C)
    return y.reshape(B, L, C)


def _hyena_filters(L, filt_w1, filt_b1, filt_w2, filt_b2, filt_w3, filt_freq):
    f32 = jnp.float32
    t = jnp.arange(L, dtype=f32)[:, None] / L
    bands = jnp.arange(1, FILT_BANDS + 1, dtype=f32)[None, :]
    ang = (2.0 * math.pi) * bands * t
    feat = jnp.concatenate([t, jnp.sin(ang), jnp.cos(ang)], axis=-1)
    h = jnp.sin(filt_freq[0].astype(f32) * (feat @ filt_w1.astype(f32) + filt_b1.astype(f32)))
    h = jnp.sin(filt_freq[1].astype(f32) * (h @ filt_w2.astype(f32) + filt_b2.astype(f32)))
    h = h @ filt_w3.astype(f32)
    slow = abs(math.log(HYENA_TARGET) / HYENA_SLOW_PCT)
    fast = abs(math.log(HYENA_TARGET) / HYENA_FAST_PCT)
    deltas = jnp.linspace(slow, fast, N_FILT, dtype=f32)
    h = h * jnp.exp(-t * deltas[None, :])
    h = h.reshape(L, 2, D_HYENA)
    return h / jnp.sum(jnp.abs(h), axis=(0, 1), keepdims=True)


def _fftconv(z, h_fwd, h_bwd, bias):
    L = z.shape[1]
    h_full = jnp.concatenate([h_fwd, jnp.zeros_like(h_fwd[:1]), h_bwd[:0:-1]], axis=0)
    Hf = jnp.fft.rfft(h_full, n=2 * L, axis=0)
    Zf = jnp.fft.rfft(z, n=2 * L, axis=1)
    y = jnp.fft.irfft(Zf * Hf[None], n=2 * L, axis=1)[:, :L]
    return y + z * bias


def _rwkv_scan(r, w, k, v, a_vec, b_vec, s0, reverse):
    def step(S, inp):
        r_t, w_t, k_t, v_t, a_t, b_t = inp
        sa = jnp.einsum('bhvk,bhk->bhv', S, a_t)
        S = S * w_t[:, :, None, :] + sa[..., None] * b_t[:, :, None, :] + v_t[..., None] * k_t[:, :, None, :]
        return S, jnp.einsum('bhvk,bhk->bhv', S, r_t)
    xs = tuple(jnp.moveaxis(t, 1, 0) for t in (r, w, k, v, a_vec, b_vec))
    S, ys = lax.scan(step, s0, xs, reverse=reverse)
    return jnp.moveaxis(ys, 0, 1), S


def _rwkv7(r, k, v, wd, ad, gd, s0, p):
    f32 = jnp.float32
    B, L, _ = r.shape
    heads = lambda t: t.reshape(B, L, RWKV_HEADS, RWKV_HEAD)
    r, k, v, wd, ad, gd = (t.astype(f32) for t in (r, k, v, wd, ad, gd))
    g = jax.nn.sigmoid(gd) @ p['rwkv_g_lora'].astype(f32)
    kk = heads(k * p['rwkv_k_k'].astype(f32))
    kk = kk / jnp.maximum(jnp.sqrt(jnp.sum(kk * kk, -1, keepdims=True)), NORM_EPS)
    rh, vh = heads(r), heads(v)
    y = jnp.zeros_like(rh)
    bonus = jnp.zeros_like(rh)
    states = []
    for d in range(2):
        w_raw = p['rwkv_w0'][d].astype(f32) + jnp.tanh(wd) @ p['rwkv_w_lora'][d].astype(f32)
        decay = jnp.exp(-jnp.exp(-jax.nn.softplus(-w_raw) - 0.5))
        a = jax.nn.sigmoid(p['rwkv_a0'][d].astype(f32) + ad @ p['rwkv_a_lora'][d].astype(f32))
        k_d = heads(k * (1.0 + (a - 1.0) * p['rwkv_k_a'].astype(f32)))
        y_d, S_d = _rwkv_scan(rh, heads(decay), k_d, vh, -kk, kk * heads(a),
                              s0[:, d].astype(f32), reverse=(d == 1))
        y = y + y_d
        bonus = bonus + jnp.sum(rh * k_d * p['rwkv_r_k'].astype(f32), -1, keepdims=True) * vh
        states.append(S_d)
    mu = y.mean(-1, keepdims=True)
    var = jnp.square(y - mu).mean(-1, keepdims=True)
    y = ((y - mu) * lax.rsqrt(var + GN_EPS)).reshape(B, L, D_RWKV)
    y = y * p['lnx_g'].astype(f32) + p['lnx_b'].astype(f32) + bonus.reshape(B, L, D_RWKV)
    return y * g, jnp.stack(states, axis=1)


def _layer(x, cond, s0, on_grid, p):
    dt = x.dtype
    L = x.shape[1]
    sh1, sc1, g1, sh2, sc2, g2 = _ada(cond, p['ada_w'], p['ada_b'])
    h = _layernorm(x) * (1.0 + sc1) + sh1
    P = _dwconv1d(h @ p['w_in'], p['conv_in'])
    splits = [int(s) for s in np.cumsum([D_HYENA] * 3 + [D_RWKV] * 3 + [LORA_W, LORA_A])]
    hv, hx0, hx1, r, k, v, wd, ad, gd = jnp.split(P, splits, axis=-1)
    filt = _hyena_filters(L, p['filt_w1'], p['filt_b1'], p['filt_w2'], p['filt_b2'],
                          p['filt_w3'], p['filt_freq'])
    f32 = jnp.float32
    y_h = hx0.astype(f32) * _fftconv(hx1.astype(f32) * hv.astype(f32), filt[:, 0], filt[:, 1],
                                      p['hyena_bias'].astype(f32))
    y_r, s_final = _rwkv7(r, k, v, wd, ad, gd, s0, p)
    mix = jnp.concatenate([y_h, y_r], axis=-1).astype(dt) @ p['w_out']
    x = _layernorm(ALPHA * x + g1 * mix, p['ln1_g'], p['ln1_b'])
    h = _layernorm(x) * (1.0 + sc2) + sh2
    u, gt = jnp.split(h @ p['w_ffn_up'], 2, axis=-1)
    if on_grid:
        u = _dwconv2d_grid(u, p['ffn_conv_w'])
    else:
        u = _dwconv1d(u, p['ffn_conv_w'][1])
    u = u + p['ffn_conv_b']
    ff = (jax.nn.gelu(u) * gt) @ p['w_ffn_down']
    x = _layernorm(ALPHA * x + g2 * ff, p['ln2_g'], p['ln2_b'])
    return x, s_final


def setup_inputs(seed: int = 0) -> dict:
    key = jax.random.key(seed)
    ks = iter(jax.random.split(key, 48))
    f32 = jnp.float32
    nrm = lambda shape, scale: jax.random.normal(next(ks), shape, f32) * scale
    Ld = DEPTH
    conv3_base = jnp.zeros((3,), f32).at[1].set(0.5).at[0].set(0.25).at[2].set(0.25)
    conv33_base = jnp.zeros((3, 3), f32).at[1, 1].set(1.0)
    return {
        'x_prompt': nrm((BATCH, SEQ, D_MODEL), 1.0),
        'x_sample': nrm((DEC_BATCH, DEC_SEQ, D_MODEL), 1.0),
        'state_rwkv': nrm((DEC_BATCH, DEPTH, 2, RWKV_HEADS, RWKV_HEAD, RWKV_HEAD), 0.5),
        'c': nrm((DEC_BATCH, D_MODEL), 1.0),
        'c_ctx': nrm((D_MODEL,), 1.0),
        'ada_w': nrm((Ld, D_MODEL, 6 * D_MODEL), 0.5 * D_MODEL ** -0.5),
        'ada_b': nrm((Ld, 6 * D_MODEL), 0.01),
        'w_in': nrm((Ld, D_MODEL, C_IN), D_MODEL ** -0.5),
        'conv_in': conv3_base[None, :, None] + nrm((Ld, 3, C_IN), 0.1),
        'filt_w1': nrm((Ld, FILT_FEAT, FILT_HIDDEN), FILT_FEAT ** -0.5),
        'filt_b1': nrm((Ld, FILT_HIDDEN), 0.1),
        'filt_w2': nrm((Ld, FILT_HIDDEN, FILT_HIDDEN), FILT_HIDDEN ** -0.5),
        'filt_b2': nrm((Ld, FILT_HIDDEN), 0.1),
        'filt_w3': nrm((Ld, FILT_HIDDEN, N_FILT), FILT_HIDDEN ** -0.5),
        'filt_freq': 1.0 + nrm((Ld, 2, FILT_HIDDEN), 0.1),
        'hyena_bias': nrm((Ld, D_HYENA), 0.1),
        'rwkv_w0': jax.random.uniform(next(ks), (Ld, 2, D_RWKV), f32, -5.0, 1.0),
        'rwkv_w_lora': nrm((Ld, 2, LORA_W, D_RWKV), LORA_W ** -0.5),
        'rwkv_a0': nrm((Ld, 2, D_RWKV), 0.5),
        'rwkv_a_lora': nrm((Ld, 2, LORA_A, D_RWKV), LORA_A ** -0.5),
        'rwkv_g_lora': nrm((Ld, LORA_G, D_RWKV), LORA_G ** -0.5),
        'rwkv_k_k': 0.85 + nrm((Ld, D_RWKV), 0.05),
        'rwkv_k_a': 1.0 + nrm((Ld, D_RWKV), 0.05),
        'rwkv_r_k': nrm((Ld, RWKV_HEADS, RWKV_HEAD), 0.1),
        'lnx_g': 1.0 + nrm((Ld, D_RWKV), 0.05),
        'lnx_b': nrm((Ld, D_RWKV), 0.01),
        'w_out': nrm((Ld, D_MODEL, D_MODEL), BETA * D_MODEL ** -0.5),
        'ln1_g': 1.0 + nrm((Ld, D_MODEL), 0.05),
        'ln1_b': nrm((Ld, D_MODEL), 0.01),
        'w_ffn_up': nrm((Ld, D_MODEL, 2 * D_FF), D_MODEL ** -0.5),
        'ffn_conv_w': conv33_base[None, :, :, None] + nrm((Ld, 3, 3, D_FF), 0.1),
        'ffn_conv_b': nrm((Ld, D_FF), 0.01),
        'w_ffn_down': nrm((Ld, D_FF, D_MODEL), BETA * D_FF ** -0.5),
        'ln2_g': 1.0 + nrm((Ld, D_MODEL), 0.05),
        'ln2_b': nrm((Ld, D_MODEL), 0.01),
    }


def reference(x_prompt, x_sample, state_rwkv, c, c_ctx, ada_w, ada_b, w_in, conv_in,
              filt_w1, filt_b1, filt_w2, filt_b2, filt_w3, filt_freq, hyena_bias,
              rwkv_w0, rwkv_w_lora, rwkv_a0, rwkv_a_lora, rwkv_g_lora, rwkv_k_k, rwkv_k_a,
              rwkv_r_k, lnx_g, lnx_b, w_out, ln1_g, ln1_b, w_ffn_up, ffn_conv_w, ffn_conv_b,
              w_ffn_down, ln2_g, ln2_b):
    stacked = dict(ada_w=ada_w, ada_b=ada_b, w_in=w_in, conv_in=conv_in,
                   filt_w1=filt_w1, filt_b1=filt_b1, filt_w2=filt_w2, filt_b2=filt_b2,
                   filt_w3=filt_w3, filt_freq=filt_freq, hyena_bias=hyena_bias,
                   rwkv_w0=rwkv_w0, rwkv_w_lora=rwkv_w_lora, rwkv_a0=rwkv_a0,
                   rwkv_a_lora=rwkv_a_lora, rwkv_g_lora=rwkv_g_lora, rwkv_k_k=rwkv_k_k,
                   rwkv_k_a=rwkv_k_a, rwkv_r_k=rwkv_r_k, lnx_g=lnx_g, lnx_b=lnx_b,
                   w_out=w_out, ln1_g=ln1_g, ln1_b=ln1_b, w_ffn_up=w_ffn_up,
                   ffn_conv_w=ffn_conv_w, ffn_conv_b=ffn_conv_b, w_ffn_down=w_ffn_down,
                   ln2_g=ln2_g, ln2_b=ln2_b)
    b_ctx = x_prompt.shape[0]
    zero_state = jnp.zeros((b_ctx, 2, RWKV_HEADS, RWKV_HEAD, RWKV_HEAD), jnp.float32)
    y_prompt = x_prompt
    y_sample = x_sample
    ctx_states = []
    for layer in range(DEPTH):
        p = {name: arr[layer] for name, arr in stacked.items()}
        y_prompt, s_ctx = _layer(y_prompt, c_ctx, zero_state, False, p)
        ctx_states.append(s_ctx)
        y_sample, _ = _layer(y_sample, c, state_rwkv[:, layer], True, p)
    new_state_rwkv = jnp.stack(ctx_states, axis=1)
    return (y_prompt, y_sample, new_state_rwkv)
```

```python
import math
import contextlib
import numpy as np
import ml_dtypes
import concourse.bass as bass
import concourse.mybir as mybir
from concourse.bass_utils import run_bass_kernel_spmd

F32 = mybir.dt.float32
BF16 = mybir.dt.bfloat16
I32 = mybir.dt.int32
F32R = mybir.dt.float32r
AF = mybir.ActivationFunctionType
ALU = mybir.AluOpType
AX = mybir.AxisListType

D = 1024
CIN = 3328
DFF = 2816
NFC = 22
LN_EPS = 1e-5
GN_EPS = 64e-5
ALPHA = 2.0 ** 0.25
TWO_PI = 6.283185
NDS = 24
ATTACH_WAIT = True
NHW = 16


class Sched:
    def __init__(self, nc):
        self.nc = nc
        self.e = dict(pe=nc.tensor, act=nc.scalar, dve=nc.vector, pool=nc.gpsimd, sp=nc.sync)
        self.sem = {k: nc.alloc_semaphore("sem_" + k) for k in ('pe', 'act', 'dve', 'pool')}
        self.cnt = {k: 0 for k in self.sem}
        self.seen = {k: {} for k in self.e}
        self.dsem = [nc.alloc_semaphore("dsem%d" % i) for i in range(NDS)]
        self.dtot = [0] * NDS
        self.dnext = {'hw': 0, 'sw': 0}
        self.W = {}
        self.R = {}
        self.nwaits = 0
        self.mute = False

    def _wait(self, c, tok, defer=None):
        if tok is None:
            return
        if tok[0] == 'c':
            _, p, idx = tok
            if p == c:
                if c == 'pe':
                    return
            if self.seen[c].get(p, 0) >= idx:
                return
            sem, val = self.sem[p], idx
            self.seen[c][p] = idx
        else:
            _, j, tot = tok
            key = ('d', j)
            if self.seen[c].get(key, 0) >= tot:
                return
            sem, val = self.dsem[j], tot
            self.seen[c][key] = tot
        if defer is not None:
            defer.append((sem, val))
        else:
            self.e[c].wait_ge(sem, val)
        self.nwaits += 1

    def _deps(self, reads, writes):
        deps = []
        for k in reads:
            if k in self.W:
                deps.append(self.W[k])
        for k in writes:
            if k in self.W:
                deps.append(self.W[k])
            deps.extend(self.R.get(k, ()))
        return deps

    def _commit(self, tok, reads, writes):
        for k in reads:
            self.R.setdefault(k, []).append(tok)
        for k in writes:
            self.W[k] = tok
            self.R[k] = []

    def op(self, c, fn, reads=(), writes=()):
        if self.mute:
            return None
        pr = [k for k in reads if isinstance(k, tuple) and k[0] == 'ps']
        if pr:
            writes = list(writes) + [k for k in pr if k not in writes]
        pend = []
        for d in self._deps(reads, writes):
            self._wait(c, d, defer=pend)
        attach = ATTACH_WAIT and len(pend) > 0
        for sem, val in (pend[:-1] if attach else pend):
            self.e[c].wait_ge(sem, val)
        ins = fn(self.e[c])
        if attach:
            ins._wait_ge(pend[-1][0], pend[-1][1])
        self.cnt[c] += 1
        ins.then_inc(self.sem[c], 1)
        tok = ('c', c, self.cnt[c])
        self._commit(tok, reads, writes)
        return tok

    def dma(self, q, out, in_, reads=(), writes=()):
        if self.mute:
            return None
        if q == 'pool':
            j = NHW + self.dnext['sw']
            self.dnext['sw'] = (self.dnext['sw'] + 1) % (NDS - NHW)
        else:
            j = self.dnext['hw']
            self.dnext['hw'] = (self.dnext['hw'] + 1) % NHW
        if self.dtot[j] > 0:
            self._wait(q, ('d', j, self.dtot[j]))
        for d in self._deps(reads, writes):
            self._wait(q, d)
        self.e[q].dma_start(out=out, in_=in_).then_inc(self.dsem[j], 16)
        self.dtot[j] += 16
        tok = ('d', j, self.dtot[j])
        self._commit(tok, reads, writes)
        return tok

    def barrier(self):
        if self.mute:
            return
        for c in self.e:
            for p in self.sem:
                if p != c and self.cnt[p] > 0:
                    self._wait(c, ('c', p, self.cnt[p]))
            for j in range(NDS):
                if self.dtot[j] > 0:
                    self._wait(c, ('d', j, self.dtot[j]))
        self.W.clear()
        self.R.clear()

    def finish(self):
        for j in range(NDS):
            if self.dtot[j] > 0:
                self._wait('sp', ('d', j, self.dtot[j]))
        for p in self.sem:
            if self.cnt[p] > 0:
                self._wait('sp', ('c', p, self.cnt[p]))


def _dft_tables(L):
    d = np.arange(L, dtype=np.float64)
    ang = np.pi * np.outer(d, d) / L
    C = np.cos(ang)
    S = np.sin(ang)
    alt = (-1.0) ** d
    S[:, 0] = alt
    n = L // 128
    Cl = C.reshape(n, 128, L).transpose(1, 0, 2)
    Sl = S.reshape(n, 128, L).transpose(1, 0, 2)
    Sgl = S.T.reshape(n, 128, L).transpose(1, 0, 2)
    wk = np.full((L,), 1.0 / L)
    wk[0] = 1.0 / (2 * L)
    wkl = wk.reshape(n, 128).T
    return (np.ascontiguousarray(Cl).astype(ml_dtypes.bfloat16), np.ascontiguousarray(Sl).astype(ml_dtypes.bfloat16),
            np.ascontiguousarray(Sgl).astype(ml_dtypes.bfloat16), np.ascontiguousarray(wkl).astype(np.float32))


def _feat_table(L):
    t = np.arange(L, dtype=np.float32)[:, None] / np.float32(L)
    bands = np.arange(1, 17, dtype=np.float32)[None, :]
    ang = np.float32(2.0 * math.pi) * bands * t
    feat = np.concatenate([t, np.sin(ang), np.cos(ang)], axis=-1).astype(np.float32)
    tcol = (np.arange(L, dtype=np.float32) / np.float32(L)).reshape(L // 128, 128).T
    ftp = np.zeros((64, L), np.float32)
    ftp[:33] = feat.T
    return ftp, np.ascontiguousarray(tcol)


def _consts():
    c = {}
    c['ident'] = np.eye(128, dtype=np.float32)
    sel = np.zeros((2, 2, 128), np.float32)
    sel[0, 0, :] = 1.0
    sel[1, 1, :] = 1.0
    c['selc'] = sel
    for L in (1024, 256):
        Cl, Sl, Sgl, wk = _dft_tables(L)
        c['Cm%d' % L] = Cl
        c['Sm%d' % L] = Sl
        c['Sg%d' % L] = Sgl
        c['wk%d' % L] = wk
        ft, tcol = _feat_table(L)
        c['featT%d' % L] = ft
        c['tcol%d' % L] = tcol
    slow = abs(math.log(1e-2) / 1.5)
    fast = abs(math.log(1e-2) / 0.3)
    deltas = np.linspace(slow, fast, 1024, dtype=np.float32)
    c['negdelta'] = np.ascontiguousarray(np.broadcast_to(-deltas[None, :], (128, 1024))).astype(np.float32)
    sm = np.ones((128, 1024), np.float32)
    sm[:, ::32] = 0.0
    c['scanmask'] = sm
    t = np.arange(128)
    same = (t[:, None] // 32) == (t[None, :] // 32)
    c['mstrict'] = (same & (t[None, :] < t[:, None])).astype(np.float32)
    c['mincl'] = (same & (t[None, :] <= t[:, None])).astype(np.float32)
    c['mstrictT'] = np.ascontiguousarray(c['mstrict'].T)
    c['minclT'] = np.ascontiguousarray(c['mincl'].T)
    c['ident4'] = np.ascontiguousarray(np.broadcast_to(np.eye(128, dtype=np.float32)[:, None, :], (128, 4, 128)))
    c['rowmask'] = (t[:, None] // 32 == np.arange(4)[None, :]).astype(np.float32)
    return c


CONST_DT = {'Cm1024': BF16, 'Sm1024': BF16, 'Cm256': BF16, 'Sm256': BF16}


def _col(v, nchunk):
    return np.ascontiguousarray(np.asarray(v, np.float32).reshape(nchunk, 128).T)


class Builder:
    def __init__(self, dbg=()):
        self.nc = bass.Bass("TRN2", target_bir_lowering=False)
        self.S = Sched(self.nc)
        self.dbg = set(dbg)
        self.din = {}
        self.dout = {}
        self.es = contextlib.ExitStack()
        self.es_dft = contextlib.ExitStack()
        self.es_aw = contextlib.ExitStack()
        self.ps = [self.nc.alloc_psum_tensor("ps%d" % i, [128, 512], F32) for i in range(8)]
        self.psi = 0
        self.ps_reserved = set()

    def inp(self, name, shape, dt=F32):
        t = self.nc.dram_tensor("i_" + name, list(shape), dt, kind="ExternalInput").ap()
        self.din[name] = t
        return t

    def outp(self, name, shape, dt=F32):
        t = self.nc.dram_tensor("o_" + name, list(shape), dt, kind="ExternalOutput").ap()
        self.dout[name] = t
        return t

    def sb(self, name, shape, dt=F32):
        return self.nc.alloc_sbuf_tensor(name, list(shape), dt)

    def st(self, name, shape, dt=F32):
        self.uid = getattr(self, 'uid', 0) + 1
        return self.nc.sbuf_tensor("%s_u%d" % (name, self.uid), list(shape), dt)

    def psum(self):
        i = self.psi
        self.psi = (self.psi + 1) % 8
        if i in self.ps_reserved:
            return self.psum()
        return self.ps[i], ('ps', i)

    def dump(self, name, tile_ap, shape, key, dt=F32):
        if name not in self.dbg:
            return
        o = self.outp("dbg_" + name, shape, dt)
        self.S.dma('sp', o, tile_ap, reads=key if isinstance(key, list) else [key])


def phase_mod(B):
    nc, S = B.nc, B.S
    cT_d = B.inp('cT', [128, 8, 2])
    B.ada_w_d = B.inp('ada_w', [1024, 6144]).rearrange("(kc p) n -> p kc n", p=128)
    ada_bc_d = B.inp('ada_bc', [128, 48])
    ada_b_d = B.inp('ada_b', [6144])
    ident_d = B.inp('ident', [128, 128])
    sel_d = B.inp('selc', [2, 2, 128])

    B.modc = B.sb('modc', [128, 48, 2])
    B.gbc = B.sb('gbc', [128, 2, 2, 1024])
    B.ident = B.sb('ident_s', [128, 128])
    B.ones = B.sb('ones_s', [128, 128])
    B.identb = B.sb('identb_s', [128, 128], BF16)
    B.onesb = B.sb('onesb_s', [128, 2], BF16)
    B.adab = B.sb('adab_s', [128, 48])
    B.cTb = B.sb('cTb_s', [128, 8, 2], BF16)
    B.selc = B.sb('selc_s', [2, 2, 128])
    S.dma('sp', B.ident[:], ident_d, writes=['ident'])
    S.dma('sp', B.selc[:], sel_d, writes=['selc'])
    S.dma('sp', B.adab[:], ada_bc_d, writes=['adab'])
    B.ada_b_d = ada_b_d
    S.op('pool', lambda e: e.memset(B.ones[:], 1.0), writes=['ones'])
    S.op('pool', lambda e: e.memset(B.onesb[:], 1.0), writes=['onesb'])
    S.op('pool', lambda e: e.memset(B.modc[:], 0.0), writes=['modc'])
    S.op('dve', lambda e: e.tensor_copy(out=B.identb[:], in_=B.ident[:]), reads=['ident'], writes=['identb'])
    with contextlib.ExitStack() as es:
        cTs = es.enter_context(B.st("cTs", [128, 8, 2], F32))
        S.dma('sp', cTs[:], cT_d, writes=['cTs'])
        S.op('act', lambda e: e.activation(out=B.cTb[:], in_=cTs[:], func=AF.Silu), reads=['cTs'], writes=['cTb'])
        for lo in (8, 32):
            S.op('dve', lambda e, lo=lo: e.tensor_scalar_add(out=B.adab[:, lo:lo + 8], in0=B.adab[:, lo:lo + 8], scalar1=1.0),
                 reads=['adab'], writes=['adab'])
        S.barrier()


def start_ada_stream(B):
    B.aw = [B.es_aw.enter_context(B.st("aw%d" % i, [128, 8, 512], BF16)) for i in range(4)]
    for blk in range(4):
        B.S.dma('pool', B.aw[blk][:], B.ada_w_d[:, :, blk * 512:(blk + 1) * 512], writes=[('aw', blk)])


def mod_part(B, part):
    nc, S = B.nc, B.S
    with contextlib.ExitStack() as es:
        mrow = es.enter_context(B.st("mrow", [2, 512], F32))
        gbias = es.enter_context(B.st("gbias", [128, 1024], F32))
        if part == 1:
            S.dma('sp', gbias[:], B.ada_b_d[2048:3072].partition_broadcast(128), writes=['gbias'])
        if part == 2:
            S.dma('sp', gbias[:], B.ada_b_d[5120:6144].partition_broadcast(128), writes=['gbias'])
        for blk in range(part * 4, part * 4 + 4):
            buf, bk = B.aw[blk % 4], ('aw', blk % 4)
            ps, pk = B.psum()
            for kc in range(8):
                S.op('pe', lambda e, kc=kc, ps=ps: e.matmul(ps[0:2, :], B.cTb[:, kc, :], buf[:, kc, :], start=(kc == 0), stop=(kc == 7)),
                     reads=[bk, 'cTb'], writes=[pk])
            if part < 2:
                S.dma('pool', buf[:], B.ada_w_d[:, :, (blk + 4) * 512:(blk + 5) * 512], writes=[bk])
            if blk in (4, 5, 10, 11):
                gi, half = (0 if blk < 6 else 1), blk % 2
                S.op('act', lambda e, ps=ps: e.copy(out=mrow[:], in_=ps[0:2, :]), reads=[pk], writes=['mrow'])
                for c in range(2):
                    ps2, pk2 = B.psum()
                    S.op('pe', lambda e, c=c, ps2=ps2: e.matmul(ps2[:, :], B.selc[:, c, :], mrow[:], start=True, stop=True),
                         reads=['selc', 'mrow'], writes=[pk2])
                    S.op('dve', lambda e, c=c, ps2=ps2: e.tensor_tensor(out=B.gbc[:, c, gi, half * 512:(half + 1) * 512], in0=ps2[:, :],
                                                                        in1=gbias[:, half * 512:(half + 1) * 512], op=ALU.add),
                         reads=[pk2, 'gbias'], writes=[('gbc', c, gi, half)])
            else:
                S.op('act', lambda e, ps=ps: e.copy(out=mrow[:], in_=ps[0:2, :]), reads=[pk], writes=['mrow'])
                pt, pkt = B.psum()
                for fcl in range(4):
                    S.op('pe', lambda e, fcl=fcl, pt=pt: e.transpose(pt[:, fcl * 2:fcl * 2 + 2], mrow[:, fcl * 128:(fcl + 1) * 128], B.ident[0:2, 0:2]),
                         reads=['mrow', 'ident'], writes=[pkt])
                for c in range(2):
                    S.op('dve', lambda e, c=c, pt=pt: e.tensor_tensor(out=B.modc[:, blk * 4:(blk + 1) * 4, c], in0=pt[:, c:8:2],
                                                                      in1=B.adab[:, blk * 4:(blk + 1) * 4], op=ALU.add),
                         reads=[pkt, 'adab'], writes=['modc'])
        S.barrier()
    if part == 2:
        B.dump('modc', B.modc[:], [128, 48, 2], 'modc')
        B.dump('gbc', B.gbc[:, 0, 0, :], [128, 1024], ('gbc', 0, 0, 1))


def alloc_dft(B):
    nc, S = B.nc, B.S
    B.fin = dict(w1=B.inp('filt_w1', [64, 64]), w2=B.inp('filt_w2', [64, 64]), w3=B.inp('filt_w3', [64, 1024]),
                 fc=B.inp('filt_cols', [64, 4]), nd=B.inp('negdelta', [128, 1024]))
    B.fin.update({'ft%d' % L_: B.inp('featT%d' % L_, [64, L_]) for L_ in (1024, 256)})
    B.fin.update({'tc%d' % L_: B.inp('tcol%d' % L_, [128, L_ // 128]) for L_ in (1024, 256)})
    B.Cm, B.Sm, B.Sg, B.wk, B.Hr, B.Hi, B.rn = {}, {}, {}, {}, {}, {}, {}
    for L in (1024, 256):
        nT = L // 128
        TD = lambda name, shape, dt=F32: B.es_dft.enter_context(B.st(name, shape, dt))
        B.Cm[L] = TD('Cm%d' % L, [128, nT, L], BF16)
        B.Sm[L] = TD('Sm%d' % L, [128, nT, L], BF16)
        B.wk[L] = TD('wk%d' % L, [128, nT])
        B.Hr[L] = TD('Hr%d' % L, [128, nT, 512], BF16)
        B.Hi[L] = TD('Hi%d' % L, [128, nT, 512], BF16)
        B.rn[L] = TD('rn%d' % L, [128, 4])
        S.dma('act', B.Cm[L][:], B.inp('Cm%d' % L, [128, nT, L], BF16), writes=[('Cm', L)])
        S.dma('act', B.Sm[L][:], B.inp('Sm%d' % L, [128, nT, L], BF16), writes=[('Sm', L)])
        B.Sg[L] = TD('Sg%d' % L, [128, nT, L], BF16)
        S.dma('act', B.Sg[L][:], B.inp('Sg%d' % L, [128, nT, L], BF16), writes=[('Sg', L)])
        S.dma('sp', B.wk[L][:], B.inp('wk%d' % L, [128, nT]), writes=[('wk', L)])


def phase_filters(B, Ls):
    nc, S = B.nc, B.S
    with contextlib.ExitStack() as es:
        def T(name, shape, dt=F32):
            return es.enter_context(B.st(name, shape, dt))
        w1s = T('w1s', [64, 64]); w2s = T('w2s', [64, 64]); w3s = T('w3s', [64, 1024]); fcs = T('fcs', [64, 4])
        sc = T('fsc', [64, 4])
        nds = T('nds', [128, 1024])
        S.dma('sp', w1s[:], B.fin['w1'], writes=['w1s'])
        S.dma('sp', w2s[:], B.fin['w2'], writes=['w2s'])
        S.dma('sp', w3s[:], B.fin['w3'], writes=['w3s'])
        S.dma('sp', fcs[:], B.fin['fc'], writes=['fcs'])
        S.dma('sp', nds[:], B.fin['nd'], writes=['nds'])
        inv2pi = 1.0 / (2.0 * math.pi)
        for l in range(2):
            S.op('dve', lambda e, l=l: e.tensor_scalar(out=sc[:, 2 * l:2 * l + 1], in0=fcs[:, 2 * l:2 * l + 1],
                                                        scalar1=inv2pi, scalar2=None, op0=ALU.mult),
                 reads=['fcs'], writes=['fsc'])
            S.op('dve', lambda e, l=l: e.tensor_tensor(out=sc[:, 2 * l + 1:2 * l + 2], in0=sc[:, 2 * l:2 * l + 1],
                                                        in1=fcs[:, 2 * l + 1:2 * l + 2], op=ALU.mult),
                 reads=['fcs', 'fsc'], writes=['fsc'])
        for L in Ls:
            nT = L // 128
            with contextlib.ExitStack() as es2:
                def T2(name, shape, dt=F32):
                    return es2.enter_context(B.st(name + str(L), shape, dt))
                ft = T2('ft', [64, L]); tcs = T2('tcs', [128, nT])
                hA = T2('hA', [64, L]); hB = T2('hB', [64, L])
                ytmp = T2('ytmp', [64, 512]); yi = T2('yi', [64, 512], I32); yf = T2('yf', [64, 512])
                hfull = [T2('hfull%d' % i, [128, 1024]) for i in range(2)]
                win = T2('win', [128, 512]); absb = T2('absb', [128, 1024], BF16)
                hp = T2('hp', [128, nT, 512], BF16); hm = T2('hm', [128, nT, 512], BF16)
                S.dma('sp', ft[:], B.fin['ft%d' % L], writes=['ft'])
                S.dma('sp', tcs[:], B.fin['tc%d' % L], writes=['tcs'])
                tw = min(L, 512)
                for layer in range(2):
                    src, dst = (ft, hA) if layer == 0 else (hA, hB)
                    wl = w1s if layer == 0 else w2s
                    for tt in range(L // tw):
                        sl = slice(tt * tw, (tt + 1) * tw)
                        ps, pk = B.psum()
                        S.op('pe', lambda e, ps=ps, sl=sl, src=src, wl=wl: e.matmul(ps[0:64, 0:tw], wl[:, :], src[:, sl],
                                                                                   start=True, stop=True),
                             reads=['w1s', 'w2s', 'ft', ('h', layer - 1)], writes=[pk])
                        S.op('act', lambda e, ps=ps, layer=layer: e.activation(
                            out=ytmp[:, 0:tw], in_=ps[0:64, 0:tw], func=AF.Identity,
                            scale=sc[:, 2 * layer:2 * layer + 1], bias=sc[:, 2 * layer + 1:2 * layer + 2]),
                            reads=[pk, 'fsc'], writes=['ytmp'])
                        S.op('dve', lambda e: e.tensor_copy(out=yi[:, 0:tw], in_=ytmp[:, 0:tw]), reads=['ytmp'], writes=['yi'])
                        S.op('dve', lambda e: e.tensor_copy(out=yf[:, 0:tw], in_=yi[:, 0:tw]), reads=['yi'], writes=['yf'])
                        S.op('dve', lambda e: e.tensor_tensor(out=ytmp[:, 0:tw], in0=ytmp[:, 0:tw], in1=yf[:, 0:tw],
                                                              op=ALU.subtract), reads=['ytmp', 'yf'], writes=['ytmp'])
                        S.op('act', lambda e, dst=dst, sl=sl: e.activation(out=dst[:, sl], in_=ytmp[:, 0:tw], func=AF.Sin,
                                                                            scale=TWO_PI),
                             reads=['ytmp'], writes=[('h', layer)])
                B.dump('hA%d' % L, hA[:], [64, L], ('h', 0))
                B.dump('hB%d' % L, hB[:], [64, L], ('h', 1))
                psn, pkn = B.psum()
                B.ps_reserved.add(pkn[1])
                first = True
                for tb in range(nT):
                    hf, hfk = hfull[tb % 2], ('hfull', tb % 2)
                    for half in range(2):
                        hs = slice(half * 512, (half + 1) * 512)
                        ps, pk = B.psum()
                        S.op('pe', lambda e, ps=ps, tb=tb, hs=hs: e.matmul(ps[:, :], hB[:, tb * 128:(tb + 1) * 128], w3s[:, hs],
                                                                           start=True, stop=True),
                             reads=[('h', 1), 'w3s'], writes=[pk])
                        S.op('act', lambda e, tb=tb, hs=hs: e.activation(out=win[:], in_=nds[:, hs], func=AF.Exp,
                                                                          scale=tcs[:, tb:tb + 1]),
                             reads=['nds', 'tcs'], writes=['win'])
                        S.op('dve', lambda e, ps=ps, hf=hf, hs=hs: e.tensor_tensor(out=hf[:, hs], in0=ps[:, :], in1=win[:], op=ALU.mult),
                             reads=[pk, 'win'], writes=[hfk])
                    S.op('act', lambda e, hf=hf: e.activation(out=absb[:], in_=hf[:], func=AF.Abs), reads=[hfk], writes=['absb'])
                    for side in range(2):
                        for cc in range(4):
                            c0 = side * 512 + cc * 128
                            S.op('pe', lambda e, c0=c0, cc=cc, first=first: e.matmul(
                                psn[:, cc * 2:cc * 2 + 2], absb[:, c0:c0 + 128], B.identb[:, 0:2] if False else B.onesb[:, 0:2],
                                start=first, stop=(tb == nT - 1 and side == 1), skip_group_check=True),
                                reads=['absb', 'onesb'], writes=[pkn])
                            first = False
                    if tb == 0:
                        S.op('dve', lambda e, hf=hf: e.memset(hf[0:1, 512:1024], 0.0), reads=['absb'], writes=[hfk])
                    S.op('pool', lambda e, tb=tb, hf=hf: e.tensor_tensor(out=hp[:, tb, :], in0=hf[:, 0:512], in1=hf[:, 512:1024], op=ALU.add),
                         reads=[hfk], writes=[('hp', tb)])
                    S.op('pool', lambda e, tb=tb, hf=hf: e.tensor_tensor(out=hm[:, tb, :], in0=hf[:, 512:1024], in1=hf[:, 0:512], op=ALU.subtract),
                         reads=[hfk], writes=[('hm', tb)])
                S.op('dve', lambda e: e.reciprocal(out=B.rn[L][:], in_=psn[:, 0:8:2]), reads=[pkn], writes=[('rn', L)])
                B.ps_reserved.discard(pkn[1])
                for (mat, mk, src, sk, dst, dk) in ((B.Cm[L], ('Cm', L), hp, 'hp', B.Hr[L], ('Hr', L)),
                                                    (B.Sm[L], ('Sm', L), hm, 'hm', B.Hi[L], ('Hi', L))):
                    for kc in range(nT):
                        ps, pk = B.psum()
                        for tb in range(nT):
                            S.op('pe', lambda e, ps=ps, tb=tb, kc=kc, mat=mat, src=src: e.matmul(
                                ps[:, :], mat[:, tb, kc * 128:(kc + 1) * 128], src[:, tb, :],
                                start=(tb == 0), stop=(tb == nT - 1)), reads=[mk, (sk, tb)], writes=[pk])
                        S.op('act', lambda e, ps=ps, kc=kc, dst=dst: e.activation(
                            out=dst[:, kc, :], in_=ps[:, :], func=AF.Identity, scale=B.wk[L][:, kc:kc + 1]),
                            reads=[pk, ('wk', L)], writes=[dk])
                ps, pk = B.psum()
                for tb in range(nT):
                    S.op('pe', lambda e, ps=ps, tb=tb: e.matmul(ps[0:1, :], B.Sm[L][:, tb, 0:1], hp[:, tb, :],
                                                                start=(tb == 0), stop=(tb == nT - 1)),
                         reads=[('Sm', L), ('hp', tb)], writes=[pk])
                S.op('act', lambda e, ps=ps: e.activation(out=B.Hi[L][0:1, 0, :], in_=ps[0:1, :], func=AF.Identity,
                                                          scale=B.wk[L][0:1, 0:1]),
                     reads=[pk, ('wk', L)], writes=[('Hi', L)])
                B.dump('Hr%d' % L, B.Hr[L][:], [128, nT, 512], ('Hr', L), BF16)
                B.dump('Hi%d' % L, B.Hi[L][:], [128, nT, 512], ('Hi', L), BF16)
                B.dump('rn%d' % L, B.rn[L][:], [128, 4], ('rn', L))
                S.barrier()


def shared_inputs(inp):
    g = {}
    a = lambda n: np.asarray(inp[n][0], np.float32)
    g.update(_consts())
    g['ada_w'] = a('ada_w')
    g['ada_bc'] = _col(a('ada_b'), 48)
    g['ada_b'] = a('ada_b')
    g['filt_w1'] = np.concatenate([a('filt_w1'), np.zeros((31, 64), np.float32)], 0); g['filt_w2'] = a('filt_w2'); g['filt_w3'] = a('filt_w3')
    ff = a('filt_freq')
    g['filt_cols'] = np.ascontiguousarray(np.stack([ff[0], a('filt_b1'), ff[1], a('filt_b2')], axis=1))
    g['w_in'] = a('w_in')
    g['convT'] = np.ascontiguousarray(a('conv_in').reshape(3, 26, 128).transpose(2, 1, 0))
    g['hbias'] = _col(a('hyena_bias'), 4)
    g['w0c'] = np.ascontiguousarray(a('rwkv_w0').reshape(2, 4, 128).transpose(2, 0, 1))
    g['a0c'] = np.ascontiguousarray(a('rwkv_a0').reshape(2, 4, 128).transpose(2, 0, 1))
    g['w_lora'] = a('rwkv_w_lora'); g['a_lora'] = a('rwkv_a_lora'); g['g_lora'] = a('rwkv_g_lora')
    g['kkc'] = _col(a('rwkv_k_k'), 4); g['kac'] = _col(a('rwkv_k_a'), 4)
    g['rkc'] = _col(a('rwkv_r_k').reshape(-1), 4)
    g['lnxg'] = _col(a('lnx_g'), 4); g['lnxb'] = _col(a('lnx_b'), 4)
    g['w_out'] = a('w_out')
    for n in ('ln1_g', 'ln1_b', 'ln2_g', 'ln2_b'):
        g[n] = a(n)
    g['w_up'] = a('w_ffn_up')
    g['fconvT'] = np.ascontiguousarray(a('ffn_conv_w').reshape(9, NFC, 128).transpose(2, 1, 0))
    g['fconvb'] = _col(a('ffn_conv_b'), NFC)
    g['w_down'] = a('w_ffn_down')
    return g


def core_inputs(inp, core):
    m = {}
    m['xs'] = np.ascontiguousarray(inp['x_sample'][core], dtype=np.float32)
    m['xp'] = np.ascontiguousarray(inp['x_prompt'][2 * core:2 * core + 2].reshape(512, D), dtype=np.float32)
    cv = np.stack([inp['c'][core], inp['c_ctx']], axis=0).astype(np.float32)
    m['cT'] = np.ascontiguousarray(cv.reshape(2, 8, 128).transpose(2, 1, 0))
    st = np.asarray(inp['state_rwkv'][core, 0], np.float32)
    st = st.reshape(2, 4, 2, 64, 64).transpose(2, 4, 0, 1, 3).reshape(128, 2, 4, 64)
    m['st0T'] = np.ascontiguousarray(st)
    return m


def ln_stats(B, bufs, xt, xk):
    nc, S = B.nc, B.S
    st, mv, sfx = bufs
    ks, km = 'bst' + sfx, 'bmv' + sfx
    for j in range(2):
        S.op('dve', lambda e, j=j: e.bn_stats(out=st[:, j, :], in_=xt[:, j * 512:(j + 1) * 512]), reads=[xk], writes=[ks])
    S.op('dve', lambda e: e.bn_aggr(out=mv[:], in_=st[:].rearrange("p a b -> p (a b)")), reads=[ks], writes=[km])
    S.op('dve', lambda e: e.tensor_scalar_add(out=mv[:, 1:2], in0=mv[:, 1:2], scalar1=LN_EPS), reads=[km], writes=[km])
    S.op('act', lambda e: e.activation(out=mv[:, 1:2], in_=mv[:, 1:2], func=AF.Sqrt), reads=[km], writes=[km])
    S.op('dve', lambda e: e.reciprocal(out=mv[:, 1:2], in_=mv[:, 1:2]), reads=[km], writes=[km])
    return mv, km


def ln_bufs(B, es, tag, n=2):
    return [(es.enter_context(B.st("bst" + tag, [128, 2, 6])), es.enter_context(B.st("bmv" + tag, [128, 2])), "%s%d" % (tag, i))
            for i in range(n)]


def run_window(gen_fns, width):
    pending = list(gen_fns)
    active = []
    while pending or active:
        while pending and len(active) < width:
            active.append(pending.pop(0)())
        for g_ in list(active):
            try:
                next(g_)
            except StopIteration:
                active.remove(g_)


def ln_to_hT(B, x_dram, T, hT, hkey, sc_chunk, sh_chunk, ci, tag, x_sb=None, keep=()):
    nc, S = B.nc, B.S
    with contextlib.ExitStack() as es:
        NBUF = 4
        lb = ln_bufs(B, es, tag, n=NBUF)
        xts = [es.enter_context(B.st("lnx" + tag, [128, 1024])) for i in range(NBUF)] if x_sb is None else None
        xhs = [es.enter_context(B.st("lnh" + tag, [128, 1024])) for i in range(NBUF)]

        def tile_gen(tb):
            i = tb % NBUF
            if x_sb is None:
                xa, xk = xts[i][:], ('lnx' + tag, i)
                S.dma('sp', xa, x_dram[tb * 128:(tb + 1) * 128, :], writes=[xk])
                yield
            else:
                xa, xk = x_sb[:, tb, :], ('x1', tb)
            st, mv, sfx = lb[i]
            ks, km = 'bst' + sfx, 'bmv' + sfx
            for j in range(2):
                S.op('dve', lambda e, j=j: e.bn_stats(out=st[:, j, :], in_=xa[:, j * 512:(j + 1) * 512]), reads=[xk], writes=[ks])
            S.op('dve', lambda e: e.bn_aggr(out=mv[:], in_=st[:].rearrange("p a b -> p (a b)")), reads=[ks], writes=[km])
            S.op('dve', lambda e: e.tensor_scalar_add(out=mv[:, 1:2], in0=mv[:, 1:2], scalar1=LN_EPS), reads=[km], writes=[km])
            yield
            S.op('act', lambda e: e.activation(out=mv[:, 1:2], in_=mv[:, 1:2], func=AF.Sqrt), reads=[km], writes=[km])
            yield
            S.op('dve', lambda e: e.reciprocal(out=mv[:, 1:2], in_=mv[:, 1:2]), reads=[km], writes=[km])
            xh, hk = xhs[i], ('lnh' + tag, i)
            S.op('dve', lambda e: e.tensor_scalar(out=xh[:], in0=xa, scalar1=mv[:, 0:1], scalar2=mv[:, 1:2],
                                                  op0=ALU.subtract, op1=ALU.mult), reads=[xk, km], writes=[hk])
            yield
            for half in range(2):
                ps, pk = B.psum()
                for q in range(4):
                    fc = half * 4 + q
                    S.op('pe', lambda e, q=q, fc=fc, ps=ps: e.transpose(ps[:, q * 128:(q + 1) * 128], xh[:, fc * 128:(fc + 1) * 128],
                                                                        B.ident[:]), reads=[hk, 'ident'], writes=[pk])
                yield
                for q in range(4):
                    fc = half * 4 + q
                    S.op('act', lambda e, q=q, fc=fc, ps=ps: e.activation(
                        out=hT[:, fc, tb * 128:(tb + 1) * 128], in_=ps[:, q * 128:(q + 1) * 128], func=AF.Identity,
                        scale=B.modc[:, sc_chunk + fc, ci:ci + 1], bias=B.modc[:, sh_chunk + fc, ci:ci + 1]),
                        reads=[pk, 'modc'], writes=[(hkey, fc, tb)])

        run_window([(lambda tb=tb: tile_gen(tb)) for tb in range(T // 128)], NBUF)
        S.barrier()


def load_w_bf16(B, dst, w_dram_kcn, cols, key, nk=8):
    B.S.dma('pool', dst, w_dram_kcn, writes=[key])


class Mix:
    pass


class StopBuild(Exception):
    pass


def chk(B, level):
    if getattr(B, 'rw_stop', 99) == level and not B.S.mute:
        B.S.barrier()
        B.S.mute = True


def p_chunk(B, M, f, wbuf, wkey, pc_ap, pc_key, wcol=None):
    nc, S = B.nc, B.S
    T, L, nseq = M.T, M.L, M.nseq
    praw = M.praw[M.pi % len(M.praw)]
    prk = ('praw', M.pi % len(M.praw))
    M.pi += 1
    for tt in range(T // 512):
        ps, pk = B.psum()
        for kc in range(8):
            wc0 = (f % 4) * 128 if wcol is None else wcol
            S.op('pe', lambda e, kc=kc, ps=ps, wc0=wc0: e.matmul(ps[:, :], wbuf[:, kc, wc0:wc0 + 128],
                                                        M.hT[:, kc, tt * 512:(tt + 1) * 512], start=(kc == 0), stop=(kc == 7)),
                 reads=[wkey] + [('hT', kc, tb) for tb in range(tt * 4, tt * 4 + 4)], writes=[pk])
        if L >= 512:
            off = tt * 512
            S.op('act', lambda e, ps=ps: e.copy(out=praw[:, 0, 1 + off:1 + off + 512], in_=ps[:, :]), reads=[pk], writes=[prk])
        else:
            S.op('act', lambda e, ps=ps: e.copy(out=praw[:, :, 1:L + 1], in_=ps[:, :].rearrange("p (s l) -> p s l", s=nseq)),
                 reads=[pk], writes=[prk])
    cw = B.convT
    S.op('act', lambda e: e.activation(out=pc_ap, in_=praw[:, :, 1:L + 1], func=AF.Identity, scale=cw[:, f, 1:2]),
         reads=[prk, 'convT'], writes=[pc_key])
    S.op('dve', lambda e: e.scalar_tensor_tensor(out=pc_ap, in0=praw[:, :, 0:L], scalar=cw[:, f, 0:1], in1=pc_ap,
                                                 op0=ALU.mult, op1=ALU.add), reads=[prk, 'convT', pc_key], writes=[pc_key])
    S.op('dve', lambda e: e.scalar_tensor_tensor(out=pc_ap, in0=praw[:, :, 2:L + 2], scalar=cw[:, f, 2:3], in1=pc_ap,
                                                 op0=ALU.mult, op1=ALU.add), reads=[prk, 'convT', pc_key], writes=[pc_key])


def load_win_group(B, M, grp, col0=None, ncol=None):
    if col0 is None:
        col0 = grp * 512
        ncol = min(512, CIN - grp * 512)
    pf = getattr(M, 'prefetched', None)
    if pf is not None and pf[0] == col0 and pf[1] == ncol:
        M.prefetched = None
        return pf[2], pf[3]
    buf = M.wbuf[M.wi % 2]
    key = ('wbuf', M.wi % 2)
    M.wi += 1
    B.S.dma('pool', buf[:, :, 0:ncol], B.w_in_d[:, :, col0:col0 + ncol], writes=[key])
    return buf, key


def mix_setup(B, M, es):
    nc, S = B.nc, B.S
    T, L, nseq, ci = M.T, M.L, M.nseq, M.ci
    M.uid = getattr(M, 'uid', 0) + 1

    def TT(name, shape, dt=F32):
        return es.enter_context(B.st(name + M.name, shape, dt))
    M.hT = TT('hT', [128, 8, T], BF16)
    M.praw = [TT('praw%d' % i, [128, nseq, L + 2]) for i in range(getattr(M, 'npraw', 2))]
    M.pi = 0
    M.wbuf = [TT('wbuf%d' % i, [128, 8, getattr(M, 'wcols', 512)], BF16) for i in range(2)]
    M.wi = 0
    M.prefetched = None
    if getattr(M, 'first_cols', None) is not None:
        col0, ncol = M.first_cols
        buf, key = load_win_group(B, M, None, col0=col0, ncol=ncol)
        M.prefetched = (col0, ncol, buf, key)
    ln_to_hT(B, M.x_dram, T, M.hT, 'hT', 8, 0, ci, M.name, keep=[('wbuf', 0)])
    for i in range(len(M.praw)):
        S.op('pool', lambda e, i=i: e.memset(M.praw[i][:], 0.0), writes=[('praw', i)])


def phase_mix_hyena(B, M):
    nc, S = B.nc, B.S
    T, L, nseq, ci = M.T, M.L, M.nseq, M.ci
    nblk, nTL = T // 128, L // 128

    with contextlib.ExitStack() as es2:
        M.wcols = 256
        M.first_cols = (0, 256)
        mix_setup(B, M, es2)

        def T2(name, shape, dt=F32):
            return es2.enter_context(B.st(name + M.name, shape, dt))
        hx1f = T2('hx1t', [128, 1024])
        hx1t = hx1f[:, 0:nseq * L].rearrange("p (s l) -> p s l", s=nseq)
        hx0 = T2('hx0', [128, 4, nseq, L], BF16)
        z = T2('z', [128, 4, nseq, L])
        zT = T2('zT', [128, nblk, 512], BF16)
        Pr = T2('Pr', [128, nseq, nTL, 512], BF16)
        Q = T2('Q', [128, nseq, nTL, 512], BF16)
        tmpa = hx1f[:, 0:512]
        tmps = [[T2('tmp0_%d' % j, [128, 512], BF16)[:] for j in range(4)],
                [hx1f[:, j * 256:(j + 1) * 256].bitcast(BF16) for j in range(4)]]
        for grp in range(3):
            for q in range(4):
                f = grp * 4 + q
                if q % 2 == 0:
                    wb, wk_ = load_win_group(B, M, None, col0=f * 128, ncol=256)
                wc = (q % 2) * 128
                if grp < 2:
                    dst = (z, hx0)[grp]
                    p_chunk(B, M, f, wb, wk_, dst[:, q, :, :], (('z', 'hx0')[grp], q), wcol=wc)
                else:
                    p_chunk(B, M, f, wb, wk_, hx1t, 'hx1t', wcol=wc)
                    S.op('dve', lambda e, q=q: e.tensor_tensor(out=z[:, q], in0=z[:, q], in1=hx1t, op=ALU.mult),
                         reads=[('z', q), 'hx1t'], writes=[('z', q)])
        B.dump('Phy' + M.name, hx0[:], [128, 4, nseq, L], [('hx0', q) for q in range(4)], BF16)
        zf = z[:].rearrange("p q s l -> p q (s l)")
        for tb in range(nblk):
            ps, pk = B.psum()
            for q in range(4):
                S.op('pe', lambda e, q=q, ps=ps: e.transpose(ps[:, q * 128:(q + 1) * 128], zf[:, q, tb * 128:(tb + 1) * 128], B.ident[:]),
                     reads=[('z', q), 'ident'], writes=[pk])
            S.op('act', lambda e, ps=ps: e.copy(out=zT[:, tb, :], in_=ps[:, :]), reads=[pk], writes=[('zT', tb)])
        for q in range(4):
            S.op('dve', lambda e, q=q: e.tensor_scalar(out=z[:, q], in0=z[:, q], scalar1=B.hbias[:, q:q + 1], scalar2=None, op0=ALU.mult),
                 reads=[('z', q), 'hbias'], writes=[('z', q)])
        Cm, Sm, Sg, Hr, Hi = B.Cm[L], B.Sm[L], B.Sg[L], B.Hr[L], B.Hi[L]
        for s in range(nseq):
            for kc in range(nTL):
                psr, pkr = B.psum()
                pss, pks = B.psum()
                for (mat, mk, ps, pk) in ((Cm, ('Cm', L), psr, pkr), (Sm, ('Sm', L), pss, pks)):
                    for tb in range(nTL):
                        S.op('pe', lambda e, mat=mat, ps=ps, tb=tb: e.matmul(ps[:, :], mat[:, tb, kc * 128:(kc + 1) * 128],
                                                                             zT[:, s * nTL + tb, :], start=(tb == 0), stop=(tb == nTL - 1)),
                             reads=[mk, ('zT', s * nTL + tb)], writes=[pk])
                ta, tb_, ka, kb = tmps[kc % 2][0], tmps[kc % 2][1], ('tmpa', kc % 2), ('tmpb', kc % 2)
                S.op('dve', lambda e, ta=ta: e.tensor_tensor(out=ta, in0=psr[:, :], in1=Hr[:, kc, :], op=ALU.mult), reads=[pkr, ('Hr', L)], writes=[ka])
                S.op('dve', lambda e, tb_=tb_: e.tensor_tensor(out=tb_, in0=pss[:, :], in1=Hi[:, kc, :], op=ALU.mult), reads=[pks, ('Hi', L)], writes=[kb])
                S.op('pool', lambda e, ta=ta, tb_=tb_: e.tensor_tensor(out=Pr[:, s, kc, :], in0=ta, in1=tb_, op=ALU.add), reads=[ka, kb], writes=[('Pr', s, kc)])
                tc, td, kc_, kd_ = tmps[kc % 2][2], tmps[kc % 2][3], ('tmpc', kc % 2), ('tmpd', kc % 2)
                S.op('dve', lambda e, tc=tc: e.tensor_tensor(out=tc, in0=pss[:, :], in1=Hr[:, kc, :], op=ALU.mult), reads=[pks, ('Hr', L)], writes=[kc_])
                S.op('dve', lambda e, td=td: e.tensor_tensor(out=td, in0=psr[:, :], in1=Hi[:, kc, :], op=ALU.mult), reads=[pkr, ('Hi', L)], writes=[kd_])
                S.op('pool', lambda e, tc=tc, td=td: e.tensor_tensor(out=Q[:, s, kc, :], in0=tc, in1=td, op=ALU.subtract), reads=[kc_, kd_], writes=[('Q', s, kc)])
                if kc == 0:
                    S.op('dve', lambda e: e.tensor_tensor(out=Pr[0:1, s, 0, :], in0=psr[0:1, :], in1=Hr[0:1, 0, :], op=ALU.mult),
                         reads=[pkr, ('Hr', L), ('Pr', s, 0)], writes=[('Pr', s, 0)])
                    S.op('dve', lambda e: e.tensor_tensor(out=Q[0:1, s, 0, :], in0=pss[0:1, :], in1=Hi[0:1, 0, :], op=ALU.mult),
                         reads=[pks, ('Hi', L), ('Q', s, 0)], writes=[('Q', s, 0)])
            tw = min(L, 512)
            for q in range(4):
                for tt in range(L // tw):
                    ps, pk = B.psum()
                    n = 0
                    for (src, sk, mat, mk) in ((Pr, 'Pr', Cm, ('Cm', L)), (Q, 'Q', Sg, ('Sg', L))):
                        for kc in range(nTL):
                            S.op('pe', lambda e, src=src, mat=mat, kc=kc, ps=ps, n=n: e.matmul(
                                ps[:, 0:tw], src[:, s, kc, q * 128:(q + 1) * 128], mat[:, kc, tt * tw:(tt + 1) * tw],
                                start=(n == 0), stop=(n == 2 * nTL - 1)), reads=[(sk, s, kc), mk], writes=[pk])
                            n += 1
                    sl = slice(tt * tw, (tt + 1) * tw)
                    S.op('dve', lambda e, ps=ps, sl=sl: e.scalar_tensor_tensor(out=tmpa[:, 0:tw], in0=ps[:, 0:tw], scalar=B.rn[L][:, q:q + 1],
                                                                               in1=z[:, q, s, sl], op0=ALU.mult, op1=ALU.add),
                         reads=[pk, ('rn', L), ('z', q)], writes=['tmpa'])
                    S.op('dve', lambda e, sl=sl: e.tensor_tensor(out=M.yT[:, q, s * L + tt * tw:s * L + (tt + 1) * tw], in0=tmpa[:, 0:tw],
                                                                 in1=hx0[:, q, s, sl], op=ALU.mult),
                         reads=['tmpa', ('hx0', q)], writes=[('yT', q)])
        B.dump('yh' + M.name, M.yT[:, 0:4, :], [128, 4, T], [('yT', q) for q in range(4)], BF16)
        S.barrier()


RWT = BF16
SUB = 32


def rwkv_setup(B, es):
    nc, S = B.nc, B.S
    sb = lambda name, shape, dt=F32: es.enter_context(B.st(name, shape, dt))
    B.lora = sb('lora_s', [128, 2, 512], F32R)
    def P(name, shape, dname=None, dt=F32):
        t = sb(name + '_s', shape, dt)
        S.dma('sp', t[:], B.inp(dname or name, shape, dt), writes=[name])
        return t
    B.w0c = P('w0c', [128, 2, 4]); B.a0c = P('a0c', [128, 2, 4])
    B.kkc = P('kkc', [128, 4]); B.kac = P('kac', [128, 4]); B.rkc = P('rkc', [128, 4])
    B.lnxg = P('lnxg', [128, 4]); B.lnxb = P('lnxb', [128, 4])
    B.glora = P('g_lora', [128, 512])
    wl = B.inp('w_lora', [2, 64, 512]); al = B.inp('a_lora', [2, 64, 512])
    with contextlib.ExitStack() as es_l:
        lora32 = es_l.enter_context(B.st('lora32', [128, 2, 512]))
        for d in range(2):
            S.dma('sp', lora32[0:64, d, :], wl[d], writes=['lora32'])
            S.dma('sp', lora32[64:128, d, :], al[d], writes=['lora32'])
        S.op('dve', lambda e: e.tensor_copy(out=B.lora[:], in_=lora32[:]), reads=['lora32'], writes=['lora'])
        S.barrier()
    B.omka = sb('omka_s', [128, 4])
    S.op('dve', lambda e: e.tensor_scalar(out=B.omka[:], in0=B.kac[:], scalar1=-1.0, scalar2=1.0, op0=ALU.mult, op1=ALU.add),
         reads=['kac'], writes=['omka'])
    B.scanmask = P('scanmask', [128, 1024])
    B.maskT4 = sb('maskT4_s', [128, 512])
    mst = B.inp('mstrictT', [128, 128]); mit = B.inp('minclT', [128, 128])
    for j, m in enumerate((mst, mit, mst, mit)):
        S.dma('sp', B.maskT4[:, j * 128:(j + 1) * 128], m, writes=['maskT4'])
    B.mstrict = P('mstrict', [128, 128])
    B.rowmask = P('rowmask', [128, 4])
    B.ident4 = P('ident4', [128, 4, 128])
    B.rowmask16 = sb('rowmask16_s', [128, 4], BF16)
    S.op('dve', lambda e: e.tensor_copy(out=B.rowmask16[:], in_=B.rowmask[:]), reads=['rowmask'], writes=['rowmask16'])
    B.blk1 = sb('blk1_s', [128, 128])
    S.op('pool', lambda e: e.memset(B.blk1[:], 0.0), writes=['blk1'])
    S.op('pool', lambda e: e.memset(B.blk1[0:64, 0:64], 1.0), writes=['blk1'])
    S.op('pool', lambda e: e.memset(B.blk1[64:128, 64:128], 1.0), writes=['blk1'])
    B.blk1r = sb('blk1r_s', [128, 128], F32R)
    S.op('pool', lambda e: e.memset(B.blk1r[:].bitcast(F32), 0.0), writes=['blk1r'])
    S.op('pool', lambda e: e.memset(B.blk1r[0:64, 0:64].bitcast(F32), 1.0), writes=['blk1r'])
    S.op('pool', lambda e: e.memset(B.blk1r[64:128, 64:128].bitcast(F32), 1.0), writes=['blk1r'])


def run_interleaved(gens):
    gens = [g for g in gens]
    done = set()
    while len(done) < len(gens):
        for i, g_ in enumerate(gens):
            if i in done:
                continue
            try:
                next(g_)
            except StopIteration:
                done.add(i)


def phase_rwkv(B, M):
    nc, S = B.nc, B.S
    M.combo = 0
    M.pending = None
    T, L, nseq, ci = M.T, M.L, M.nseq, M.ci
    SEG = 256
    nseg = L // SEG
    nsub = SEG // SUB
    with contextlib.ExitStack() as es:
        M.wcols = 256
        M.npraw = 1
        M.first_cols = (3072, 256)
        mix_setup(B, M, es)
        M.npraw = 2

        def TT(name, shape, dt=F32):
            return es.enter_context(B.st(name + M.name, shape, dt))
        NSL = [128, nseq, L]
        wdad = TT('wdad', NSL); gsig = TT('gsig', NSL)
        wb, wk_ = load_win_group(B, M, 6)
        p_chunk(B, M, 24, wb, wk_, wdad[:], 'wdad', wcol=0)
        p_chunk(B, M, 25, wb, wk_, gsig[:], 'gsig', wcol=128)
        S.op('act', lambda e: e.activation(out=wdad[0:64], in_=wdad[0:64], func=AF.Tanh), reads=['wdad'], writes=['wdad'])
        S.op('act', lambda e: e.activation(out=gsig[:], in_=gsig[:], func=AF.Sigmoid), reads=['gsig'], writes=['gsig'])
        wdf = wdad[:].rearrange("p s l -> p (s l)")
        wdr = TT('wdr', NSL, F32R)
        S.op('dve', lambda e: e.tensor_copy(out=wdr[:], in_=wdad[:]), reads=['wdad'], writes=['wdr'])
        wdrf = wdr[:].rearrange("p s l -> p (s l)")
        gsf = gsig[:].rearrange("p s l -> p (s l)")
        rr = TT('rr', NSL, BF16); kx = TT('kx', NSL, BF16); vv = TT('vv', NSL, BF16); kk = TT('kk', NSL)
        tB = TT('tB', NSL)
        ysum = TT('ysum', NSL); ybon = TT('ybon', NSL)
        aS = TT('aS', NSL); lw = TT('lw', NSL); kd = TT('kd', NSL, BF16); bb = TT('bb', NSL, BF16)
        tA = aS
        sq = TT('sq', NSL, F32R)
        cw2 = [TT('cw%d' % i, [128, SEG]) for i in range(2)]; eW3 = [TT('eW%d' % i, [128, SEG]) for i in range(3)]; eWi2 = [TT('eWi%d' % i, [128, SEG]) for i in range(2)]
        at2 = [TT('at%d' % i, [128, SEG], RWT) for i in range(2)]; rt2 = [TT('rt%d' % i, [128, SEG], RWT) for i in range(2)]
        bt2 = [TT('bt%d' % i, [128, SEG], RWT) for i in range(2)]; kt2 = [TT('kt%d' % i, [128, SEG], RWT) for i in range(2)]
        vp2 = [TT('vp%d' % i, [128, SEG], RWT) for i in range(2)]
        Gall2 = [TT('Gall%d' % i, [128, nsub, 128], F32R) for i in range(2)]; SlW2 = [TT('SlW%d' % i, [128, nsub, 64]) for i in range(2)]
        Sall2 = [TT('Sall%d' % i, [128, nsub + 1, 64], F32R) for i in range(2)]
        Sbd2 = [TT('Sbd%d' % i, [128, nsub, 128], RWT) for i in range(2)]
        RhT2 = [TT('RhT%d' % i, [128, SEG], RWT) for i in range(2)]; YlT2 = [TT('YlT%d' % i, [128, SEG]) for i in range(2)]
        ytmp2 = [TT('ytmpc%d' % i, [128, SEG]) for i in range(2)]
        NU = 4
        NB = NU // 2
        AM = [TT('AM%d' % i, [128, 512], RWT) for i in range(NU)]; TM = [TT('TM%d' % i, [128, 320], RWT) for i in range(NU)]
        TMP = [TT('TMP%d' % i, [128, 384], RWT) for i in range(NU)]; XP = [TT('XP%d' % i, [128, 256], RWT) for i in range(NU)]
        BX = [TT('BX%d' % i, [128, 256], RWT) for i in range(NU)]; UX = [TT('UX%d' % i, [128, 256], RWT) for i in range(NU)]
        VX = [TT('VX%d' % i, [128, 256], RWT) for i in range(NU)]
        WW = [[TT('WW%d_%d' % (u_, i), [128, 384], RWT) for i in range(2)] for u_ in range(NU)]
        for t_, k_ in [(Gall2[i], ('Gall', i)) for i in range(2)] + [(Sbd2[i], ('Sbd', i)) for i in range(2)] + [(TMP[i], ('TMP', i)) for i in range(NU)] + [(XP[i], ('XP', i)) for i in range(NU)]:
            S.op('pool', lambda e, t_=t_: e.memset(t_[:].bitcast(F32) if t_.dtype == F32R else t_[:], 0.0), writes=[k_])
        ytmp = TT('ytmp', [128, 512])
        flat = lambda t: t[:].rearrange("p s l -> p (s l)")

        for hp in range(4):
            for (f, dst, key) in ((12 + hp, rr, 'rr'), (16 + hp, kx, 'kx'), (20 + hp, vv, 'vv')):
                wb, wk_ = load_win_group(B, M, None, col0=f * 128, ncol=128)
                p_chunk(B, M, f, wb, wk_, dst[:], key, wcol=0)
            if hp == 0:
                B.dump('rr' + M.name, rr[:], NSL, 'rr', BF16)
            S.op('act', lambda e: e.activation(out=kk[:], in_=kx[:], func=AF.Identity, scale=B.kkc[:, hp:hp + 1]),
                 reads=['kx', 'kkc'], writes=['kk'])
            S.op('act', lambda e: e.activation(out=sq[:], in_=kk[:], func=AF.Square), reads=['kk'], writes=['sq'])
            for tt in range(T // 512):
                sl = slice(tt * 512, (tt + 1) * 512)
                ps, pk = B.psum()
                S.op('pe', lambda e, ps=ps, sl=sl: e.matmul(ps[:, :], B.blk1r[:], flat(sq)[:, sl], start=True, stop=True),
                     reads=['blk1r', 'sq'], writes=[pk])
                S.op('act', lambda e, ps=ps, sl=sl: e.activation(out=flat(tB)[:, sl], in_=ps[:, :], func=AF.Sqrt), reads=[pk], writes=['tB'])
            S.op('dve', lambda e: e.tensor_scalar_max(out=tB[:], in0=tB[:], scalar1=1e-12), reads=['tB'], writes=['tB'])
            S.op('dve', lambda e: e.reciprocal(out=tB[:], in_=tB[:]), reads=['tB'], writes=['tB'])
            S.op('dve', lambda e: e.tensor_tensor(out=kk[:], in0=kk[:], in1=tB[:], op=ALU.mult), reads=['kk', 'tB'], writes=['kk'])
            chk(B, 1)
            S.op('pool', lambda e: e.memset(ybon[:], 0.0), writes=['ybon'])
            S.op('pool', lambda e: e.memset(ysum[:], 0.0), writes=['ysum'])
            def dprep_gen(d):
                for tt in range(T // 512):
                    sl = slice(tt * 512, (tt + 1) * 512)
                    ps, pk = B.psum()
                    S.op('pe', lambda e, ps=ps, sl=sl: e.matmul(ps[:, :], B.lora[0:64, d, hp * 128:(hp + 1) * 128], wdrf[0:64, sl],
                                                                start=True, stop=True), reads=['lora', 'wdr'], writes=[pk])
                    yield
                    S.op('act', lambda e, ps=ps, sl=sl: e.activation(out=flat(lw)[:, sl], in_=ps[:, :], func=AF.Sigmoid,
                                                                      bias=B.w0c[:, d, hp:hp + 1]), reads=[pk, 'w0c'], writes=['lw'])
                    ps2, pk2 = B.psum()
                    S.op('pe', lambda e, ps2=ps2, sl=sl: e.matmul(ps2[:, :], B.lora[64:128, d, hp * 128:(hp + 1) * 128], wdrf[64:128, sl],
                                                                  start=True, stop=True), reads=['lora', 'wdr'], writes=[pk2])
                    yield
                    S.op('act', lambda e, ps2=ps2, sl=sl: e.activation(out=flat(aS)[:, sl], in_=ps2[:, :], func=AF.Sigmoid,
                                                                        bias=B.a0c[:, d, hp:hp + 1]), reads=[pk2, 'a0c'], writes=['aS'])
                S.op('act', lambda e: e.activation(out=lw[:], in_=lw[:], func=AF.Copy, scale=-math.exp(-0.5)),
                     reads=['lw'], writes=['lw'])
                yield
                S.op('dve', lambda e: e.tensor_scalar(out=kd[:], in0=aS[:], scalar1=B.kac[:, hp:hp + 1], scalar2=B.omka[:, hp:hp + 1],
                                                      op0=ALU.mult, op1=ALU.add), reads=['aS', 'kac', 'omka'], writes=['kd'])
                yield
                S.op('dve', lambda e: e.tensor_tensor(out=kd[:], in0=kd[:], in1=kx[:], op=ALU.mult), reads=['kd', 'kx'], writes=['kd'])
                yield
                S.op('dve', lambda e: e.tensor_tensor(out=bb[:], in0=kk[:], in1=aS[:], op=ALU.mult), reads=['kk', 'aS'], writes=['bb'])
                yield
                S.op('dve', lambda e: e.scalar_tensor_tensor(out=sq[:], in0=rr[:], scalar=B.rkc[:, hp:hp + 1], in1=kd[:],
                                                             op0=ALU.mult, op1=ALU.mult), reads=['rr', 'rkc', 'kd'], writes=['sq'])
                yield
                for tt in range(T // 512):
                    sl = slice(tt * 512, (tt + 1) * 512)
                    ps, pk = B.psum()
                    S.op('pe', lambda e, ps=ps, sl=sl: e.matmul(ps[:, :], B.blk1r[:], flat(sq)[:, sl], start=True, stop=True),
                         reads=['blk1r', 'sq'], writes=[pk])
                    yield
                    S.op('dve', lambda e, ps=ps, sl=sl: e.tensor_tensor(out=flat(tB)[:, sl], in0=ps[:, :], in1=flat(vv)[:, sl], op=ALU.mult),
                         reads=[pk, 'vv'], writes=['tB'])
                    yield
                S.op('dve', lambda e: e.tensor_tensor(out=ybon[:], in0=ybon[:], in1=tB[:], op=ALU.add), reads=['ybon', 'tB'], writes=['ybon'])

            dpre = False
            ppre = False
            for d in range(2):
                if not dpre:
                    run_interleaved([dprep_gen(d)])
                dpre = False
                chk(B, 2)
                combos = [(s_, g_) for s_ in range(nseq) for g_ in range(nseg)]
                cpar = [M.combo + i for i in range(len(combos))]
                M.combo += len(combos)

                def prep_gen(s, seg, cidx, d=d):
                    par = cidx % 2
                    cw, eW, eWi = cw2[par], eW3[cidx % 3], eWi2[par]
                    at_, rt_, bt_, kt_, vp = at2[par], rt2[par], bt2[par], kt2[par], vp2[par]
                    kE = ('eW', cidx % 3)
                    kc, kei, ka, kr, kb, kk_, kv = ('cw', par), ('eWi', par), ('at', par), ('rt', par), ('bt', par), ('kt', par), ('vp', par)

                    def pv(t):
                        v_ = t[:, s, ::-1] if d == 1 else t[:, s, :]
                        return v_[:, seg * SEG:(seg + 1) * SEG]
                    S.op('dve', lambda e: e.tensor_tensor_scan(out=cw[:], data0=B.scanmask[:, 0:SEG], data1=pv(lw), initial=0.0,
                                                               op0=ALU.mult, op1=ALU.add), reads=['scanmask', 'lw'], writes=[kc])
                    yield
                    S.op('act', lambda e: e.activation(out=eW[:], in_=cw[:], func=AF.Exp), reads=[kc], writes=[kE])
                    S.op('act', lambda e: e.activation(out=eWi[:], in_=cw[:], func=AF.Exp, scale=-1.0), reads=[kc], writes=[kei])
                    yield
                    S.op('pool', lambda e: e.tensor_tensor(out=rt_[:], in0=pv(rr), in1=eW[:], op=ALU.mult), reads=['rr', kE], writes=[kr])
                    S.op('pool', lambda e: e.tensor_tensor(out=bt_[:], in0=pv(bb), in1=eWi[:], op=ALU.mult), reads=['bb', kei], writes=[kb])
                    yield
                    S.op('pool', lambda e: e.tensor_tensor(out=kt_[:], in0=pv(kd), in1=eWi[:], op=ALU.mult), reads=['kd', kei], writes=[kk_])
                    S.op('pool', lambda e: e.tensor_copy(out=vp[:], in_=pv(vv)), reads=['vv'], writes=[kv])
                    yield
                    S.op('dve', lambda e: e.tensor_tensor(out=cw[:], in0=cw[:], in1=pv(lw), op=ALU.subtract), reads=[kc, 'lw'], writes=[kc])
                    yield
                    S.op('act', lambda e: e.activation(out=cw[:], in_=cw[:], func=AF.Exp), reads=[kc], writes=[kc])
                    yield
                    S.op('dve', lambda e: e.scalar_tensor_tensor(out=at_[:], in0=pv(kk), scalar=-1.0, in1=cw[:], op0=ALU.mult, op1=ALU.mult),
                         reads=['kk', kc], writes=[ka])

                if not ppre:
                    run_interleaved([prep_gen(combos[0][0], combos[0][1], cpar[0])])
                ppre = False
                for ci_, (s, seg) in enumerate(combos):
                    cidx = cpar[ci_]
                    par = cidx % 2
                    nxt = [prep_gen(combos[ci_ + 1][0], combos[ci_ + 1][1], cpar[ci_ + 1])] if ci_ + 1 < len(combos) else []
                    if ci_ + 1 == len(combos) and d == 0:
                        def seq_gen(*gs):
                            for g_ in gs:
                                yield from g_
                        nxt = nxt + [seq_gen(dprep_gen(1), prep_gen(combos[0][0], combos[0][1], M.combo, d=1))]
                        dpre = True
                        ppre = True
                    eW, Gall, SlW, Sall, Sbd, RhT, YlT = eW3[cidx % 3], Gall2[par], SlW2[par], Sall2[par], Sbd2[par], RhT2[par], YlT2[par]
                    at_, rt_, bt_, kt_, vp = at2[par], rt2[par], bt2[par], kt2[par], vp2[par]
                    kE, kG, kSl, kS, kSb, kR, kY = ('eW', cidx % 3), ('Gall', par), ('SlW', par), ('Sall', par), ('Sbd', par), ('RhT', par), ('YlT', par)
                    kA_, kR_, kB_, kK_, kV_ = ('at', par), ('rt', par), ('bt', par), ('kt', par), ('vp', par)
                    chk(B, 3)
                    def unit(blk, h2):
                        u = (blk % NB) * 2 + h2
                        bs = slice(blk * 128, (blk + 1) * 128)
                        hs = slice(64 * h2, 64 * h2 + 64)
                        AMh, TMh, akey, tkey = AM[u], TM[u], ('AM', u), ('TM', u)
                        Wu = WW[u]
                        psA, pkA = B.psum()
                        for j, (lh, lk, rh, rk) in enumerate(((bt_, kB_, at_, kA_), (bt_, kB_, rt_, kR_),
                                                              (kt_, kK_, at_, kA_), (kt_, kK_, rt_, kR_))):
                            S.op('pe', lambda e, j=j, lh=lh, rh=rh: e.matmul(psA[:, j * 128:(j + 1) * 128], lh[hs, bs], rh[hs, bs],
                                                                             start=True, stop=True), reads=[lk, rk], writes=[pkA])
                        yield
                        S.op('dve', lambda e: e.tensor_tensor(out=AMh[:], in0=psA[:, :], in1=B.maskT4[:], op=ALU.mult),
                             reads=[pkA, 'maskT4'], writes=[akey])
                        psB, pkB = B.psum()
                        S.op('pe', lambda e: e.matmul(psB[:, 0:128], at_[hs, bs], bt_[hs, bs], start=True, stop=True),
                             reads=[kA_, kB_], writes=[pkB])
                        psTb = psB[:, 256:384].bitcast(BF16)
                        for j, (src, sk) in enumerate(((bt_, kB_), (kt_, kK_), (vp, kV_), (at_, kA_))):
                            S.op('pe', lambda e, j=j, src=src: e.transpose(psTb[:, j * 64:(j + 1) * 64], src[hs, bs], B.identb[hs, hs]),
                                 reads=[sk, 'identb'], writes=[pkB])
                        yield
                        S.op('dve', lambda e: e.tensor_tensor(out=Wu[0][:, 128:256], in0=psB[:, 0:128], in1=B.mstrict[:], op=ALU.mult),
                             reads=[pkB, 'mstrict'], writes=[('WW', u, 0)])
                        S.op('act', lambda e: e.copy(out=TMh[:, 0:256], in_=psTb[:, 0:256]), reads=[pkB], writes=[tkey])
                        S.op('act', lambda e: e.copy(out=TMP[u][:].rearrange("p (j k) -> p j k", j=3)[:, :, 64 * h2:64 * h2 + 64],
                                                     in_=psTb[:, 0:192].rearrange("p (j k) -> p j k", j=3)), reads=[pkB], writes=[('TMP', u)])
                        rm3p = B.rowmask16[:].unsqueeze(2).to_broadcast([128, 4, 64])
                        for (dst, dk, src) in ((BX[u], ('BX', u), TMh[:, 0:64]), (VX[u], ('VX', u), TMh[:, 128:192])):
                            S.op('pool', lambda e, dst=dst, src=src: e.tensor_tensor(out=dst[:].rearrange("p (s k) -> p s k", s=4),
                                                                                     in0=src.unsqueeze(1).to_broadcast([128, 4, 64]), in1=rm3p, op=ALU.mult),
                                 reads=[tkey, 'rowmask16'], writes=[dk])
                        psZ, pkZ = B.psum()
                        S.op('pe', lambda e: e.matmul(psZ[:, 0:64], AMh[:, 256:384], TMh[:, 128:192], start=True, stop=True),
                             reads=[akey, tkey], writes=[pkZ])
                        yield
                        S.op('act', lambda e: e.copy(out=TMh[:, 256:320], in_=psZ[:, 0:64]), reads=[pkZ], writes=[tkey])
                        for j in range(5):
                            c, n = j % 2, (j + 1) % 2
                            psX, pkX = B.psum()
                            if j == 0:
                                S.op('pe', lambda e: e.matmul(psX[:, 0:128], AMh[:, 0:128], TMh[:, 192:320], start=True, stop=True),
                                     reads=[akey, tkey], writes=[pkX])
                                S.op('pe', lambda e: e.matmul(psX[:, 128:256], AMh[:, 0:128], Wu[0][:, 128:256], start=True, stop=True),
                                     reads=[akey, ('WW', u, 0)], writes=[pkX])
                                S.op('pe', lambda e: e.matmul(psX[:, 256:384], Wu[0][:, 128:256], AMh[:, 0:128], start=True, stop=True),
                                     reads=[akey, ('WW', u, 0)], writes=[pkX])
                                xprev, xpk = TMh[:, 192:320], tkey
                            else:
                                wk_c = ('WW', u, c)
                                if j < 4:
                                    S.op('pe', lambda e, c=c: e.matmul(psX[:, 0:256], Wu[c][:, 256:384], Wu[c][:, 0:256], start=True, stop=True),
                                         reads=[wk_c], writes=[pkX])
                                    S.op('pe', lambda e, c=c: e.matmul(psX[:, 256:384], Wu[c][:, 128:256], Wu[c][:, 256:384], start=True, stop=True),
                                         reads=[wk_c], writes=[pkX])
                                else:
                                    S.op('pe', lambda e, c=c: e.matmul(psX[:, 0:128], Wu[c][:, 256:384], Wu[c][:, 0:128], start=True, stop=True),
                                         reads=[wk_c], writes=[pkX])
                                xprev, xpk = Wu[c][:, 0:128], wk_c
                            yield
                            if j < 4:
                                S.op('dve', lambda e, n=n, xprev=xprev: e.tensor_tensor(out=Wu[n][:, 0:128], in0=psX[:, 0:128], in1=xprev, op=ALU.add),
                                     reads=[pkX, xpk], writes=[('WW', u, n)])
                                S.op('act', lambda e, n=n: e.copy(out=Wu[n][:, 128:384], in_=psX[:, 128:384]), reads=[pkX], writes=[('WW', u, n)])
                            else:
                                S.op('dve', lambda e, xprev=xprev: e.tensor_tensor(
                                    out=XP[u][:].rearrange("p (j k) -> p j k", j=2)[:, :, 64 * h2:64 * h2 + 64],
                                    in0=psX[:, 0:128].rearrange("p (j k) -> p j k", j=2),
                                    in1=xprev.rearrange("p (j k) -> p j k", j=2), op=ALU.add),
                                    reads=[pkX, xpk], writes=[('XP', u)])
                        yield
                        rm3 = B.rowmask[:].unsqueeze(2).to_broadcast([128, 4, 64])
                        for (dst, dk, src, sk) in ((UX[u], ('UX', u), XP[u][:, 128 + 64 * h2:128 + 64 * h2 + 64], ('XP', u)),):
                            S.op('dve', lambda e, dst=dst, src=src: e.tensor_tensor(out=dst[:].rearrange("p (s k) -> p s k", s=4),
                                                                                    in0=src.unsqueeze(1).to_broadcast([128, 4, 64]), in1=rm3, op=ALU.mult),
                                 reads=[sk, 'rowmask'], writes=[dk])

                    def pair(blk):
                        u0 = (blk % NB) * 2
                        bs = slice(blk * 128, (blk + 1) * 128)
                        psR, pkR = B.psum()
                        for h2 in range(2):
                            u = u0 + h2
                            S.op('pe', lambda e, h2=h2, u=u: e.matmul(psR[:, 0:128], XP[u][:, 0:128], AM[u][:, 128:256], start=(h2 == 0), stop=(h2 == 1)),
                                 reads=[('XP', u), ('AM', u)], writes=[pkR])
                        for h2 in range(2):
                            u = u0 + h2
                            S.op('pe', lambda e, h2=h2, u=u: e.matmul(psR[:, 128:256], XP[u][:, 128:256], AM[u][:, 128:256], start=(h2 == 0), stop=False),
                                 reads=[('XP', u), ('AM', u)], writes=[pkR])
                            S.op('pe', lambda e, h2=h2, u=u: e.matmul(psR[:, 128:256], TMP[u][:, 256:384], AM[u][:, 384:512], start=False, stop=(h2 == 1)),
                                 reads=[('TMP', u), ('AM', u)], writes=[pkR])
                        psG, pkG = B.psum()
                        for h2 in range(2):
                            u = u0 + h2
                            S.op('pe', lambda e, h2=h2, u=u: e.matmul(psG[:, 0:256], XP[u][:, 0:128], BX[u][:], start=(h2 == 0), stop=(h2 == 1)),
                                 reads=[('XP', u), ('BX', u)], writes=[pkG])
                        for h2 in range(2):
                            u = u0 + h2
                            S.op('pe', lambda e, h2=h2, u=u: e.matmul(psG[:, 256:512], TMP[u][:, 0:128], UX[u][:], start=(h2 == 0), stop=False),
                                 reads=[('TMP', u), ('UX', u)], writes=[pkG])
                            S.op('pe', lambda e, h2=h2, u=u: e.matmul(psG[:, 256:512], TMP[u][:, 128:256], VX[u][:], start=False, stop=(h2 == 1)),
                                 reads=[('TMP', u), ('VX', u)], writes=[pkG])
                        yield
                        S.op('dve', lambda e: e.tensor_tensor(out=RhT[:, bs], in0=psR[:, 0:128], in1=rt_[:, bs], op=ALU.add),
                             reads=[pkR, kR_], writes=[kR])
                        S.op('dve', lambda e: e.tensor_copy(out=YlT[:, bs], in_=psR[:, 128:256]), reads=[pkR], writes=[kY])
                        n0 = blk * 4
                        for h2 in range(2):
                            hs = slice(64 * h2, 64 * h2 + 64)
                            S.op('dve', lambda e, hs=hs: e.tensor_tensor(out=Gall[hs, n0:n0 + 4, hs],
                                                                         in0=psG[hs, 0:256].rearrange("p (s k) -> p s k", s=4),
                                                                         in1=B.ident4[hs, :, hs], op=ALU.add),
                                 reads=[pkG, 'ident4'], writes=[kG])
                        yield
                        for sc in range(4):
                            wc = eW[:, (n0 + sc) * 32 + 31:(n0 + sc) * 32 + 32]
                            S.op('act', lambda e, sc=sc, wc=wc: e.activation(out=SlW[:, n0 + sc, :], in_=psG[:, 256 + sc * 64:256 + (sc + 1) * 64],
                                                                              func=AF.Identity, scale=wc),
                                 reads=[pkG, kE], writes=[kSl])

                    pend = [M.pending] if M.pending is not None else []
                    M.pending = None
                    for b0 in range(0, SEG // 128, NB):
                        run_interleaved([unit(b0 + i, h2) for i in range(NB) for h2 in range(2)] + pend + nxt)
                        run_interleaved([pair(b0 + i) for i in range(NB)] + pend)
                    run_interleaved(pend + nxt)
                    def chain_gen(par=par, s=s, seg=seg, d=d, hp=hp, eW=eW, Gall=Gall, SlW=SlW, Sall=Sall, Sbd=Sbd, RhT=RhT, YlT=YlT,
                                  kE=kE, kG=kG, kSl=kSl, kS=kS, kSb=kSb, kR=kR, kY=kY):
                        ytm, kyt = ytmp2[par], ('ytmpc', par)
                        Sprev, kSp = Sall2[1 - par], ('Sall', 1 - par)
                        if seg > 0:
                            S.op('act', lambda e: e.copy(out=Sall[:, 0, :], in_=Sprev[:, nsub, :].bitcast(F32)), reads=[kSp], writes=[kS])
                        elif M.st0 is not None:
                            S.op('act', lambda e: e.copy(out=Sall[:, 0, :], in_=M.st0[:, d, hp, :]), reads=['st0'], writes=[kS])
                        else:
                            S.op('pool', lambda e: e.memset(Sall[:, 0, :].bitcast(F32), 0.0), writes=[kS])
                        for n in range(nsub):
                            psC, pkC = B.psum()
                            S.op('pe', lambda e, n=n: e.matmul(psC[:, 0:64], Gall[:, n, :], Sall[:, n, :], start=True, stop=True),
                                 reads=[kG, kS], writes=[pkC])
                            yield
                            S.op('dve', lambda e, n=n: e.scalar_tensor_tensor(out=Sall[:, n + 1, :], in0=psC[:, 0:64], scalar=eW[:, n * 32 + 31:n * 32 + 32],
                                                                              in1=SlW[:, n, :], op0=ALU.mult, op1=ALU.add),
                                 reads=[pkC, kE, kSl], writes=[kS])
                            yield
                        for h2 in range(2):
                            hs = slice(64 * h2, 64 * h2 + 64)
                            S.op('act', lambda e, hs=hs: e.copy(out=Sbd[hs, :, hs], in_=Sall[hs, 0:nsub, :].bitcast(F32)), reads=[kS], writes=[kSb])
                        yield
                        tw = SEG
                        psY, pkY = B.psum()
                        for n in range(nsub):
                            c0 = n * 32
                            S.op('pe', lambda e, n=n, c0=c0: e.matmul(psY[:, c0:c0 + 32], Sbd[:, n, :], RhT[:, n * 32:(n + 1) * 32], start=True, stop=True),
                                 reads=[kSb, kR], writes=[pkY])
                        yield
                        S.op('dve', lambda e: e.tensor_tensor(out=ytm[:, 0:tw], in0=psY[:, 0:tw], in1=YlT[:, 0:tw], op=ALU.add),
                             reads=[pkY, kY], writes=[kyt])
                        if d == 0:
                            yv = ysum[:, s, seg * SEG:(seg + 1) * SEG]
                            S.op('pool', lambda e, yv=yv: e.tensor_tensor(out=yv, in0=yv, in1=ytm[:, 0:tw], op=ALU.add), reads=['ysum', kyt], writes=['ysum'])
                        else:
                            lo = L - (seg + 1) * SEG
                            yv = ysum[:, s, lo:lo + tw]
                            S.op('pool', lambda e, yv=yv: e.tensor_tensor(out=yv, in0=yv, in1=ytm[:, tw - 1::-1], op=ALU.add),
                                 reads=['ysum', kyt], writes=['ysum'])
                        if M.ns_out is not None and seg == nseg - 1:
                            psF, pkF = B.psum()
                            S.op('pe', lambda e: e.transpose(psF[0:64, 0:128], Sall[:, nsub, :].bitcast(F32), B.ident[:]), reads=[kS, 'ident'], writes=[pkF])
                            yield
                            S.op('act', lambda e: e.copy(out=M.nst[0:64, :], in_=psF[0:64, 0:128]), reads=[pkF], writes=['nst'])
                            S.dma('sp', M.ns_out[s, d, 2 * hp:2 * hp + 2].rearrange("h v k -> v h k"),
                                  M.nst[0:64, :].rearrange("v (h k) -> v h k", h=2), reads=['nst'])
                    M.pending = chain_gen()
            if M.pending is not None:
                run_interleaved([M.pending])
                M.pending = None
            if hp == 0:
                B.dump('ysum' + M.name, ysum[:], NSL, 'ysum')
            chk(B, 11)
            S.op('act', lambda e: e.activation(out=sq[:], in_=ysum[:], func=AF.Square), reads=['ysum'], writes=['sq'])
            for tt in range(T // 512):
                sl = slice(tt * 512, (tt + 1) * 512)
                ps1, pk1 = B.psum()
                ps2, pk2 = B.psum()
                S.op('pe', lambda e, sl=sl, ps1=ps1: e.matmul(ps1[:, :], B.blk1[:], flat(ysum)[:, sl], start=True, stop=True), reads=['blk1', 'ysum'], writes=[pk1])
                S.op('pe', lambda e, sl=sl, ps2=ps2: e.matmul(ps2[:, :], B.blk1r[:], flat(sq)[:, sl], start=True, stop=True), reads=['blk1r', 'sq'], writes=[pk2])
                S.op('act', lambda e, ps1=ps1: e.activation(out=ytmp[:], in_=ps1[:, :], func=AF.Identity, scale=1.0 / 64), reads=[pk1], writes=['ytmp'])
                S.op('dve', lambda e, sl=sl: e.tensor_tensor(out=flat(tB)[:, sl], in0=ytmp[:], in1=ytmp[:], op=ALU.mult), reads=['ytmp'], writes=['tB'])
                S.op('dve', lambda e, sl=sl, ps2=ps2: e.scalar_tensor_tensor(out=flat(tB)[:, sl], in0=ps2[:, :], scalar=1.0 / 64, in1=flat(tB)[:, sl],
                                                                             op0=ALU.mult, op1=ALU.subtract), reads=[pk2, 'tB'], writes=['tB'])
                S.op('dve', lambda e, sl=sl: e.tensor_scalar_add(out=flat(tB)[:, sl], in0=flat(tB)[:, sl], scalar1=GN_EPS), reads=['tB'], writes=['tB'])
                S.op('act', lambda e, sl=sl: e.activation(out=flat(tB)[:, sl], in_=flat(tB)[:, sl], func=AF.Sqrt), reads=['tB'], writes=['tB'])
                S.op('dve', lambda e, sl=sl: e.reciprocal(out=flat(tB)[:, sl], in_=flat(tB)[:, sl]), reads=['tB'], writes=['tB'])
                S.op('dve', lambda e, sl=sl: e.tensor_tensor(out=flat(ysum)[:, sl], in0=flat(ysum)[:, sl], in1=ytmp[:], op=ALU.subtract),
                     reads=['ysum', 'ytmp'], writes=['ysum'])
                S.op('dve', lambda e, sl=sl: e.tensor_tensor(out=flat(ysum)[:, sl], in0=flat(ysum)[:, sl], in1=flat(tB)[:, sl], op=ALU.mult),
                     reads=['ysum', 'tB'], writes=['ysum'])
                S.op('dve', lambda e, sl=sl: e.tensor_scalar(out=flat(ysum)[:, sl], in0=flat(ysum)[:, sl], scalar1=B.lnxg[:, hp:hp + 1],
                                                             scalar2=B.lnxb[:, hp:hp + 1], op0=ALU.mult, op1=ALU.add),
                     reads=['ysum', 'lnxg', 'lnxb'], writes=['ysum'])
                S.op('dve', lambda e, sl=sl: e.tensor_tensor(out=flat(ysum)[:, sl], in0=flat(ysum)[:, sl], in1=flat(ybon)[:, sl], op=ALU.add),
                     reads=['ysum', 'ybon'], writes=['ysum'])
                psg, pkg = B.psum()
                S.op('pe', lambda e, sl=sl, psg=psg: e.matmul(psg[:, :], B.glora[:, hp * 128:(hp + 1) * 128], gsf[:, sl], start=True, stop=True),
                     reads=['g_lora', 'gsig'], writes=[pkg])
                S.op('dve', lambda e, sl=sl, psg=psg: e.tensor_tensor(out=M.yT[:, 4 + hp, sl], in0=psg[:, :], in1=flat(ysum)[:, sl], op=ALU.mult),
                     reads=[pkg, 'ysum'], writes=[('yT', 4 + hp)])
        B.dump('yr' + M.name, M.yT[:, 4:8, :], [128, 4, T], [('yT', 4 + q) for q in range(4)], BF16)
        S.barrier()


def post_setup(B, es):
    nc, S = B.nc, B.S
    def TT(name, shape, dt=F32):
        return es.enter_context(B.st(name, shape, dt))
    B.fconvT = TT('fconvT_s', [128, NFC, 9]); S.dma('sp', B.fconvT[:], B.inp('fconvT', [128, NFC, 9]), writes=['fconvT'])
    B.fconvb = TT('fconvb_s', [128, NFC]); S.dma('sp', B.fconvb[:], B.inp('fconvb', [128, NFC]), writes=['fconvb'])
    B.w_up_d = B.inp('w_up', [1024, 2 * DFF]).rearrange("(kc p) n -> p kc n", p=128)
    B.w_out_d = B.inp('w_out', [1024, 1024]).rearrange("(kc p) n -> p kc n", p=128)
    B.w_down_d = B.inp('w_down', [DFF, 1024]).rearrange("(fc p) n -> p fc n", p=128)
    B.ln_d = {n: B.inp(n, [1024]) for n in ('ln1_g', 'ln1_b', 'ln2_g', 'ln2_b')}
    B.lnp = {}


def load_ln(B, es, names):
    for n in names:
        t = es.enter_context(B.st(n + '_bc', [128, 1024]))
        B.S.dma('sp', t[:], B.ln_d[n].partition_broadcast(128), writes=[n])
        B.lnp[n] = t


def resid_ln_gen(B, bufs, ps_pair, pks, xres_ap, xres_key, gi, ci, gname, bname, out_ap, out_key, scale_in):
    nc, S = B.nc, B.S
    tmp, tk, lb = bufs
    st, mv, sfx = lb
    ks, km = 'bst' + sfx, 'bmv' + sfx
    for half in range(2):
        hsl = slice(half * 512, (half + 1) * 512)
        S.op('dve', lambda e, half=half, hsl=hsl: e.tensor_tensor(out=tmp[:, hsl], in0=ps_pair[half][:, :], in1=B.gbc[:, ci, gi, hsl], op=ALU.mult),
             reads=[pks[half], ('gbc', ci, gi, half)], writes=[tk])
    S.op('dve', lambda e: e.scalar_tensor_tensor(out=tmp[:], in0=xres_ap, scalar=scale_in, in1=tmp[:], op0=ALU.mult, op1=ALU.add),
         reads=[xres_key, tk], writes=[tk])
    for j in range(2):
        S.op('dve', lambda e, j=j: e.bn_stats(out=st[:, j, :], in_=tmp[:, j * 512:(j + 1) * 512]), reads=[tk], writes=[ks])
    S.op('dve', lambda e: e.bn_aggr(out=mv[:], in_=st[:].rearrange("p a b -> p (a b)")), reads=[ks], writes=[km])
    S.op('dve', lambda e: e.tensor_scalar_add(out=mv[:, 1:2], in0=mv[:, 1:2], scalar1=LN_EPS), reads=[km], writes=[km])
    yield
    S.op('act', lambda e: e.activation(out=mv[:, 1:2], in_=mv[:, 1:2], func=AF.Sqrt), reads=[km], writes=[km])
    yield
    S.op('dve', lambda e: e.reciprocal(out=mv[:, 1:2], in_=mv[:, 1:2]), reads=[km], writes=[km])
    S.op('dve', lambda e: e.tensor_scalar(out=tmp[:], in0=tmp[:], scalar1=mv[:, 0:1], scalar2=mv[:, 1:2], op0=ALU.subtract, op1=ALU.mult),
         reads=[tk, km], writes=[tk])
    S.op('dve', lambda e: e.tensor_tensor(out=tmp[:], in0=tmp[:], in1=B.lnp[gname][:], op=ALU.mult), reads=[tk, gname], writes=[tk])
    S.op('dve', lambda e: e.tensor_tensor(out=out_ap, in0=tmp[:], in1=B.lnp[bname][:], op=ALU.add), reads=[tk, bname], writes=[out_key])


def phase_post(B, M):
    nc, S = B.nc, B.S
    T, L, nseq, ci = M.T, M.L, M.nseq, M.ci
    nblk = T // 128
    with contextlib.ExitStack() as es:
        def TT(name, shape, dt=F32):
            return es.enter_context(B.st(name + M.name, shape, dt))
        x1 = TT('x1', [128, nblk, 1024])
        B.wdown = TT('wdown', [128, NFC, 1024], BF16)
        esA = contextlib.ExitStack()
        B.wout = esA.enter_context(B.st('wout' + M.name, [128, 8, 1024], BF16))
        for hh in range(4):
            S.dma('pool', B.wout[:, 2 * hh:2 * hh + 2, :], B.w_out_d[:, 2 * hh:2 * hh + 2, :], writes=[('wout', hh)])
        for i in range(2):
            S.dma('pool', B.wdown[:, i * 11:(i + 1) * 11, :], B.w_down_d[:, i * 11:(i + 1) * 11, :], writes=[('wdown', i)])
        load_ln(B, esA, ('ln1_g', 'ln1_b'))
        with contextlib.ExitStack() as es2:
            lb = ln_bufs(B, es2, 'rl')
            xts = [es2.enter_context(B.st('pxt', [128, 1024])) for i in range(2)]
            tmps = [es2.enter_context(B.st('rl_tmp', [128, 1024])) for i in range(2)]
            def tile1(tb):
                i = tb % 2
                xt, xk = xts[i], ('pxt', i)
                S.dma('sp', xt[:], M.x_dram[tb * 128:(tb + 1) * 128, :], writes=[xk])
                pss, pks = [], []
                for half in range(2):
                    ps, pk = B.psum()
                    for cc in range(8):
                        S.op('pe', lambda e, cc=cc, ps=ps, half=half: e.matmul(ps[:, :], M.yT[:, cc, tb * 128:(tb + 1) * 128],
                                                                              B.wout[:, cc, half * 512:(half + 1) * 512],
                                                                              start=(cc == 0), stop=(cc == 7)),
                             reads=[('yT', cc), ('wout', cc // 2)], writes=[pk])
                    pss.append(ps); pks.append(pk)
                yield
                yield from resid_ln_gen(B, (tmps[i], ('rl_tmp', i), lb[i]), pss, pks, xt[:], xk, 0, ci, 'ln1_g', 'ln1_b', x1[:, tb, :], ('x1', tb), ALPHA)
            run_window([(lambda tb=tb: tile1(tb)) for tb in range(nblk)], 2)
            S.barrier()
        B.dump('x1' + M.name, x1[:], [128, nblk, 1024], [('x1', tb) for tb in range(nblk)])
        esA.close()
        h2T = M.yT
        load_ln(B, es, ('ln2_g', 'ln2_b'))
        act = TT('act', [128, NFC, T], BF16)
        esU = contextlib.ExitStack()
        TU = lambda name, shape, dt=F32: esU.enter_context(B.st(name + M.name, shape, dt))
        G = 2 if M.grid else 4
        wu = [TU('wu%d' % i, [128, 8, 2 * G * 128], BF16) for i in range(2)]
        S.dma('pool', wu[0][:, :, 0:G * 128], B.w_up_d[:, :, 0:G * 128], writes=[('wu', 0)])
        S.dma('pool', wu[0][:, :, G * 128:2 * G * 128], B.w_up_d[:, :, DFF:DFF + G * 128], writes=[('wu', 0)])
        ln_to_hT(B, None, T, h2T, 'h2T', 32, 24, ci, M.name + 'f', x_sb=x1)
        if M.grid:
            upad = [TU('upad%d' % i, [128, 18, 66]) for i in range(2)]
        else:
            upad = [TU('upad%d' % i, [128, nseq, L + 2]) for i in range(2)]
        for i in range(2):
            S.op('pool', lambda e, i=i: e.memset(upad[i][:], 0.0), writes=[('upad', i)])
        cvs = [TU('cv%d' % i, [128, T]) for i in range(2)]
        for f in range(NFC):
            gi_, j_ = f // G, f % G
            wub, wuk = wu[gi_ % 2], ('wu', gi_ % 2)
            if j_ == 0 and f > 0:
                ng = min(G, NFC - f)
                S.dma('pool', wub[:, :, 0:ng * 128], B.w_up_d[:, :, f * 128:(f + ng) * 128], writes=[wuk])
                S.dma('pool', wub[:, :, G * 128:(G + ng) * 128], B.w_up_d[:, :, DFF + f * 128:DFF + (f + ng) * 128], writes=[wuk])
            uo, go = j_ * 128, (G + j_) * 128
            up, upk = upad[f % 2], ('upad', f % 2)
            cv, cvk = cvs[f % 2], ('cv', f % 2)
            for tt in range(T // 512):
                ps, pk = B.psum()
                for kc in range(8):
                    S.op('pe', lambda e, kc=kc, ps=ps: e.matmul(ps[:, :], wub[:, kc, uo:uo + 128], h2T[:, kc, tt * 512:(tt + 1) * 512],
                                                                start=(kc == 0), stop=(kc == 7)),
                         reads=[wuk] + [('h2T', kc, tb) for tb in range(tt * 4, tt * 4 + 4)], writes=[pk])
                if M.grid:
                    S.op('act', lambda e, ps=ps: e.copy(out=up[:, 1 + tt * 8:1 + tt * 8 + 8, 1:65], in_=ps[:, :].rearrange("p (r c) -> p r c", c=64)),
                         reads=[pk], writes=[upk])
                else:
                    S.op('act', lambda e, ps=ps: e.copy(out=up[:, :, 1:L + 1], in_=ps[:, :].rearrange("p (s l) -> p s l", s=nseq)),
                         reads=[pk], writes=[upk])
            cw_ = B.fconvT
            if M.grid:
                cvv = cv[:].rearrange("p (r c) -> p r c", c=64)
                taps = [(dr, dc) for dr in range(3) for dc in range(3)]
                view = lambda dr, dc: up[:, dr:dr + 16, dc:dc + 64]
                center = 4
            else:
                cvv = cv[:].rearrange("p (s l) -> p s l", s=nseq)
                taps = [(1, dc) for dc in range(3)]
                view = lambda dr, dc: up[:, :, dc:dc + L]
                center = 1
            ctap = taps[center]
            S.op('act', lambda e: e.activation(out=cvv, in_=view(*ctap), func=AF.Identity, scale=cw_[:, f, 4:5], bias=B.fconvb[:, f:f + 1]),
                 reads=[upk, 'fconvT', 'fconvb'], writes=[cvk])
            for (dr, dc) in taps:
                if (dr, dc) == ctap:
                    continue
                wi = dr * 3 + dc
                S.op('dve', lambda e, dr=dr, dc=dc, wi=wi: e.scalar_tensor_tensor(out=cvv, in0=view(dr, dc), scalar=cw_[:, f, wi:wi + 1], in1=cvv,
                                                                                  op0=ALU.mult, op1=ALU.add),
                     reads=[upk, 'fconvT', cvk], writes=[cvk])
            S.op('act', lambda e: e.activation(out=cv[:], in_=cv[:], func=AF.Gelu_apprx_tanh), reads=[cvk], writes=[cvk])
            for tt in range(T // 512):
                ps, pk = B.psum()
                for kc in range(8):
                    S.op('pe', lambda e, kc=kc, ps=ps: e.matmul(ps[:, :], wub[:, kc, go:go + 128], h2T[:, kc, tt * 512:(tt + 1) * 512],
                                                                start=(kc == 0), stop=(kc == 7)),
                         reads=[wuk] + [('h2T', kc, tb) for tb in range(tt * 4, tt * 4 + 4)], writes=[pk])
                S.op('dve', lambda e, ps=ps, tt=tt: e.tensor_tensor(out=act[:, f, tt * 512:(tt + 1) * 512], in0=ps[:, :], in1=cv[:, tt * 512:(tt + 1) * 512],
                                                                    op=ALU.mult), reads=[pk, cvk], writes=[('act', f)])
        S.barrier()
        esU.close()
        with contextlib.ExitStack() as es2:
            lb = ln_bufs(B, es2, 'rl2')
            ots = [es2.enter_context(B.st('pot', [128, 1024])) for i in range(2)]
            tmps = [es2.enter_context(B.st('rl_tmp2', [128, 1024])) for i in range(2)]
            def tile2(tb):
                i = tb % 2
                ot, ok_ = ots[i], ('pot', i)
                pss, pks = [], []
                for half in range(2):
                    ps, pk = B.psum()
                    for f in range(NFC):
                        S.op('pe', lambda e, f=f, ps=ps, half=half: e.matmul(ps[:, :], act[:, f, tb * 128:(tb + 1) * 128],
                                                                            B.wdown[:, f, half * 512:(half + 1) * 512],
                                                                            start=(f == 0), stop=(f == NFC - 1)),
                             reads=[('act', f), ('wdown', f // 11)], writes=[pk])
                    pss.append(ps); pks.append(pk)
                yield
                yield from resid_ln_gen(B, (tmps[i], ('rl_tmp2', i), lb[i]), pss, pks, x1[:, tb, :], ('x1', tb), 1, ci, 'ln2_g', 'ln2_b', ot[:], ok_, ALPHA)
                S.dma('sp', M.out_dram[tb * 128:(tb + 1) * 128, :], ot[:], reads=[ok_])
            run_window([(lambda tb=tb: tile2(tb)) for tb in range(nblk)], 2)
            S.barrier()

def build_full(dbg=()):
    B = Builder(dbg=dbg)
    nc, S = B.nc, B.S
    phase_mod(B)
    Ms = Mix(); Ms.name = 's'; Ms.T = 1024; Ms.L = 1024; Ms.nseq = 1; Ms.ci = 0; Ms.grid = True
    Mp = Mix(); Mp.name = 'p'; Mp.T = 512; Mp.L = 256; Mp.nseq = 2; Mp.ci = 1; Mp.grid = False
    Ms.x_dram = B.inp('xs', [1024, D]); Mp.x_dram = B.inp('xp', [512, D])
    Ms.out_dram = B.outp('ys', [1024, D]); Mp.out_dram = B.outp('yp', [512, D])
    Ms.yT = B.sb('yT_s', [128, 8, 1024], BF16); Mp.yT = B.sb('yT_p', [128, 8, 512], BF16)
    B.w_in_d = B.inp('w_in', [1024, CIN]).rearrange("(kc p) n -> p kc n", p=128)
    B.convT = B.sb('convT_s', [128, 26, 3]); S.dma('sp', B.convT[:], B.inp('convT', [128, 26, 3]), writes=['convT'])
    B.hbias = B.sb('hbias_s', [128, 4]); S.dma('sp', B.hbias[:], B.inp('hbias', [128, 4]), writes=['hbias'])
    alloc_dft(B)
    start_ada_stream(B)
    phase_filters(B, (1024,))
    mod_part(B, 0)
    phase_filters(B, (256,))
    mod_part(B, 1)
    phase_mix_hyena(B, Mp)
    mod_part(B, 2)
    B.es_aw.close()
    phase_mix_hyena(B, Ms)
    B.es_dft.close()
    with contextlib.ExitStack() as es:
        rwkv_setup(B, es)
        Ms.st0 = es.enter_context(B.st('st0_s', [128, 2, 4, 64]))
        S.dma('sp', Ms.st0[:], B.inp('st0T', [128, 2, 4, 64]), writes=['st0'])
        Ms.ns_out = None
        Mp.st0 = None
        Mp.ns_out = B.outp('ns', [2, 2, 8, 64, 64])
        Mp.nst = es.enter_context(B.st('nst_s', [64, 128]))
        phase_rwkv(B, Mp)
        phase_rwkv(B, Ms)
    with contextlib.ExitStack() as es:
        post_setup(B, es)
        phase_post(B, Ms)
        phase_post(B, Mp)
    S.finish()
    return B


_CACHE = {}


def kernel(**inputs):
    B = build_full()
    inp = {k: np.asarray(v) for k, v in inputs.items()}
    g = shared_inputs(inp)
    in_maps = []
    for core in range(8):
        m = dict(g)
        m.update(core_inputs(inp, core))
        in_maps.append({'i_' + k: np.ascontiguousarray(m[k]) for k in B.din})
    res = run_bass_kernel_spmd(B.nc, in_maps, core_ids=list(range(8)))
    ys = np.stack([np.asarray(r['o_ys'], np.float32) for r in res.results], 0)
    yp = np.concatenate([np.asarray(r['o_yp'], np.float32).reshape(2, 256, D) for r in res.results], 0)
    ns = np.concatenate([np.asarray(r['o_ns'], np.float32) for r in res.results], 0)[:, None]
    return (np.ascontiguousarray(yp), np.ascontiguousarray(ys), np.ascontiguousarray(ns))
```

```python
import math
import contextlib
import numpy as np
import ml_dtypes
import concourse.bass as bass
import concourse.mybir as mybir
from concourse.bass_utils import run_bass_kernel_spmd

F32 = mybir.dt.float32
BF16 = mybir.dt.bfloat16
I32 = mybir.dt.int32
F32R = mybir.dt.float32r
AF = mybir.ActivationFunctionType
ALU = mybir.AluOpType
AX = mybir.AxisListType

D = 1024
CIN = 3328
DFF = 2816
NFC = 22
LN_EPS = 1e-5
GN_EPS = 64e-5
ALPHA = 2.0 ** 0.25
TWO_PI = 6.283185
NDS = 24
ATTACH_WAIT = True
NHW = 16


class Sched:
    def __init__(self, nc):
        self.nc = nc
        self.e = dict(pe=nc.tensor, act=nc.scalar, dve=nc.vector, pool=nc.gpsimd, sp=nc.sync)
        self.sem = {k: nc.alloc_semaphore("sem_" + k) for k in ('pe', 'act', 'dve', 'pool')}
        self.cnt = {k: 0 for k in self.sem}
        self.seen = {k: {} for k in self.e}
        self.dsem = [nc.alloc_semaphore("dsem%d" % i) for i in range(NDS)]
        self.dtot = [0] * NDS
        self.dnext = {'hw': 0, 'sw': 0}
        self.W = {}
        self.R = {}
        self.nwaits = 0
        self.mute = False

    def _wait(self, c, tok, defer=None):
        if tok is None:
            return
        if tok[0] == 'c':
            _, p, idx = tok
            if p == c:
                if c == 'pe':
                    return
            if self.seen[c].get(p, 0) >= idx:
                return
            sem, val = self.sem[p], idx
            self.seen[c][p] = idx
        else:
            _, j, tot = tok
            key = ('d', j)
            if self.seen[c].get(key, 0) >= tot:
                return
            sem, val = self.dsem[j], tot
            self.seen[c][key] = tot
        if defer is not None:
            defer.append((sem, val))
        else:
            self.e[c].wait_ge(sem, val)
        self.nwaits += 1

    def _deps(self, reads, writes):
        deps = []
        for k in reads:
            if k in self.W:
                deps.append(self.W[k])
        for k in writes:
            if k in self.W:
                deps.append(self.W[k])
            deps.extend(self.R.get(k, ()))
        return deps

    def _commit(self, tok, reads, writes):
        for k in reads:
            self.R.setdefault(k, []).append(tok)
        for k in writes:
            self.W[k] = tok
            self.R[k] = []

    def op(self, c, fn, reads=(), writes=()):
        if self.mute:
            return None
        pr = [k for k in reads if isinstance(k, tuple) and k[0] == 'ps']
        if pr:
            writes = list(writes) + [k for k in pr if k not in writes]
        pend = []
        for d in self._deps(reads, writes):
            self._wait(c, d, defer=pend)
        attach = ATTACH_WAIT and len(pend) > 0
        for sem, val in (pend[:-1] if attach else pend):
            self.e[c].wait_ge(sem, val)
        ins = fn(self.e[c])
        if attach:
            ins._wait_ge(pend[-1][0], pend[-1][1])
        self.cnt[c] += 1
        ins.then_inc(self.sem[c], 1)
        tok = ('c', c, self.cnt[c])
        self._commit(tok, reads, writes)
        return tok

    def dma(self, q, out, in_, reads=(), writes=()):
        if self.mute:
            return None
        if q == 'pool':
            j = NHW + self.dnext['sw']
            self.dnext['sw'] = (self.dnext['sw'] + 1) % (NDS - NHW)
        else:
            j = self.dnext['hw']
            self.dnext['hw'] = (self.dnext['hw'] + 1) % NHW
        if self.dtot[j] > 0:
            self._wait(q, ('d', j, self.dtot[j]))
        for d in self._deps(reads, writes):
            self._wait(q, d)
        self.e[q].dma_start(out=out, in_=in_).then_inc(self.dsem[j], 16)
        self.dtot[j] += 16
        tok = ('d', j, self.dtot[j])
        self._commit(tok, reads, writes)
        return tok

    def barrier(self):
        if self.mute:
            return
        for c in self.e:
            for p in self.sem:
                if p != c and self.cnt[p] > 0:
                    self._wait(c, ('c', p, self.cnt[p]))
            for j in range(NDS):
                if self.dtot[j] > 0:
                    self._wait(c, ('d', j, self.dtot[j]))
        self.W.clear()
        self.R.clear()

    def finish(self):
        for j in range(NDS):
            if self.dtot[j] > 0:
                self._wait('sp', ('d', j, self.dtot[j]))
        for p in self.sem:
            if self.cnt[p] > 0:
                self._wait('sp', ('c', p, self.cnt[p]))


def _dft_tables(L):
    d = np.arange(L, dtype=np.float64)
    ang = np.pi * np.outer(d, d) / L
    C = np.cos(ang)
    S = np.sin(ang)
    alt = (-1.0) ** d
    S[:, 0] = alt
    n = L // 128
    Cl = C.reshape(n, 128, L).transpose(1, 0, 2)
    Sl = S.reshape(n, 128, L).transpose(1, 0, 2)
    Sgl = S.T.reshape(n, 128, L).transpose(1, 0, 2)
    wk = np.full((L,), 1.0 / L)
    wk[0] = 1.0 / (2 * L)
    wkl = wk.reshape(n, 128).T
    return (np.ascontiguousarray(Cl).astype(ml_dtypes.bfloat16), np.ascontiguousarray(Sl).astype(ml_dtypes.bfloat16),
            np.ascontiguousarray(Sgl).astype(ml_dtypes.bfloat16), np.ascontiguousarray(wkl).astype(np.float32))


def _feat_table(L):
    t = np.arange(L, dtype=np.float32)[:, None] / np.float32(L)
    bands = np.arange(1, 17, dtype=np.float32)[None, :]
    ang = np.float32(2.0 * math.pi) * bands * t
    feat = np.concatenate([t, np.sin(ang), np.cos(ang)], axis=-1).astype(np.float32)
    tcol = (np.arange(L, dtype=np.float32) / np.float32(L)).reshape(L // 128, 128).T
    ftp = np.zeros((64, L), np.float32)
    ftp[:33] = feat.T
    return ftp, np.ascontiguousarray(tcol)


def _consts():
    c = {}
    c['ident'] = np.eye(128, dtype=np.float32)
    sel = np.zeros((2, 2, 128), np.float32)
    sel[0, 0, :] = 1.0
    sel[1, 1, :] = 1.0
    c['selc'] = sel
    for L in (1024, 256):
        Cl, Sl, Sgl, wk = _dft_tables(L)
        c['Cm%d' % L] = Cl
        c['Sm%d' % L] = Sl
        c['Sg%d' % L] = Sgl
        c['wk%d' % L] = wk
        ft, tcol = _feat_table(L)
        c['featT%d' % L] = ft
        c['tcol%d' % L] = tcol
    slow = abs(math.log(1e-2) / 1.5)
    fast = abs(math.log(1e-2) / 0.3)
    deltas = np.linspace(slow, fast, 1024, dtype=np.float32)
    c['negdelta'] = np.ascontiguousarray(np.broadcast_to(-deltas[None, :], (128, 1024))).astype(np.float32)
    sm = np.ones((128, 1024), np.float32)
    sm[:, ::32] = 0.0
    c['scanmask'] = sm
    t = np.arange(128)
    same = (t[:, None] // 32) == (t[None, :] // 32)
    c['mstrict'] = (same & (t[None, :] < t[:, None])).astype(np.float32)
    c['mincl'] = (same & (t[None, :] <= t[:, None])).astype(np.float32)
    c['mstrictT'] = np.ascontiguousarray(c['mstrict'].T)
    c['minclT'] = np.ascontiguousarray(c['mincl'].T)
    c['ident4'] = np.ascontiguousarray(np.broadcast_to(np.eye(128, dtype=np.float32)[:, None, :], (128, 4, 128)))
    c['rowmask'] = (t[:, None] // 32 == np.arange(4)[None, :]).astype(np.float32)
    return c


CONST_DT = {'Cm1024': BF16, 'Sm1024': BF16, 'Cm256': BF16, 'Sm256': BF16}


def _col(v, nchunk):
    return np.ascontiguousarray(np.asarray(v, np.float32).reshape(nchunk, 128).T)


class Builder:
    def __init__(self, dbg=()):
        self.nc = bass.Bass("TRN2", target_bir_lowering=False)
        self.S = Sched(self.nc)
        self.dbg = set(dbg)
        self.din = {}
        self.dout = {}
        self.es = contextlib.ExitStack()
        self.es_dft = contextlib.ExitStack()
        self.es_aw = contextlib.ExitStack()
        self.ps = [self.nc.alloc_psum_tensor("ps%d" % i, [128, 512], F32) for i in range(8)]
        self.psi = 0
        self.ps_reserved = set()

    def inp(self, name, shape, dt=F32):
        t = self.nc.dram_tensor("i_" + name, list(shape), dt, kind="ExternalInput").ap()
        self.din[name] = t
        return t

    def outp(self, name, shape, dt=F32):
        t = self.nc.dram_tensor("o_" + name, list(shape), dt, kind="ExternalOutput").ap()
        self.dout[name] = t
        return t

    def sb(self, name, shape, dt=F32):
        return self.nc.alloc_sbuf_tensor(name, list(shape), dt)

    def st(self, name, shape, dt=F32):
        self.uid = getattr(self, 'uid', 0) + 1
        return self.nc.sbuf_tensor("%s_u%d" % (name, self.uid), list(shape), dt)

    def psum(self):
        i = self.psi
        self.psi = (self.psi + 1) % 8
        if i in self.ps_reserved:
            return self.psum()
        return self.ps[i], ('ps', i)

    def dump(self, name, tile_ap, shape, key, dt=F32):
        if name not in self.dbg:
            return
        o = self.outp("dbg_" + name, shape, dt)
        self.S.dma('sp', o, tile_ap, reads=key if isinstance(key, list) else [key])


def phase_mod(B):
    nc, S = B.nc, B.S
    cT_d = B.inp('cT', [128, 8, 2])
    B.ada_w_d = B.inp('ada_w', [1024, 6144]).rearrange("(kc p) n -> p kc n", p=128)
    ada_bc_d = B.inp('ada_bc', [128, 48])
    ada_b_d = B.inp('ada_b', [6144])
    ident_d = B.inp('ident', [128, 128])
    sel_d = B.inp('selc', [2, 2, 128])

    B.modc = B.sb('modc', [128, 48, 2])
    B.gbc = B.sb('gbc', [128, 2, 2, 1024])
    B.ident = B.sb('ident_s', [128, 128])
    B.ones = B.sb('ones_s', [128, 128])
    B.identb = B.sb('identb_s', [128, 128], BF16)
    B.onesb = B.sb('onesb_s', [128, 2], BF16)
    B.adab = B.sb('adab_s', [128, 48])
    B.cTb = B.sb('cTb_s', [128, 8, 2], BF16)
    B.selc = B.sb('selc_s', [2, 2, 128])
    S.dma('sp', B.ident[:], ident_d, writes=['ident'])
    S.dma('sp', B.selc[:], sel_d, writes=['selc'])
    S.dma('sp', B.adab[:], ada_bc_d, writes=['adab'])
    B.ada_b_d = ada_b_d
    S.op('pool', lambda e: e.memset(B.ones[:], 1.0), writes=['ones'])
    S.op('pool', lambda e: e.memset(B.onesb[:], 1.0), writes=['onesb'])
    S.op('pool', lambda e: e.memset(B.modc[:], 0.0), writes=['modc'])
    S.op('dve', lambda e: e.tensor_copy(out=B.identb[:], in_=B.ident[:]), reads=['ident'], writes=['identb'])
    with contextlib.ExitStack() as es:
        cTs = es.enter_context(B.st("cTs", [128, 8, 2], F32))
        S.dma('sp', cTs[:], cT_d, writes=['cTs'])
        S.op('act', lambda e: e.activation(out=B.cTb[:], in_=cTs[:], func=AF.Silu), reads=['cTs'], writes=['cTb'])
        for lo in (8, 32):
            S.op('dve', lambda e, lo=lo: e.tensor_scalar_add(out=B.adab[:, lo:lo + 8], in0=B.adab[:, lo:lo + 8], scalar1=1.0),
                 reads=['adab'], writes=['adab'])
        S.barrier()


def start_ada_stream(B):
    B.aw = [B.es_aw.enter_context(B.st("aw%d" % i, [128, 8, 512], BF16)) for i in range(4)]
    for blk in range(4):
        B.S.dma('pool', B.aw[blk][:], B.ada_w_d[:, :, blk * 512:(blk + 1) * 512], writes=[('aw', blk)])


def mod_part(B, part):
    nc, S = B.nc, B.S
    with contextlib.ExitStack() as es:
        mrow = es.enter_context(B.st("mrow", [2, 512], F32))
        gbias = es.enter_context(B.st("gbias", [128, 1024], F32))
        if part == 1:
            S.dma('sp', gbias[:], B.ada_b_d[2048:3072].partition_broadcast(128), writes=['gbias'])
        if part == 2:
            S.dma('sp', gbias[:], B.ada_b_d[5120:6144].partition_broadcast(128), writes=['gbias'])
        for blk in range(part * 4, part * 4 + 4):
            buf, bk = B.aw[blk % 4], ('aw', blk % 4)
            ps, pk = B.psum()
            for kc in range(8):
                S.op('pe', lambda e, kc=kc, ps=ps: e.matmul(ps[0:2, :], B.cTb[:, kc, :], buf[:, kc, :], start=(kc == 0), stop=(kc == 7)),
                     reads=[bk, 'cTb'], writes=[pk])
            if part < 2:
                S.dma('pool', buf[:], B.ada_w_d[:, :, (blk + 4) * 512:(blk + 5) * 512], writes=[bk])
            if blk in (4, 5, 10, 11):
                gi, half = (0 if blk < 6 else 1), blk % 2
                S.op('act', lambda e, ps=ps: e.copy(out=mrow[:], in_=ps[0:2, :]), reads=[pk], writes=['mrow'])
                for c in range(2):
                    ps2, pk2 = B.psum()
                    S.op('pe', lambda e, c=c, ps2=ps2: e.matmul(ps2[:, :], B.selc[:, c, :], mrow[:], start=True, stop=True),
                         reads=['selc', 'mrow'], writes=[pk2])
                    S.op('dve', lambda e, c=c, ps2=ps2: e.tensor_tensor(out=B.gbc[:, c, gi, half * 512:(half + 1) * 512], in0=ps2[:, :],
                                                                        in1=gbias[:, half * 512:(half + 1) * 512], op=ALU.add),
                         reads=[pk2, 'gbias'], writes=[('gbc', c, gi, half)])
            else:
                S.op('act', lambda e, ps=ps: e.copy(out=mrow[:], in_=ps[0:2, :]), reads=[pk], writes=['mrow'])
                pt, pkt = B.psum()
                for fcl in range(4):
                    S.op('pe', lambda e, fcl=fcl, pt=pt: e.transpose(pt[:, fcl * 2:fcl * 2 + 2], mrow[:, fcl * 128:(fcl + 1) * 128], B.ident[0:2, 0:2]),
                         reads=['mrow', 'ident'], writes=[pkt])
                for c in range(2):
                    S.op('dve', lambda e, c=c, pt=pt: e.tensor_tensor(out=B.modc[:, blk * 4:(blk + 1) * 4, c], in0=pt[:, c:8:2],
                                                                      in1=B.adab[:, blk * 4:(blk + 1) * 4], op=ALU.add),
                         reads=[pkt, 'adab'], writes=['modc'])
        S.barrier()
    if part == 2:
        B.dump('modc', B.modc[:], [128, 48, 2], 'modc')
        B.dump('gbc', B.gbc[:, 0, 0, :], [128, 1024], ('gbc', 0, 0, 1))


def alloc_dft(B):
    nc, S = B.nc, B.S
    B.fin = dict(w1=B.inp('filt_w1', [64, 64]), w2=B.inp('filt_w2', [64, 64]), w3=B.inp('filt_w3', [64, 1024]),
                 fc=B.inp('filt_cols', [64, 4]), nd=B.inp('negdelta', [128, 1024]))
    B.fin.update({'ft%d' % L_: B.inp('featT%d' % L_, [64, L_]) for L_ in (1024, 256)})
    B.fin.update({'tc%d' % L_: B.inp('tcol%d' % L_, [128, L_ // 128]) for L_ in (1024, 256)})
    B.Cm, B.Sm, B.Sg, B.wk, B.Hr, B.Hi, B.rn = {}, {}, {}, {}, {}, {}, {}
    for L in (1024, 256):
        nT = L // 128
        TD = lambda name, shape, dt=F32: B.es_dft.enter_context(B.st(name, shape, dt))
        B.Cm[L] = TD('Cm%d' % L, [128, nT, L], BF16)
        B.Sm[L] = TD('Sm%d' % L, [128, nT, L], BF16)
        B.wk[L] = TD('wk%d' % L, [128, nT])
        B.Hr[L] = TD('Hr%d' % L, [128, nT, 512], BF16)
        B.Hi[L] = TD('Hi%d' % L, [128, nT, 512], BF16)
        B.rn[L] = TD('rn%d' % L, [128, 4])
        S.dma('act', B.Cm[L][:], B.inp('Cm%d' % L, [128, nT, L], BF16), writes=[('Cm', L)])
        S.dma('act', B.Sm[L][:], B.inp('Sm%d' % L, [128, nT, L], BF16), writes=[('Sm', L)])
        B.Sg[L] = TD('Sg%d' % L, [128, nT, L], BF16)
        S.dma('act', B.Sg[L][:], B.inp('Sg%d' % L, [128, nT, L], BF16), writes=[('Sg', L)])
        S.dma('sp', B.wk[L][:], B.inp('wk%d' % L, [128, nT]), writes=[('wk', L)])


def phase_filters(B, Ls):
    nc, S = B.nc, B.S
    with contextlib.ExitStack() as es:
        def T(name, shape, dt=F32):
            return es.enter_context(B.st(name, shape, dt))
        w1s = T('w1s', [64, 64]); w2s = T('w2s', [64, 64]); w3s = T('w3s', [64, 1024], F32R); w3s32 = T('w3s32', [64, 1024]); fcs = T('fcs', [64, 4])
        sc = T('fsc', [64, 4])
        nds = T('nds', [128, 1024])
        S.dma('sp', w1s[:], B.fin['w1'], writes=['w1s'])
        S.dma('sp', w2s[:], B.fin['w2'], writes=['w2s'])
        S.dma('sp', w3s32[:], B.fin['w3'], writes=['w3s32'])
        S.op('dve', lambda e: e.tensor_copy(out=w3s[:], in_=w3s32[:]), reads=['w3s32'], writes=['w3s'])
        S.dma('sp', fcs[:], B.fin['fc'], writes=['fcs'])
        S.dma('sp', nds[:], B.fin['nd'], writes=['nds'])
        inv2pi = 1.0 / (2.0 * math.pi)
        for l in range(2):
            S.op('dve', lambda e, l=l: e.tensor_scalar(out=sc[:, 2 * l:2 * l + 1], in0=fcs[:, 2 * l:2 * l + 1],
                                                        scalar1=inv2pi, scalar2=None, op0=ALU.mult),
                 reads=['fcs'], writes=['fsc'])
            S.op('dve', lambda e, l=l: e.tensor_tensor(out=sc[:, 2 * l + 1:2 * l + 2], in0=sc[:, 2 * l:2 * l + 1],
                                                        in1=fcs[:, 2 * l + 1:2 * l + 2], op=ALU.mult),
                 reads=['fcs', 'fsc'], writes=['fsc'])
        for L in Ls:
            nT = L // 128
            with contextlib.ExitStack() as es2:
                def T2(name, shape, dt=F32):
                    return es2.enter_context(B.st(name + str(L), shape, dt))
                ft = T2('ft', [64, L]); tcs = T2('tcs', [128, nT])
                hA = T2('hA', [64, L]); hB = T2('hB', [64, L], F32R)
                ytmp = T2('ytmp', [64, 512]); yi = T2('yi', [64, 512], I32); yf = T2('yf', [64, 512])
                hfull = [T2('hfull%d' % i, [128, 1024]) for i in range(2)]
                win = T2('win', [128, 512]); absb = T2('absb', [128, 1024], BF16)
                hp = T2('hp', [128, nT, 512], BF16); hm = T2('hm', [128, nT, 512], BF16)
                S.dma('sp', ft[:], B.fin['ft%d' % L], writes=['ft'])
                S.dma('sp', tcs[:], B.fin['tc%d' % L], writes=['tcs'])
                tw = min(L, 512)
                for layer in range(2):
                    src, dst = (ft, hA) if layer == 0 else (hA, hB)
                    wl = w1s if layer == 0 else w2s
                    for tt in range(L // tw):
                        sl = slice(tt * tw, (tt + 1) * tw)
                        ps, pk = B.psum()
                        S.op('pe', lambda e, ps=ps, sl=sl, src=src, wl=wl: e.matmul(ps[0:64, 0:tw], wl[:, :], src[:, sl],
                                                                                   start=True, stop=True),
                             reads=['w1s', 'w2s', 'ft', ('h', layer - 1)], writes=[pk])
                        S.op('act', lambda e, ps=ps, layer=layer: e.activation(
                            out=ytmp[:, 0:tw], in_=ps[0:64, 0:tw], func=AF.Identity,
                            scale=sc[:, 2 * layer:2 * layer + 1], bias=sc[:, 2 * layer + 1:2 * layer + 2]),
                            reads=[pk, 'fsc'], writes=['ytmp'])
                        S.op('dve', lambda e: e.tensor_copy(out=yi[:, 0:tw], in_=ytmp[:, 0:tw]), reads=['ytmp'], writes=['yi'])
                        S.op('dve', lambda e: e.tensor_copy(out=yf[:, 0:tw], in_=yi[:, 0:tw]), reads=['yi'], writes=['yf'])
                        S.op('dve', lambda e: e.tensor_tensor(out=ytmp[:, 0:tw], in0=ytmp[:, 0:tw], in1=yf[:, 0:tw],
                                                              op=ALU.subtract), reads=['ytmp', 'yf'], writes=['ytmp'])
                        S.op('act', lambda e, dst=dst, sl=sl: e.activation(out=dst[:, sl], in_=ytmp[:, 0:tw], func=AF.Sin,
                                                                            scale=TWO_PI),
                             reads=['ytmp'], writes=[('h', layer)])
                B.dump('hA%d' % L, hA[:], [64, L], ('h', 0))
                B.dump('hB%d' % L, hB[:], [64, L], ('h', 1))
                psn, pkn = B.psum()
                B.ps_reserved.add(pkn[1])
                first = True
                for tb in range(nT):
                    hf, hfk = hfull[tb % 2], ('hfull', tb % 2)
                    for half in range(2):
                        hs = slice(half * 512, (half + 1) * 512)
                        ps, pk = B.psum()
                        S.op('pe', lambda e, ps=ps, tb=tb, hs=hs: e.matmul(ps[:, :], hB[:, tb * 128:(tb + 1) * 128], w3s[:, hs],
                                                                           start=True, stop=True),
                             reads=[('h', 1), 'w3s'], writes=[pk])
                        S.op('act', lambda e, tb=tb, hs=hs: e.activation(out=win[:], in_=nds[:, hs], func=AF.Exp,
                                                                          scale=tcs[:, tb:tb + 1]),
                             reads=['nds', 'tcs'], writes=['win'])
                        S.op('dve', lambda e, ps=ps, hf=hf, hs=hs: e.tensor_tensor(out=hf[:, hs], in0=ps[:, :], in1=win[:], op=ALU.mult),
                             reads=[pk, 'win'], writes=[hfk])
                    S.op('act', lambda e, hf=hf: e.activation(out=absb[:], in_=hf[:], func=AF.Abs), reads=[hfk], writes=['absb'])
                    for side in range(2):
                        for cc in range(4):
                            c0 = side * 512 + cc * 128
                            S.op('pe', lambda e, c0=c0, cc=cc, first=first: e.matmul(
                                psn[:, cc * 2:cc * 2 + 2], absb[:, c0:c0 + 128], B.identb[:, 0:2] if False else B.onesb[:, 0:2],
                                start=first, stop=(tb == nT - 1 and side == 1), skip_group_check=True),
                                reads=['absb', 'onesb'], writes=[pkn])
                            first = False
                    if tb == 0:
                        S.op('dve', lambda e, hf=hf: e.memset(hf[0:1, 512:1024], 0.0), reads=['absb'], writes=[hfk])
                    S.op('pool', lambda e, tb=tb, hf=hf: e.tensor_tensor(out=hp[:, tb, :], in0=hf[:, 0:512], in1=hf[:, 512:1024], op=ALU.add),
                         reads=[hfk], writes=[('hp', tb)])
                    S.op('pool', lambda e, tb=tb, hf=hf: e.tensor_tensor(out=hm[:, tb, :], in0=hf[:, 512:1024], in1=hf[:, 0:512], op=ALU.subtract),
                         reads=[hfk], writes=[('hm', tb)])
                S.op('dve', lambda e: e.reciprocal(out=B.rn[L][:], in_=psn[:, 0:8:2]), reads=[pkn], writes=[('rn', L)])
                B.ps_reserved.discard(pkn[1])
                for (mat, mk, src, sk, dst, dk) in ((B.Cm[L], ('Cm', L), hp, 'hp', B.Hr[L], ('Hr', L)),
                                                    (B.Sm[L], ('Sm', L), hm, 'hm', B.Hi[L], ('Hi', L))):
                    for kc in range(nT):
                        ps, pk = B.psum()
                        for tb in range(nT):
                            S.op('pe', lambda e, ps=ps, tb=tb, kc=kc, mat=mat, src=src: e.matmul(
                                ps[:, :], mat[:, tb, kc * 128:(kc + 1) * 128], src[:, tb, :],
                                start=(tb == 0), stop=(tb == nT - 1)), reads=[mk, (sk, tb)], writes=[pk])
                        S.op('act', lambda e, ps=ps, kc=kc, dst=dst: e.activation(
                            out=dst[:, kc, :], in_=ps[:, :], func=AF.Identity, scale=B.wk[L][:, kc:kc + 1]),
                            reads=[pk, ('wk', L)], writes=[dk])
                ps, pk = B.psum()
                for tb in range(nT):
                    S.op('pe', lambda e, ps=ps, tb=tb: e.matmul(ps[0:1, :], B.Sm[L][:, tb, 0:1], hp[:, tb, :],
                                                                start=(tb == 0), stop=(tb == nT - 1)),
                         reads=[('Sm', L), ('hp', tb)], writes=[pk])
                S.op('act', lambda e, ps=ps: e.activation(out=B.Hi[L][0:1, 0, :], in_=ps[0:1, :], func=AF.Identity,
                                                          scale=B.wk[L][0:1, 0:1]),
                     reads=[pk, ('wk', L)], writes=[('Hi', L)])
                B.dump('Hr%d' % L, B.Hr[L][:], [128, nT, 512], ('Hr', L), BF16)
                B.dump('Hi%d' % L, B.Hi[L][:], [128, nT, 512], ('Hi', L), BF16)
                B.dump('rn%d' % L, B.rn[L][:], [128, 4], ('rn', L))
                S.barrier()


def shared_inputs(inp):
    g = {}
    a = lambda n: np.asarray(inp[n][0], np.float32)
    g.update(_consts())
    g['ada_w'] = a('ada_w')
    g['ada_bc'] = _col(a('ada_b'), 48)
    g['ada_b'] = a('ada_b')
    g['filt_w1'] = np.concatenate([a('filt_w1'), np.zeros((31, 64), np.float32)], 0); g['filt_w2'] = a('filt_w2'); g['filt_w3'] = a('filt_w3')
    ff = a('filt_freq')
    g['filt_cols'] = np.ascontiguousarray(np.stack([ff[0], a('filt_b1'), ff[1], a('filt_b2')], axis=1))
    g['w_in'] = a('w_in')
    g['convT'] = np.ascontiguousarray(a('conv_in').reshape(3, 26, 128).transpose(2, 1, 0))
    g['hbias'] = _col(a('hyena_bias'), 4)
    g['w0c'] = np.ascontiguousarray(a('rwkv_w0').reshape(2, 4, 128).transpose(2, 0, 1))
    g['a0c'] = np.ascontiguousarray(a('rwkv_a0').reshape(2, 4, 128).transpose(2, 0, 1))
    g['w_lora'] = a('rwkv_w_lora'); g['a_lora'] = a('rwkv_a_lora'); g['g_lora'] = a('rwkv_g_lora')
    g['kkc'] = _col(a('rwkv_k_k'), 4); g['kac'] = _col(a('rwkv_k_a'), 4)
    g['rkc'] = _col(a('rwkv_r_k').reshape(-1), 4)
    g['lnxg'] = _col(a('lnx_g'), 4); g['lnxb'] = _col(a('lnx_b'), 4)
    g['w_out'] = a('w_out')
    for n in ('ln1_g', 'ln1_b', 'ln2_g', 'ln2_b'):
        g[n] = a(n)
    g['w_up'] = a('w_ffn_up')
    g['fconvT'] = np.ascontiguousarray(a('ffn_conv_w').reshape(9, NFC, 128).transpose(2, 1, 0))
    g['fconvb'] = _col(a('ffn_conv_b'), NFC)
    g['w_down'] = a('w_ffn_down')
    return g


def core_inputs(inp, core):
    m = {}
    m['xs'] = np.ascontiguousarray(inp['x_sample'][core], dtype=np.float32)
    m['xp'] = np.ascontiguousarray(inp['x_prompt'][2 * core:2 * core + 2].reshape(512, D), dtype=np.float32)
    cv = np.stack([inp['c'][core], inp['c_ctx']], axis=0).astype(np.float32)
    m['cT'] = np.ascontiguousarray(cv.reshape(2, 8, 128).transpose(2, 1, 0))
    st = np.asarray(inp['state_rwkv'][core, 0], np.float32)
    st = st.reshape(2, 4, 2, 64, 64).transpose(2, 4, 0, 1, 3).reshape(128, 2, 4, 64)
    m['st0T'] = np.ascontiguousarray(st)
    return m


def ln_stats(B, bufs, xt, xk):
    nc, S = B.nc, B.S
    st, mv, sfx = bufs
    ks, km = 'bst' + sfx, 'bmv' + sfx
    for j in range(2):
        S.op('dve', lambda e, j=j: e.bn_stats(out=st[:, j, :], in_=xt[:, j * 512:(j + 1) * 512]), reads=[xk], writes=[ks])
    S.op('dve', lambda e: e.bn_aggr(out=mv[:], in_=st[:].rearrange("p a b -> p (a b)")), reads=[ks], writes=[km])
    S.op('dve', lambda e: e.tensor_scalar_add(out=mv[:, 1:2], in0=mv[:, 1:2], scalar1=LN_EPS), reads=[km], writes=[km])
    S.op('act', lambda e: e.activation(out=mv[:, 1:2], in_=mv[:, 1:2], func=AF.Sqrt), reads=[km], writes=[km])
    S.op('dve', lambda e: e.reciprocal(out=mv[:, 1:2], in_=mv[:, 1:2]), reads=[km], writes=[km])
    return mv, km


def ln_bufs(B, es, tag, n=2):
    return [(es.enter_context(B.st("bst" + tag, [128, 2, 6])), es.enter_context(B.st("bmv" + tag, [128, 2])), "%s%d" % (tag, i))
            for i in range(n)]


def run_window(gen_fns, width):
    pending = list(gen_fns)
    active = []
    while pending or active:
        while pending and len(active) < width:
            active.append(pending.pop(0)())
        for g_ in list(active):
            try:
                next(g_)
            except StopIteration:
                active.remove(g_)


def ln_to_hT(B, x_dram, T, hT, hkey, sc_chunk, sh_chunk, ci, tag, x_sb=None, keep=()):
    nc, S = B.nc, B.S
    with contextlib.ExitStack() as es:
        NBUF = 4
        lb = ln_bufs(B, es, tag, n=NBUF)
        xts = [es.enter_context(B.st("lnx" + tag, [128, 1024])) for i in range(NBUF)] if x_sb is None else None
        xhs = [es.enter_context(B.st("lnh" + tag, [128, 1024])) for i in range(NBUF)]

        def tile_gen(tb):
            i = tb % NBUF
            if x_sb is None:
                xa, xk = xts[i][:], ('lnx' + tag, i)
                S.dma('sp', xa, x_dram[tb * 128:(tb + 1) * 128, :], writes=[xk])
                yield
            else:
                xa, xk = x_sb[:, tb, :], ('x1', tb)
            st, mv, sfx = lb[i]
            ks, km = 'bst' + sfx, 'bmv' + sfx
            for j in range(2):
                S.op('dve', lambda e, j=j: e.bn_stats(out=st[:, j, :], in_=xa[:, j * 512:(j + 1) * 512]), reads=[xk], writes=[ks])
            S.op('dve', lambda e: e.bn_aggr(out=mv[:], in_=st[:].rearrange("p a b -> p (a b)")), reads=[ks], writes=[km])
            S.op('dve', lambda e: e.tensor_scalar_add(out=mv[:, 1:2], in0=mv[:, 1:2], scalar1=LN_EPS), reads=[km], writes=[km])
            yield
            S.op('act', lambda e: e.activation(out=mv[:, 1:2], in_=mv[:, 1:2], func=AF.Sqrt), reads=[km], writes=[km])
            yield
            S.op('dve', lambda e: e.reciprocal(out=mv[:, 1:2], in_=mv[:, 1:2]), reads=[km], writes=[km])
            xh, hk = xhs[i], ('lnh' + tag, i)
            S.op('dve', lambda e: e.tensor_scalar(out=xh[:], in0=xa, scalar1=mv[:, 0:1], scalar2=mv[:, 1:2],
                                                  op0=ALU.subtract, op1=ALU.mult), reads=[xk, km], writes=[hk])
            yield
            for half in range(2):
                ps, pk = B.psum()
                for q in range(4):
                    fc = half * 4 + q
                    S.op('pe', lambda e, q=q, fc=fc, ps=ps: e.transpose(ps[:, q * 128:(q + 1) * 128], xh[:, fc * 128:(fc + 1) * 128],
                                                                        B.ident[:]), reads=[hk, 'ident'], writes=[pk])
                yield
                for q in range(4):
                    fc = half * 4 + q
                    S.op('act', lambda e, q=q, fc=fc, ps=ps: e.activation(
                        out=hT[:, fc, tb * 128:(tb + 1) * 128], in_=ps[:, q * 128:(q + 1) * 128], func=AF.Identity,
                        scale=B.modc[:, sc_chunk + fc, ci:ci + 1], bias=B.modc[:, sh_chunk + fc, ci:ci + 1]),
                        reads=[pk, 'modc'], writes=[(hkey, fc, tb)])

        run_window([(lambda tb=tb: tile_gen(tb)) for tb in range(T // 128)], NBUF)
        S.barrier()


def load_w_bf16(B, dst, w_dram_kcn, cols, key, nk=8):
    B.S.dma('pool', dst, w_dram_kcn, writes=[key])


class Mix:
    pass


class StopBuild(Exception):
    pass


def chk(B, level):
    if getattr(B, 'rw_stop', 99) == level and not B.S.mute:
        B.S.barrier()
        B.S.mute = True


def p_chunk(B, M, f, wbuf, wkey, pc_ap, pc_key, wcol=None):
    nc, S = B.nc, B.S
    T, L, nseq = M.T, M.L, M.nseq
    praw = M.praw[M.pi % len(M.praw)]
    prk = ('praw', M.pi % len(M.praw))
    M.pi += 1
    for tt in range(T // 512):
        ps, pk = B.psum()
        for kc in range(8):
            wc0 = (f % 4) * 128 if wcol is None else wcol
            S.op('pe', lambda e, kc=kc, ps=ps, wc0=wc0: e.matmul(ps[:, :], wbuf[:, kc, wc0:wc0 + 128],
                                                        M.hT[:, kc, tt * 512:(tt + 1) * 512], start=(kc == 0), stop=(kc == 7)),
                 reads=[wkey] + [('hT', kc, tb) for tb in range(tt * 4, tt * 4 + 4)], writes=[pk])
        if L >= 512:
            off = tt * 512
            S.op('act', lambda e, ps=ps: e.copy(out=praw[:, 0, 1 + off:1 + off + 512], in_=ps[:, :]), reads=[pk], writes=[prk])
        else:
            S.op('act', lambda e, ps=ps: e.copy(out=praw[:, :, 1:L + 1], in_=ps[:, :].rearrange("p (s l) -> p s l", s=nseq)),
                 reads=[pk], writes=[prk])
    cw = B.convT
    S.op('act', lambda e: e.activation(out=pc_ap, in_=praw[:, :, 1:L + 1], func=AF.Identity, scale=cw[:, f, 1:2]),
         reads=[prk, 'convT'], writes=[pc_key])
    S.op('dve', lambda e: e.scalar_tensor_tensor(out=pc_ap, in0=praw[:, :, 0:L], scalar=cw[:, f, 0:1], in1=pc_ap,
                                                 op0=ALU.mult, op1=ALU.add), reads=[prk, 'convT', pc_key], writes=[pc_key])
    S.op('dve', lambda e: e.scalar_tensor_tensor(out=pc_ap, in0=praw[:, :, 2:L + 2], scalar=cw[:, f, 2:3], in1=pc_ap,
                                                 op0=ALU.mult, op1=ALU.add), reads=[prk, 'convT', pc_key], writes=[pc_key])


def load_win_group(B, M, grp, col0=None, ncol=None):
    if col0 is None:
        col0 = grp * 512
        ncol = min(512, CIN - grp * 512)
    pf = getattr(M, 'prefetched', None)
    if pf is not None and pf[0] == col0 and pf[1] == ncol:
        M.prefetched = None
        return pf[2], pf[3]
    buf = M.wbuf[M.wi % 2]
    key = ('wbuf', M.wi % 2)
    M.wi += 1
    B.S.dma('pool', buf[:, :, 0:ncol], B.w_in_d[:, :, col0:col0 + ncol], writes=[key])
    return buf, key


def mix_setup(B, M, es):
    nc, S = B.nc, B.S
    T, L, nseq, ci = M.T, M.L, M.nseq, M.ci
    M.uid = getattr(M, 'uid', 0) + 1

    def TT(name, shape, dt=F32):
        return es.enter_context(B.st(name + M.name, shape, dt))
    M.hT = TT('hT', [128, 8, T], BF16)
    M.praw = [TT('praw%d' % i, [128, nseq, L + 2]) for i in range(getattr(M, 'npraw', 2))]
    M.pi = 0
    M.wbuf = [TT('wbuf%d' % i, [128, 8, getattr(M, 'wcols', 512)], BF16) for i in range(2)]
    M.wi = 0
    M.prefetched = None
    if getattr(M, 'first_cols', None) is not None:
        col0, ncol = M.first_cols
        buf, key = load_win_group(B, M, None, col0=col0, ncol=ncol)
        M.prefetched = (col0, ncol, buf, key)
    ln_to_hT(B, M.x_dram, T, M.hT, 'hT', 8, 0, ci, M.name, keep=[('wbuf', 0)])
    for i in range(len(M.praw)):
        S.op('pool', lambda e, i=i: e.memset(M.praw[i][:], 0.0), writes=[('praw', i)])


def phase_mix_hyena(B, M):
    nc, S = B.nc, B.S
    T, L, nseq, ci = M.T, M.L, M.nseq, M.ci
    nblk, nTL = T // 128, L // 128

    with contextlib.ExitStack() as es2:
        M.wcols = 256
        M.first_cols = (0, 256)
        mix_setup(B, M, es2)

        def T2(name, shape, dt=F32):
            return es2.enter_context(B.st(name + M.name, shape, dt))
        hx1f = T2('hx1t', [128, 1024])
        hx1t = hx1f[:, 0:nseq * L].rearrange("p (s l) -> p s l", s=nseq)
        hx0 = T2('hx0', [128, 4, nseq, L], BF16)
        z = T2('z', [128, 4, nseq, L])
        zT = T2('zT', [128, nblk, 512], BF16)
        Pr = T2('Pr', [128, nseq, nTL, 512], BF16)
        Q = T2('Q', [128, nseq, nTL, 512], BF16)
        tmpa = hx1f[:, 0:512]
        tmps = [[T2('tmp0_%d' % j, [128, 512], BF16)[:] for j in range(4)],
                [hx1f[:, j * 256:(j + 1) * 256].bitcast(BF16) for j in range(4)]]
        for grp in range(3):
            for q in range(4):
                f = grp * 4 + q
                if q % 2 == 0:
                    wb, wk_ = load_win_group(B, M, None, col0=f * 128, ncol=256)
                wc = (q % 2) * 128
                if grp < 2:
                    dst = (z, hx0)[grp]
                    p_chunk(B, M, f, wb, wk_, dst[:, q, :, :], (('z', 'hx0')[grp], q), wcol=wc)
                else:
                    p_chunk(B, M, f, wb, wk_, hx1t, 'hx1t', wcol=wc)
                    S.op('dve', lambda e, q=q: e.tensor_tensor(out=z[:, q], in0=z[:, q], in1=hx1t, op=ALU.mult),
                         reads=[('z', q), 'hx1t'], writes=[('z', q)])
        B.dump('Phy' + M.name, hx0[:], [128, 4, nseq, L], [('hx0', q) for q in range(4)], BF16)
        zf = z[:].rearrange("p q s l -> p q (s l)")
        for tb in range(nblk):
            ps, pk = B.psum()
            for q in range(4):
                S.op('pe', lambda e, q=q, ps=ps: e.transpose(ps[:, q * 128:(q + 1) * 128], zf[:, q, tb * 128:(tb + 1) * 128], B.ident[:]),
                     reads=[('z', q), 'ident'], writes=[pk])
            S.op('act', lambda e, ps=ps: e.copy(out=zT[:, tb, :], in_=ps[:, :]), reads=[pk], writes=[('zT', tb)])
        for q in range(4):
            S.op('dve', lambda e, q=q: e.tensor_scalar(out=z[:, q], in0=z[:, q], scalar1=B.hbias[:, q:q + 1], scalar2=None, op0=ALU.mult),
                 reads=[('z', q), 'hbias'], writes=[('z', q)])
        Cm, Sm, Sg, Hr, Hi = B.Cm[L], B.Sm[L], B.Sg[L], B.Hr[L], B.Hi[L]
        for s in range(nseq):
            for kc in range(nTL):
                psr, pkr = B.psum()
                pss, pks = B.psum()
                for (mat, mk, ps, pk) in ((Cm, ('Cm', L), psr, pkr), (Sm, ('Sm', L), pss, pks)):
                    for tb in range(nTL):
                        S.op('pe', lambda e, mat=mat, ps=ps, tb=tb: e.matmul(ps[:, :], mat[:, tb, kc * 128:(kc + 1) * 128],
                                                                             zT[:, s * nTL + tb, :], start=(tb == 0), stop=(tb == nTL - 1)),
                             reads=[mk, ('zT', s * nTL + tb)], writes=[pk])
                ta, tb_, ka, kb = tmps[kc % 2][0], tmps[kc % 2][1], ('tmpa', kc % 2), ('tmpb', kc % 2)
                S.op('dve', lambda e, ta=ta: e.tensor_tensor(out=ta, in0=psr[:, :], in1=Hr[:, kc, :], op=ALU.mult), reads=[pkr, ('Hr', L)], writes=[ka])
                S.op('dve', lambda e, tb_=tb_: e.tensor_tensor(out=tb_, in0=pss[:, :], in1=Hi[:, kc, :], op=ALU.mult), reads=[pks, ('Hi', L)], writes=[kb])
                S.op('pool', lambda e, ta=ta, tb_=tb_: e.tensor_tensor(out=Pr[:, s, kc, :], in0=ta, in1=tb_, op=ALU.add), reads=[ka, kb], writes=[('Pr', s, kc)])
                tc, td, kc_, kd_ = tmps[kc % 2][2], tmps[kc % 2][3], ('tmpc', kc % 2), ('tmpd', kc % 2)
                S.op('dve', lambda e, tc=tc: e.tensor_tensor(out=tc, in0=pss[:, :], in1=Hr[:, kc, :], op=ALU.mult), reads=[pks, ('Hr', L)], writes=[kc_])
                S.op('dve', lambda e, td=td: e.tensor_tensor(out=td, in0=psr[:, :], in1=Hi[:, kc, :], op=ALU.mult), reads=[pkr, ('Hi', L)], writes=[kd_])
                S.op('pool', lambda e, tc=tc, td=td: e.tensor_tensor(out=Q[:, s, kc, :], in0=tc, in1=td, op=ALU.subtract), reads=[kc_, kd_], writes=[('Q', s, kc)])
                if kc == 0:
                    S.op('dve', lambda e: e.tensor_tensor(out=Pr[0:1, s, 0, :], in0=psr[0:1, :], in1=Hr[0:1, 0, :], op=ALU.mult),
                         reads=[pkr, ('Hr', L), ('Pr', s, 0)], writes=[('Pr', s, 0)])
                    S.op('dve', lambda e: e.tensor_tensor(out=Q[0:1, s, 0, :], in0=pss[0:1, :], in1=Hi[0:1, 0, :], op=ALU.mult),
                         reads=[pks, ('Hi', L), ('Q', s, 0)], writes=[('Q', s, 0)])
            tw = min(L, 512)
            for q in range(4):
                for tt in range(L // tw):
                    ps, pk = B.psum()
                    n = 0
                    for (src, sk, mat, mk) in ((Pr, 'Pr', Cm, ('Cm', L)), (Q, 'Q', Sg, ('Sg', L))):
                        for kc in range(nTL):
                            S.op('pe', lambda e, src=src, mat=mat, kc=kc, ps=ps, n=n: e.matmul(
                                ps[:, 0:tw], src[:, s, kc, q * 128:(q + 1) * 128], mat[:, kc, tt * tw:(tt + 1) * tw],
                                start=(n == 0), stop=(n == 2 * nTL - 1)), reads=[(sk, s, kc), mk], writes=[pk])
                            n += 1
                    sl = slice(tt * tw, (tt + 1) * tw)
                    S.op('dve', lambda e, ps=ps, sl=sl: e.scalar_tensor_tensor(out=tmpa[:, 0:tw], in0=ps[:, 0:tw], scalar=B.rn[L][:, q:q + 1],
                                                                               in1=z[:, q, s, sl], op0=ALU.mult, op1=ALU.add),
                         reads=[pk, ('rn', L), ('z', q)], writes=['tmpa'])
                    S.op('dve', lambda e, sl=sl: e.tensor_tensor(out=M.yT[:, q, s * L + tt * tw:s * L + (tt + 1) * tw], in0=tmpa[:, 0:tw],
                                                                 in1=hx0[:, q, s, sl], op=ALU.mult),
                         reads=['tmpa', ('hx0', q)], writes=[('yT', q)])
        B.dump('yh' + M.name, M.yT[:, 0:4, :], [128, 4, T], [('yT', q) for q in range(4)], BF16)
        S.barrier()


RWT = BF16
SUB = 32


def rwkv_setup(B, es):
    nc, S = B.nc, B.S
    sb = lambda name, shape, dt=F32: es.enter_context(B.st(name, shape, dt))
    B.lora = sb('lora_s', [128, 2, 512], F32R)
    def P(name, shape, dname=None, dt=F32):
        t = sb(name + '_s', shape, dt)
        S.dma('sp', t[:], B.inp(dname or name, shape, dt), writes=[name])
        return t
    B.w0c = P('w0c', [128, 2, 4]); B.a0c = P('a0c', [128, 2, 4])
    B.kkc = P('kkc', [128, 4]); B.kac = P('kac', [128, 4]); B.rkc = P('rkc', [128, 4])
    B.lnxg = P('lnxg', [128, 4]); B.lnxb = P('lnxb', [128, 4])
    B.glora = P('g_lora', [128, 512])
    wl = B.inp('w_lora', [2, 64, 512]); al = B.inp('a_lora', [2, 64, 512])
    with contextlib.ExitStack() as es_l:
        lora32 = es_l.enter_context(B.st('lora32', [128, 2, 512]))
        for d in range(2):
            S.dma('sp', lora32[0:64, d, :], wl[d], writes=['lora32'])
            S.dma('sp', lora32[64:128, d, :], al[d], writes=['lora32'])
        S.op('dve', lambda e: e.tensor_copy(out=B.lora[:], in_=lora32[:]), reads=['lora32'], writes=['lora'])
        S.barrier()
    B.omka = sb('omka_s', [128, 4])
    S.op('dve', lambda e: e.tensor_scalar(out=B.omka[:], in0=B.kac[:], scalar1=-1.0, scalar2=1.0, op0=ALU.mult, op1=ALU.add),
         reads=['kac'], writes=['omka'])
    B.scanmask = P('scanmask', [128, 1024])
    B.maskT4 = sb('maskT4_s', [128, 512])
    mst = B.inp('mstrictT', [128, 128]); mit = B.inp('minclT', [128, 128])
    for j, m in enumerate((mst, mit, mst, mit)):
        S.dma('sp', B.maskT4[:, j * 128:(j + 1) * 128], m, writes=['maskT4'])
    B.mstrict = P('mstrict', [128, 128])
    B.rowmask = P('rowmask', [128, 4])
    B.ident4 = P('ident4', [128, 4, 128])
    B.rowmask16 = sb('rowmask16_s', [128, 4], BF16)
    S.op('dve', lambda e: e.tensor_copy(out=B.rowmask16[:], in_=B.rowmask[:]), reads=['rowmask'], writes=['rowmask16'])
    B.blk1 = sb('blk1_s', [128, 128])
    S.op('pool', lambda e: e.memset(B.blk1[:], 0.0), writes=['blk1'])
    S.op('pool', lambda e: e.memset(B.blk1[0:64, 0:64], 1.0), writes=['blk1'])
    S.op('pool', lambda e: e.memset(B.blk1[64:128, 64:128], 1.0), writes=['blk1'])
    B.blk1r = sb('blk1r_s', [128, 128], F32R)
    S.op('pool', lambda e: e.memset(B.blk1r[:].bitcast(F32), 0.0), writes=['blk1r'])
    S.op('pool', lambda e: e.memset(B.blk1r[0:64, 0:64].bitcast(F32), 1.0), writes=['blk1r'])
    S.op('pool', lambda e: e.memset(B.blk1r[64:128, 64:128].bitcast(F32), 1.0), writes=['blk1r'])


def run_interleaved(gens):
    gens = [g for g in gens]
    done = set()
    while len(done) < len(gens):
        for i, g_ in enumerate(gens):
            if i in done:
                continue
            try:
                next(g_)
            except StopIteration:
                done.add(i)


def phase_rwkv(B, M):
    nc, S = B.nc, B.S
    M.combo = 0
    M.pending = None
    T, L, nseq, ci = M.T, M.L, M.nseq, M.ci
    SEG = 256
    nseg = L // SEG
    nsub = SEG // SUB
    with contextlib.ExitStack() as es:
        M.wcols = 256
        M.npraw = 1
        M.first_cols = (3072, 256)
        mix_setup(B, M, es)
        M.npraw = 2

        def TT(name, shape, dt=F32):
            return es.enter_context(B.st(name + M.name, shape, dt))
        NSL = [128, nseq, L]
        wdad = TT('wdad', NSL); gsig = TT('gsig', NSL)
        wb, wk_ = load_win_group(B, M, 6)
        p_chunk(B, M, 24, wb, wk_, wdad[:], 'wdad', wcol=0)
        p_chunk(B, M, 25, wb, wk_, gsig[:], 'gsig', wcol=128)
        S.op('act', lambda e: e.activation(out=wdad[0:64], in_=wdad[0:64], func=AF.Tanh), reads=['wdad'], writes=['wdad'])
        S.op('act', lambda e: e.activation(out=gsig[:], in_=gsig[:], func=AF.Sigmoid), reads=['gsig'], writes=['gsig'])
        wdf = wdad[:].rearrange("p s l -> p (s l)")
        wdr = TT('wdr', NSL, F32R)
        S.op('dve', lambda e: e.tensor_copy(out=wdr[:], in_=wdad[:]), reads=['wdad'], writes=['wdr'])
        wdrf = wdr[:].rearrange("p s l -> p (s l)")
        gsf = gsig[:].rearrange("p s l -> p (s l)")
        rr = TT('rr', NSL, BF16); kx = TT('kx', NSL, BF16); vv = TT('vv', NSL, BF16); kk = TT('kk', NSL)
        tB = TT('tB', NSL)
        ysum = TT('ysum', NSL); ybon = TT('ybon', NSL)
        aS = TT('aS', NSL); lw = TT('lw', NSL); kd = TT('kd', NSL, BF16); bb = TT('bb', NSL, BF16)
        tA = aS
        sq = TT('sq', NSL, F32R)
        cw2 = [TT('cw%d' % i, [128, SEG]) for i in range(2)]; eW3 = [TT('eW%d' % i, [128, SEG]) for i in range(3)]; eWi2 = [TT('eWi%d' % i, [128, SEG]) for i in range(2)]
        at2 = [TT('at%d' % i, [128, SEG], RWT) for i in range(2)]; rt2 = [TT('rt%d' % i, [128, SEG], RWT) for i in range(2)]
        bt2 = [TT('bt%d' % i, [128, SEG], RWT) for i in range(2)]; kt2 = [TT('kt%d' % i, [128, SEG], RWT) for i in range(2)]
        vp2 = [TT('vp%d' % i, [128, SEG], RWT) for i in range(2)]
        Gall2 = [TT('Gall%d' % i, [128, nsub, 128], F32R) for i in range(2)]; SlW2 = [TT('SlW%d' % i, [128, nsub, 64]) for i in range(2)]
        Sall2 = [TT('Sall%d' % i, [128, nsub + 1, 64], F32R) for i in range(2)]
        Sbd2 = [TT('Sbd%d' % i, [128, nsub, 128], RWT) for i in range(2)]
        RhT2 = [TT('RhT%d' % i, [128, SEG], RWT) for i in range(2)]; YlT2 = [TT('YlT%d' % i, [128, SEG]) for i in range(2)]
        ytmp2 = [TT('ytmpc%d' % i, [128, SEG]) for i in range(2)]
        NU = 4
        NB = NU // 2
        AM = [TT('AM%d' % i, [128, 512], RWT) for i in range(NU)]; TM = [TT('TM%d' % i, [128, 320], RWT) for i in range(NU)]
        TMP = [TT('TMP%d' % i, [128, 384], RWT) for i in range(NU)]; XP = [TT('XP%d' % i, [128, 256], RWT) for i in range(NU)]
        BX = [TT('BX%d' % i, [128, 256], RWT) for i in range(NU)]; UX = [TT('UX%d' % i, [128, 256], RWT) for i in range(NU)]
        VX = [TT('VX%d' % i, [128, 256], RWT) for i in range(NU)]
        WW = [[TT('WW%d_%d' % (u_, i), [128, 384], RWT) for i in range(2)] for u_ in range(NU)]
        for t_, k_ in [(Gall2[i], ('Gall', i)) for i in range(2)] + [(Sbd2[i], ('Sbd', i)) for i in range(2)] + [(TMP[i], ('TMP', i)) for i in range(NU)] + [(XP[i], ('XP', i)) for i in range(NU)]:
            S.op('pool', lambda e, t_=t_: e.memset(t_[:].bitcast(F32) if t_.dtype == F32R else t_[:], 0.0), writes=[k_])
        ytmp = TT('ytmp', [128, 512])
        flat = lambda t: t[:].rearrange("p s l -> p (s l)")

        for hp in range(4):
            for (f, dst, key) in ((12 + hp, rr, 'rr'), (16 + hp, kx, 'kx'), (20 + hp, vv, 'vv')):
                wb, wk_ = load_win_group(B, M, None, col0=f * 128, ncol=128)
                p_chunk(B, M, f, wb, wk_, dst[:], key, wcol=0)
            if hp == 0:
                B.dump('rr' + M.name, rr[:], NSL, 'rr', BF16)
            S.op('act', lambda e: e.activation(out=kk[:], in_=kx[:], func=AF.Identity, scale=B.kkc[:, hp:hp + 1]),
                 reads=['kx', 'kkc'], writes=['kk'])
            S.op('act', lambda e: e.activation(out=sq[:], in_=kk[:], func=AF.Square), reads=['kk'], writes=['sq'])
            for tt in range(T // 512):
                sl = slice(tt * 512, (tt + 1) * 512)
                ps, pk = B.psum()
                S.op('pe', lambda e, ps=ps, sl=sl: e.matmul(ps[:, :], B.blk1r[:], flat(sq)[:, sl], start=True, stop=True),
                     reads=['blk1r', 'sq'], writes=[pk])
                S.op('act', lambda e, ps=ps, sl=sl: e.activation(out=flat(tB)[:, sl], in_=ps[:, :], func=AF.Sqrt), reads=[pk], writes=['tB'])
            S.op('dve', lambda e: e.tensor_scalar_max(out=tB[:], in0=tB[:], scalar1=1e-12), reads=['tB'], writes=['tB'])
            S.op('dve', lambda e: e.reciprocal(out=tB[:], in_=tB[:]), reads=['tB'], writes=['tB'])
            S.op('dve', lambda e: e.tensor_tensor(out=kk[:], in0=kk[:], in1=tB[:], op=ALU.mult), reads=['kk', 'tB'], writes=['kk'])
            chk(B, 1)
            S.op('pool', lambda e: e.memset(ybon[:], 0.0), writes=['ybon'])
            S.op('pool', lambda e: e.memset(ysum[:], 0.0), writes=['ysum'])
            def dprep_gen(d):
                for tt in range(T // 512):
                    sl = slice(tt * 512, (tt + 1) * 512)
                    ps, pk = B.psum()
                    S.op('pe', lambda e, ps=ps, sl=sl: e.matmul(ps[:, :], B.lora[0:64, d, hp * 128:(hp + 1) * 128], wdrf[0:64, sl],
                                                                start=True, stop=True), reads=['lora', 'wdr'], writes=[pk])
                    yield
                    S.op('act', lambda e, ps=ps, sl=sl: e.activation(out=flat(lw)[:, sl], in_=ps[:, :], func=AF.Sigmoid,
                                                                      bias=B.w0c[:, d, hp:hp + 1]), reads=[pk, 'w0c'], writes=['lw'])
                    ps2, pk2 = B.psum()
                    S.op('pe', lambda e, ps2=ps2, sl=sl: e.matmul(ps2[:, :], B.lora[64:128, d, hp * 128:(hp + 1) * 128], wdrf[64:128, sl],
                                                                  start=True, stop=True), reads=['lora', 'wdr'], writes=[pk2])
                    yield
                    S.op('act', lambda e, ps2=ps2, sl=sl: e.activation(out=flat(aS)[:, sl], in_=ps2[:, :], func=AF.Sigmoid,
                                                                        bias=B.a0c[:, d, hp:hp + 1]), reads=[pk2, 'a0c'], writes=['aS'])
                S.op('act', lambda e: e.activation(out=lw[:], in_=lw[:], func=AF.Copy, scale=-math.exp(-0.5)),
                     reads=['lw'], writes=['lw'])
                yield
                S.op('dve', lambda e: e.tensor_scalar(out=kd[:], in0=aS[:], scalar1=B.kac[:, hp:hp + 1], scalar2=B.omka[:, hp:hp + 1],
                                                      op0=ALU.mult, op1=ALU.add), reads=['aS', 'kac', 'omka'], writes=['kd'])
                yield
                S.op('dve', lambda e: e.tensor_tensor(out=kd[:], in0=kd[:], in1=kx[:], op=ALU.mult), reads=['kd', 'kx'], writes=['kd'])
                yield
                S.op('dve', lambda e: e.tensor_tensor(out=bb[:], in0=kk[:], in1=aS[:], op=ALU.mult), reads=['kk', 'aS'], writes=['bb'])
                yield
                S.op('dve', lambda e: e.scalar_tensor_tensor(out=sq[:], in0=rr[:], scalar=B.rkc[:, hp:hp + 1], in1=kd[:],
                                                             op0=ALU.mult, op1=ALU.mult), reads=['rr', 'rkc', 'kd'], writes=['sq'])
                yield
                for tt in range(T // 512):
                    sl = slice(tt * 512, (tt + 1) * 512)
                    ps, pk = B.psum()
                    S.op('pe', lambda e, ps=ps, sl=sl: e.matmul(ps[:, :], B.blk1r[:], flat(sq)[:, sl], start=True, stop=True),
                         reads=['blk1r', 'sq'], writes=[pk])
                    yield
                    S.op('dve', lambda e, ps=ps, sl=sl: e.tensor_tensor(out=flat(tB)[:, sl], in0=ps[:, :], in1=flat(vv)[:, sl], op=ALU.mult),
                         reads=[pk, 'vv'], writes=['tB'])
                    yield
                S.op('dve', lambda e: e.tensor_tensor(out=ybon[:], in0=ybon[:], in1=tB[:], op=ALU.add), reads=['ybon', 'tB'], writes=['ybon'])

            dpre = False
            ppre = False
            for d in range(2):
                if not dpre:
                    run_interleaved([dprep_gen(d)])
                dpre = False
                chk(B, 2)
                combos = [(s_, g_) for s_ in range(nseq) for g_ in range(nseg)]
                cpar = [M.combo + i for i in range(len(combos))]
                M.combo += len(combos)

                def prep_gen(s, seg, cidx, d=d):
                    par = cidx % 2
                    cw, eW, eWi = cw2[par], eW3[cidx % 3], eWi2[par]
                    at_, rt_, bt_, kt_, vp = at2[par], rt2[par], bt2[par], kt2[par], vp2[par]
                    kE = ('eW', cidx % 3)
                    kc, kei, ka, kr, kb, kk_, kv = ('cw', par), ('eWi', par), ('at', par), ('rt', par), ('bt', par), ('kt', par), ('vp', par)

                    def pv(t):
                        v_ = t[:, s, ::-1] if d == 1 else t[:, s, :]
                        return v_[:, seg * SEG:(seg + 1) * SEG]
                    S.op('dve', lambda e: e.tensor_tensor_scan(out=cw[:], data0=B.scanmask[:, 0:SEG], data1=pv(lw), initial=0.0,
                                                               op0=ALU.mult, op1=ALU.add), reads=['scanmask', 'lw'], writes=[kc])
                    yield
                    S.op('act', lambda e: e.activation(out=eW[:], in_=cw[:], func=AF.Exp), reads=[kc], writes=[kE])
                    S.op('act', lambda e: e.activation(out=eWi[:], in_=cw[:], func=AF.Exp, scale=-1.0), reads=[kc], writes=[kei])
                    yield
                    S.op('pool', lambda e: e.tensor_tensor(out=rt_[:], in0=pv(rr), in1=eW[:], op=ALU.mult), reads=['rr', kE], writes=[kr])
                    S.op('pool', lambda e: e.tensor_tensor(out=bt_[:], in0=pv(bb), in1=eWi[:], op=ALU.mult), reads=['bb', kei], writes=[kb])
                    yield
                    S.op('pool', lambda e: e.tensor_tensor(out=kt_[:], in0=pv(kd), in1=eWi[:], op=ALU.mult), reads=['kd', kei], writes=[kk_])
                    S.op('pool', lambda e: e.tensor_copy(out=vp[:], in_=pv(vv)), reads=['vv'], writes=[kv])
                    yield
                    S.op('dve', lambda e: e.tensor_tensor(out=cw[:], in0=cw[:], in1=pv(lw), op=ALU.subtract), reads=[kc, 'lw'], writes=[kc])
                    yield
                    S.op('act', lambda e: e.activation(out=cw[:], in_=cw[:], func=AF.Exp), reads=[kc], writes=[kc])
                    yield
                    S.op('dve', lambda e: e.scalar_tensor_tensor(out=at_[:], in0=pv(kk), scalar=-1.0, in1=cw[:], op0=ALU.mult, op1=ALU.mult),
                         reads=['kk', kc], writes=[ka])

                if not ppre:
                    run_interleaved([prep_gen(combos[0][0], combos[0][1], cpar[0])])
                ppre = False
                for ci_, (s, seg) in enumerate(combos):
                    cidx = cpar[ci_]
                    par = cidx % 2
                    nxt = [prep_gen(combos[ci_ + 1][0], combos[ci_ + 1][1], cpar[ci_ + 1])] if ci_ + 1 < len(combos) else []
                    if ci_ + 1 == len(combos) and d == 0:
                        def seq_gen(*gs):
                            for g_ in gs:
                                yield from g_
                        nxt = nxt + [seq_gen(dprep_gen(1), prep_gen(combos[0][0], combos[0][1], M.combo, d=1))]
                        dpre = True
                        ppre = True
                    eW, Gall, SlW, Sall, Sbd, RhT, YlT = eW3[cidx % 3], Gall2[par], SlW2[par], Sall2[par], Sbd2[par], RhT2[par], YlT2[par]
                    at_, rt_, bt_, kt_, vp = at2[par], rt2[par], bt2[par], kt2[par], vp2[par]
                    kE, kG, kSl, kS, kSb, kR, kY = ('eW', cidx % 3), ('Gall', par), ('SlW', par), ('Sall', par), ('Sbd', par), ('RhT', par), ('YlT', par)
                    kA_, kR_, kB_, kK_, kV_ = ('at', par), ('rt', par), ('bt', par), ('kt', par), ('vp', par)
                    chk(B, 3)
                    def unit(blk, h2):
                        u = (blk % NB) * 2 + h2
                        bs = slice(blk * 128, (blk + 1) * 128)
                        hs = slice(64 * h2, 64 * h2 + 64)
                        AMh, TMh, akey, tkey = AM[u], TM[u], ('AM', u), ('TM', u)
                        Wu = WW[u]
                        psA, pkA = B.psum()
                        for j, (lh, lk, rh, rk) in enumerate(((bt_, kB_, at_, kA_), (bt_, kB_, rt_, kR_),
                                                              (kt_, kK_, at_, kA_), (kt_, kK_, rt_, kR_))):
                            S.op('pe', lambda e, j=j, lh=lh, rh=rh: e.matmul(psA[:, j * 128:(j + 1) * 128], lh[hs, bs], rh[hs, bs],
                                                                             start=True, stop=True), reads=[lk, rk], writes=[pkA])
                        yield
                        S.op('dve', lambda e: e.tensor_tensor(out=AMh[:], in0=psA[:, :], in1=B.maskT4[:], op=ALU.mult),
                             reads=[pkA, 'maskT4'], writes=[akey])
                        psB, pkB = B.psum()
                        S.op('pe', lambda e: e.matmul(psB[:, 0:128], at_[hs, bs], bt_[hs, bs], start=True, stop=True),
                             reads=[kA_, kB_], writes=[pkB])
                        psTb = psB[:, 256:384].bitcast(BF16)
                        for j, (src, sk) in enumerate(((bt_, kB_), (kt_, kK_), (vp, kV_), (at_, kA_))):
                            S.op('pe', lambda e, j=j, src=src: e.transpose(psTb[:, j * 64:(j + 1) * 64], src[hs, bs], B.identb[hs, hs]),
                                 reads=[sk, 'identb'], writes=[pkB])
                        yield
                        S.op('dve', lambda e: e.tensor_tensor(out=Wu[0][:, 128:256], in0=psB[:, 0:128], in1=B.mstrict[:], op=ALU.mult),
                             reads=[pkB, 'mstrict'], writes=[('WW', u, 0)])
                        S.op('act', lambda e: e.copy(out=TMh[:, 0:256], in_=psTb[:, 0:256]), reads=[pkB], writes=[tkey])
                        S.op('act', lambda e: e.copy(out=TMP[u][:].rearrange("p (j k) -> p j k", j=3)[:, :, 64 * h2:64 * h2 + 64],
                                                     in_=psTb[:, 0:192].rearrange("p (j k) -> p j k", j=3)), reads=[pkB], writes=[('TMP', u)])
                        rm3p = B.rowmask16[:].unsqueeze(2).to_broadcast([128, 4, 64])
                        for (dst, dk, src) in ((BX[u], ('BX', u), TMh[:, 0:64]), (VX[u], ('VX', u), TMh[:, 128:192])):
                            S.op('pool', lambda e, dst=dst, src=src: e.tensor_tensor(out=dst[:].rearrange("p (s k) -> p s k", s=4),
                                                                                     in0=src.unsqueeze(1).to_broadcast([128, 4, 64]), in1=rm3p, op=ALU.mult),
                                 reads=[tkey, 'rowmask16'], writes=[dk])
                        psZ, pkZ = B.psum()
                        S.op('pe', lambda e: e.matmul(psZ[:, 0:64], AMh[:, 256:384], TMh[:, 128:192], start=True, stop=True),
                             reads=[akey, tkey], writes=[pkZ])
                        yield
                        S.op('act', lambda e: e.copy(out=TMh[:, 256:320], in_=psZ[:, 0:64]), reads=[pkZ], writes=[tkey])
                        for j in range(5):
                            c, n = j % 2, (j + 1) % 2
                            psX, pkX = B.psum()
                            if j == 0:
                                S.op('pe', lambda e: e.matmul(psX[:, 0:128], AMh[:, 0:128], TMh[:, 192:320], start=True, stop=True),
                                     reads=[akey, tkey], writes=[pkX])
                                S.op('pe', lambda e: e.matmul(psX[:, 128:256], AMh[:, 0:128], Wu[0][:, 128:256], start=True, stop=True),
                                     reads=[akey, ('WW', u, 0)], writes=[pkX])
                                S.op('pe', lambda e: e.matmul(psX[:, 256:384], Wu[0][:, 128:256], AMh[:, 0:128], start=True, stop=True),
                                     reads=[akey, ('WW', u, 0)], writes=[pkX])
                                xprev, xpk = TMh[:, 192:320], tkey
                            else:
                                wk_c = ('WW', u, c)
                                if j < 4:
                                    S.op('pe', lambda e, c=c: e.matmul(psX[:, 0:256], Wu[c][:, 256:384], Wu[c][:, 0:256], start=True, stop=True),
                                         reads=[wk_c], writes=[pkX])
                                    S.op('pe', lambda e, c=c: e.matmul(psX[:, 256:384], Wu[c][:, 128:256], Wu[c][:, 256:384], start=True, stop=True),
                                         reads=[wk_c], writes=[pkX])
                                else:
                                    S.op('pe', lambda e, c=c: e.matmul(psX[:, 0:128], Wu[c][:, 256:384], Wu[c][:, 0:128], start=True, stop=True),
                                         reads=[wk_c], writes=[pkX])
                                xprev, xpk = Wu[c][:, 0:128], wk_c
                            yield
                            if j < 4:
                                S.op('dve', lambda e, n=n, xprev=xprev: e.tensor_tensor(out=Wu[n][:, 0:128], in0=psX[:, 0:128], in1=xprev, op=ALU.add),
                                     reads=[pkX, xpk], writes=[('WW', u, n)])
                                S.op('act', lambda e, n=n: e.copy(out=Wu[n][:, 128:384], in_=psX[:, 128:384]), reads=[pkX], writes=[('WW', u, n)])
                            else:
                                S.op('dve', lambda e, xprev=xprev: e.tensor_tensor(
                                    out=XP[u][:].rearrange("p (j k) -> p j k", j=2)[:, :, 64 * h2:64 * h2 + 64],
                                    in0=psX[:, 0:128].rearrange("p (j k) -> p j k", j=2),
                                    in1=xprev.rearrange("p (j k) -> p j k", j=2), op=ALU.add),
                                    reads=[pkX, xpk], writes=[('XP', u)])
                        yield
                        rm3 = B.rowmask[:].unsqueeze(2).to_broadcast([128, 4, 64])
                        for (dst, dk, src, sk) in ((UX[u], ('UX', u), XP[u][:, 128 + 64 * h2:128 + 64 * h2 + 64], ('XP', u)),):
                            S.op('dve', lambda e, dst=dst, src=src: e.tensor_tensor(out=dst[:].rearrange("p (s k) -> p s k", s=4),
                                                                                    in0=src.unsqueeze(1).to_broadcast([128, 4, 64]), in1=rm3, op=ALU.mult),
                                 reads=[sk, 'rowmask'], writes=[dk])

                    def pair(blk):
                        u0 = (blk % NB) * 2
                        bs = slice(blk * 128, (blk + 1) * 128)
                        psR, pkR = B.psum()
                        for h2 in range(2):
                            u = u0 + h2
                            S.op('pe', lambda e, h2=h2, u=u: e.matmul(psR[:, 0:128], XP[u][:, 0:128], AM[u][:, 128:256], start=(h2 == 0), stop=(h2 == 1)),
                                 reads=[('XP', u), ('AM', u)], writes=[pkR])
                        for h2 in range(2):
                            u = u0 + h2
                            S.op('pe', lambda e, h2=h2, u=u: e.matmul(psR[:, 128:256], XP[u][:, 128:256], AM[u][:, 128:256], start=(h2 == 0), stop=False),
                                 reads=[('XP', u), ('AM', u)], writes=[pkR])
                            S.op('pe', lambda e, h2=h2, u=u: e.matmul(psR[:, 128:256], TMP[u][:, 256:384], AM[u][:, 384:512], start=False, stop=(h2 == 1)),
                                 reads=[('TMP', u), ('AM', u)], writes=[pkR])
                        psG, pkG = B.psum()
                        for h2 in range(2):
                            u = u0 + h2
                            S.op('pe', lambda e, h2=h2, u=u: e.matmul(psG[:, 0:256], XP[u][:, 0:128], BX[u][:], start=(h2 == 0), stop=(h2 == 1)),
                                 reads=[('XP', u), ('BX', u)], writes=[pkG])
                        for h2 in range(2):
                            u = u0 + h2
                            S.op('pe', lambda e, h2=h2, u=u: e.matmul(psG[:, 256:512], TMP[u][:, 0:128], UX[u][:], start=(h2 == 0), stop=False),
                                 reads=[('TMP', u), ('UX', u)], writes=[pkG])
                            S.op('pe', lambda e, h2=h2, u=u: e.matmul(psG[:, 256:512], TMP[u][:, 128:256], VX[u][:], start=False, stop=(h2 == 1)),
                                 reads=[('TMP', u), ('VX', u)], writes=[pkG])
                        yield
                        S.op('dve', lambda e: e.tensor_tensor(out=RhT[:, bs], in0=psR[:, 0:128], in1=rt_[:, bs], op=ALU.add),
                             reads=[pkR, kR_], writes=[kR])
                        S.op('dve', lambda e: e.tensor_copy(out=YlT[:, bs], in_=psR[:, 128:256]), reads=[pkR], writes=[kY])
                        n0 = blk * 4
                        for h2 in range(2):
                            hs = slice(64 * h2, 64 * h2 + 64)
                            S.op('dve', lambda e, hs=hs: e.tensor_tensor(out=Gall[hs, n0:n0 + 4, hs],
                                                                         in0=psG[hs, 0:256].rearrange("p (s k) -> p s k", s=4),
                                                                         in1=B.ident4[hs, :, hs], op=ALU.add),
                                 reads=[pkG, 'ident4'], writes=[kG])
                        yield
                        for sc in range(4):
                            wc = eW[:, (n0 + sc) * 32 + 31:(n0 + sc) * 32 + 32]
                            S.op('act', lambda e, sc=sc, wc=wc: e.activation(out=SlW[:, n0 + sc, :], in_=psG[:, 256 + sc * 64:256 + (sc + 1) * 64],
                                                                              func=AF.Identity, scale=wc),
                                 reads=[pkG, kE], writes=[kSl])

                    pend = [M.pending] if M.pending is not None else []
                    M.pending = None
                    for b0 in range(0, SEG // 128, NB):
                        run_interleaved([unit(b0 + i, h2) for i in range(NB) for h2 in range(2)] + pend + nxt)
                        run_interleaved([pair(b0 + i) for i in range(NB)] + pend)
                    run_interleaved(pend + nxt)
                    def chain_gen(par=par, s=s, seg=seg, d=d, hp=hp, eW=eW, Gall=Gall, SlW=SlW, Sall=Sall, Sbd=Sbd, RhT=RhT, YlT=YlT,
                                  kE=kE, kG=kG, kSl=kSl, kS=kS, kSb=kSb, kR=kR, kY=kY):
                        ytm, kyt = ytmp2[par], ('ytmpc', par)
                        Sprev, kSp = Sall2[1 - par], ('Sall', 1 - par)
                        if seg > 0:
                            S.op('act', lambda e: e.copy(out=Sall[:, 0, :], in_=Sprev[:, nsub, :].bitcast(F32)), reads=[kSp], writes=[kS])
                        elif M.st0 is not None:
                            S.op('act', lambda e: e.copy(out=Sall[:, 0, :], in_=M.st0[:, d, hp, :]), reads=['st0'], writes=[kS])
                        else:
                            S.op('pool', lambda e: e.memset(Sall[:, 0, :].bitcast(F32), 0.0), writes=[kS])
                        for n in range(nsub):
                            psC, pkC = B.psum()
                            S.op('pe', lambda e, n=n: e.matmul(psC[:, 0:64], Gall[:, n, :], Sall[:, n, :], start=True, stop=True),
                                 reads=[kG, kS], writes=[pkC])
                            yield
                            S.op('dve', lambda e, n=n: e.scalar_tensor_tensor(out=Sall[:, n + 1, :], in0=psC[:, 0:64], scalar=eW[:, n * 32 + 31:n * 32 + 32],
                                                                              in1=SlW[:, n, :], op0=ALU.mult, op1=ALU.add),
                                 reads=[pkC, kE, kSl], writes=[kS])
                            yield
                        for h2 in range(2):
                            hs = slice(64 * h2, 64 * h2 + 64)
                            S.op('act', lambda e, hs=hs: e.copy(out=Sbd[hs, :, hs], in_=Sall[hs, 0:nsub, :].bitcast(F32)), reads=[kS], writes=[kSb])
                        yield
                        tw = SEG
                        psY, pkY = B.psum()
                        for n in range(nsub):
                            c0 = n * 32
                            S.op('pe', lambda e, n=n, c0=c0: e.matmul(psY[:, c0:c0 + 32], Sbd[:, n, :], RhT[:, n * 32:(n + 1) * 32], start=True, stop=True),
                                 reads=[kSb, kR], writes=[pkY])
                        yield
                        S.op('dve', lambda e: e.tensor_tensor(out=ytm[:, 0:tw], in0=psY[:, 0:tw], in1=YlT[:, 0:tw], op=ALU.add),
                             reads=[pkY, kY], writes=[kyt])
                        if d == 0:
                            yv = ysum[:, s, seg * SEG:(seg + 1) * SEG]
                            S.op('pool', lambda e, yv=yv: e.tensor_tensor(out=yv, in0=yv, in1=ytm[:, 0:tw], op=ALU.add), reads=['ysum', kyt], writes=['ysum'])
                        else:
                            lo = L - (seg + 1) * SEG
                            yv = ysum[:, s, lo:lo + tw]
                            S.op('pool', lambda e, yv=yv: e.tensor_tensor(out=yv, in0=yv, in1=ytm[:, tw - 1::-1], op=ALU.add),
                                 reads=['ysum', kyt], writes=['ysum'])
                        if M.ns_out is not None and seg == nseg - 1:
                            psF, pkF = B.psum()
                            S.op('pe', lambda e: e.transpose(psF[0:64, 0:128], Sall[:, nsub, :].bitcast(F32), B.ident[:]), reads=[kS, 'ident'], writes=[pkF])
                            yield
                            S.op('act', lambda e: e.copy(out=M.nst[0:64, :], in_=psF[0:64, 0:128]), reads=[pkF], writes=['nst'])
                            S.dma('sp', M.ns_out[s, d, 2 * hp:2 * hp + 2].rearrange("h v k -> v h k"),
                                  M.nst[0:64, :].rearrange("v (h k) -> v h k", h=2), reads=['nst'])
                    M.pending = chain_gen()
            if M.pending is not None:
                run_interleaved([M.pending])
                M.pending = None
            if hp == 0:
                B.dump('ysum' + M.name, ysum[:], NSL, 'ysum')
            chk(B, 11)
            S.op('act', lambda e: e.activation(out=sq[:], in_=ysum[:], func=AF.Square), reads=['ysum'], writes=['sq'])
            for tt in range(T // 512):
                sl = slice(tt * 512, (tt + 1) * 512)
                ps1, pk1 = B.psum()
                ps2, pk2 = B.psum()
                S.op('pe', lambda e, sl=sl, ps1=ps1: e.matmul(ps1[:, :], B.blk1[:], flat(ysum)[:, sl], start=True, stop=True), reads=['blk1', 'ysum'], writes=[pk1])
                S.op('pe', lambda e, sl=sl, ps2=ps2: e.matmul(ps2[:, :], B.blk1r[:], flat(sq)[:, sl], start=True, stop=True), reads=['blk1r', 'sq'], writes=[pk2])
                S.op('act', lambda e, ps1=ps1: e.activation(out=ytmp[:], in_=ps1[:, :], func=AF.Identity, scale=1.0 / 64), reads=[pk1], writes=['ytmp'])
                S.op('dve', lambda e, sl=sl: e.tensor_tensor(out=flat(tB)[:, sl], in0=ytmp[:], in1=ytmp[:], op=ALU.mult), reads=['ytmp'], writes=['tB'])
                S.op('dve', lambda e, sl=sl, ps2=ps2: e.scalar_tensor_tensor(out=flat(tB)[:, sl], in0=ps2[:, :], scalar=1.0 / 64, in1=flat(tB)[:, sl],
                                                                             op0=ALU.mult, op1=ALU.subtract), reads=[pk2, 'tB'], writes=['tB'])
                S.op('dve', lambda e, sl=sl: e.tensor_scalar_add(out=flat(tB)[:, sl], in0=flat(tB)[:, sl], scalar1=GN_EPS), reads=['tB'], writes=['tB'])
                S.op('act', lambda e, sl=sl: e.activation(out=flat(tB)[:, sl], in_=flat(tB)[:, sl], func=AF.Sqrt), reads=['tB'], writes=['tB'])
                S.op('dve', lambda e, sl=sl: e.reciprocal(out=flat(tB)[:, sl], in_=flat(tB)[:, sl]), reads=['tB'], writes=['tB'])
                S.op('dve', lambda e, sl=sl: e.tensor_tensor(out=flat(ysum)[:, sl], in0=flat(ysum)[:, sl], in1=ytmp[:], op=ALU.subtract),
                     reads=['ysum', 'ytmp'], writes=['ysum'])
                S.op('dve', lambda e, sl=sl: e.tensor_tensor(out=flat(ysum)[:, sl], in0=flat(ysum)[:, sl], in1=flat(tB)[:, sl], op=ALU.mult),
                     reads=['ysum', 'tB'], writes=['ysum'])
                S.op('dve', lambda e, sl=sl: e.tensor_scalar(out=flat(ysum)[:, sl], in0=flat(ysum)[:, sl], scalar1=B.lnxg[:, hp:hp + 1],
                                                             scalar2=B.lnxb[:, hp:hp + 1], op0=ALU.mult, op1=ALU.add),
                     reads=['ysum', 'lnxg', 'lnxb'], writes=['ysum'])
                S.op('dve', lambda e, sl=sl: e.tensor_tensor(out=flat(ysum)[:, sl], in0=flat(ysum)[:, sl], in1=flat(ybon)[:, sl], op=ALU.add),
                     reads=['ysum', 'ybon'], writes=['ysum'])
                psg, pkg = B.psum()
                S.op('pe', lambda e, sl=sl, psg=psg: e.matmul(psg[:, :], B.glora[:, hp * 128:(hp + 1) * 128], gsf[:, sl], start=True, stop=True),
                     reads=['g_lora', 'gsig'], writes=[pkg])
                S.op('dve', lambda e, sl=sl, psg=psg: e.tensor_tensor(out=M.yT[:, 4 + hp, sl], in0=psg[:, :], in1=flat(ysum)[:, sl], op=ALU.mult),
                     reads=[pkg, 'ysum'], writes=[('yT', 4 + hp)])
        B.dump('yr' + M.name, M.yT[:, 4:8, :], [128, 4, T], [('yT', 4 + q) for q in range(4)], BF16)
        S.barrier()


def post_setup(B, es):
    nc, S = B.nc, B.S
    def TT(name, shape, dt=F32):
        return es.enter_context(B.st(name, shape, dt))
    B.fconvT = TT('fconvT_s', [128, NFC, 9]); S.dma('sp', B.fconvT[:], B.inp('fconvT', [128, NFC, 9]), writes=['fconvT'])
    B.fconvb = TT('fconvb_s', [128, NFC]); S.dma('sp', B.fconvb[:], B.inp('fconvb', [128, NFC]), writes=['fconvb'])
    B.w_up_d = B.inp('w_up', [1024, 2 * DFF]).rearrange("(kc p) n -> p kc n", p=128)
    B.w_out_d = B.inp('w_out', [1024, 1024]).rearrange("(kc p) n -> p kc n", p=128)
    B.w_down_d = B.inp('w_down', [DFF, 1024]).rearrange("(fc p) n -> p fc n", p=128)
    B.ln_d = {n: B.inp(n, [1024]) for n in ('ln1_g', 'ln1_b', 'ln2_g', 'ln2_b')}
    B.lnp = {}


def load_ln(B, es, names):
    for n in names:
        t = es.enter_context(B.st(n + '_bc', [128, 1024]))
        B.S.dma('sp', t[:], B.ln_d[n].partition_broadcast(128), writes=[n])
        B.lnp[n] = t


def resid_ln_gen(B, bufs, ps_pair, pks, xres_ap, xres_key, gi, ci, gname, bname, out_ap, out_key, scale_in):
    nc, S = B.nc, B.S
    tmp, tk, lb = bufs
    st, mv, sfx = lb
    ks, km = 'bst' + sfx, 'bmv' + sfx
    for half in range(2):
        hsl = slice(half * 512, (half + 1) * 512)
        S.op('dve', lambda e, half=half, hsl=hsl: e.tensor_tensor(out=tmp[:, hsl], in0=ps_pair[half][:, :], in1=B.gbc[:, ci, gi, hsl], op=ALU.mult),
             reads=[pks[half], ('gbc', ci, gi, half)], writes=[tk])
    S.op('dve', lambda e: e.scalar_tensor_tensor(out=tmp[:], in0=xres_ap, scalar=scale_in, in1=tmp[:], op0=ALU.mult, op1=ALU.add),
         reads=[xres_key, tk], writes=[tk])
    for j in range(2):
        S.op('dve', lambda e, j=j: e.bn_stats(out=st[:, j, :], in_=tmp[:, j * 512:(j + 1) * 512]), reads=[tk], writes=[ks])
    S.op('dve', lambda e: e.bn_aggr(out=mv[:], in_=st[:].rearrange("p a b -> p (a b)")), reads=[ks], writes=[km])
    S.op('dve', lambda e: e.tensor_scalar_add(out=mv[:, 1:2], in0=mv[:, 1:2], scalar1=LN_EPS), reads=[km], writes=[km])
    yield
    S.op('act', lambda e: e.activation(out=mv[:, 1:2], in_=mv[:, 1:2], func=AF.Sqrt), reads=[km], writes=[km])
    yield
    S.op('dve', lambda e: e.reciprocal(out=mv[:, 1:2], in_=mv[:, 1:2]), reads=[km], writes=[km])
    S.op('dve', lambda e: e.tensor_scalar(out=tmp[:], in0=tmp[:], scalar1=mv[:, 0:1], scalar2=mv[:, 1:2], op0=ALU.subtract, op1=ALU.mult),
         reads=[tk, km], writes=[tk])
    S.op('dve', lambda e: e.tensor_tensor(out=tmp[:], in0=tmp[:], in1=B.lnp[gname][:], op=ALU.mult), reads=[tk, gname], writes=[tk])
    S.op('dve', lambda e: e.tensor_tensor(out=out_ap, in0=tmp[:], in1=B.lnp[bname][:], op=ALU.add), reads=[tk, bname], writes=[out_key])


def phase_post(B, M):
    nc, S = B.nc, B.S
    T, L, nseq, ci = M.T, M.L, M.nseq, M.ci
    nblk = T // 128
    with contextlib.ExitStack() as es:
        def TT(name, shape, dt=F32):
            return es.enter_context(B.st(name + M.name, shape, dt))
        x1 = TT('x1', [128, nblk, 1024])
        B.wdown = TT('wdown', [128, NFC, 1024], BF16)
        esA = contextlib.ExitStack()
        B.wout = esA.enter_context(B.st('wout' + M.name, [128, 8, 1024], BF16))
        for hh in range(4):
            S.dma('pool', B.wout[:, 2 * hh:2 * hh + 2, :], B.w_out_d[:, 2 * hh:2 * hh + 2, :], writes=[('wout', hh)])
        for i in range(2):
            S.dma('pool', B.wdown[:, i * 11:(i + 1) * 11, :], B.w_down_d[:, i * 11:(i + 1) * 11, :], writes=[('wdown', i)])
        load_ln(B, esA, ('ln1_g', 'ln1_b'))
        with contextlib.ExitStack() as es2:
            lb = ln_bufs(B, es2, 'rl')
            xts = [es2.enter_context(B.st('pxt', [128, 1024])) for i in range(2)]
            tmps = [es2.enter_context(B.st('rl_tmp', [128, 1024])) for i in range(2)]
            def tile1(tb):
                i = tb % 2
                xt, xk = xts[i], ('pxt', i)
                S.dma('sp', xt[:], M.x_dram[tb * 128:(tb + 1) * 128, :], writes=[xk])
                pss, pks = [], []
                for half in range(2):
                    ps, pk = B.psum()
                    for cc in range(8):
                        S.op('pe', lambda e, cc=cc, ps=ps, half=half: e.matmul(ps[:, :], M.yT[:, cc, tb * 128:(tb + 1) * 128],
                                                                              B.wout[:, cc, half * 512:(half + 1) * 512],
                                                                              start=(cc == 0), stop=(cc == 7)),
                             reads=[('yT', cc), ('wout', cc // 2)], writes=[pk])
                    pss.append(ps); pks.append(pk)
                yield
                yield from resid_ln_gen(B, (tmps[i], ('rl_tmp', i), lb[i]), pss, pks, xt[:], xk, 0, ci, 'ln1_g', 'ln1_b', x1[:, tb, :], ('x1', tb), ALPHA)
            run_window([(lambda tb=tb: tile1(tb)) for tb in range(nblk)], 2)
            S.barrier()
        B.dump('x1' + M.name, x1[:], [128, nblk, 1024], [('x1', tb) for tb in range(nblk)])
        esA.close()
        h2T = M.yT
        load_ln(B, es, ('ln2_g', 'ln2_b'))
        act = TT('act', [128, NFC, T], BF16)
        esU = contextlib.ExitStack()
        TU = lambda name, shape, dt=F32: esU.enter_context(B.st(name + M.name, shape, dt))
        G = 2 if M.grid else 4
        wu = [TU('wu%d' % i, [128, 8, 2 * G * 128], BF16) for i in range(2)]
        S.dma('pool', wu[0][:, :, 0:G * 128], B.w_up_d[:, :, 0:G * 128], writes=[('wu', 0)])
        S.dma('pool', wu[0][:, :, G * 128:2 * G * 128], B.w_up_d[:, :, DFF:DFF + G * 128], writes=[('wu', 0)])
        ln_to_hT(B, None, T, h2T, 'h2T', 32, 24, ci, M.name + 'f', x_sb=x1)
        if M.grid:
            upad = [TU('upad%d' % i, [128, 18, 66]) for i in range(2)]
        else:
            upad = [TU('upad%d' % i, [128, nseq, L + 2]) for i in range(2)]
        for i in range(2):
            S.op('pool', lambda e, i=i: e.memset(upad[i][:], 0.0), writes=[('upad', i)])
        cvs = [TU('cv%d' % i, [128, T]) for i in range(2)]
        for f in range(NFC):
            gi_, j_ = f // G, f % G
            wub, wuk = wu[gi_ % 2], ('wu', gi_ % 2)
            if j_ == 0 and f > 0:
                ng = min(G, NFC - f)
                S.dma('pool', wub[:, :, 0:ng * 128], B.w_up_d[:, :, f * 128:(f + ng) * 128], writes=[wuk])
                S.dma('pool', wub[:, :, G * 128:(G + ng) * 128], B.w_up_d[:, :, DFF + f * 128:DFF + (f + ng) * 128], writes=[wuk])
            uo, go = j_ * 128, (G + j_) * 128
            up, upk = upad[f % 2], ('upad', f % 2)
            cv, cvk = cvs[f % 2], ('cv', f % 2)
            for tt in range(T // 512):
                ps, pk = B.psum()
                for kc in range(8):
                    S.op('pe', lambda e, kc=kc, ps=ps: e.matmul(ps[:, :], wub[:, kc, uo:uo + 128], h2T[:, kc, tt * 512:(tt + 1) * 512],
                                                                start=(kc == 0), stop=(kc == 7)),
                         reads=[wuk] + [('h2T', kc, tb) for tb in range(tt * 4, tt * 4 + 4)], writes=[pk])
                if M.grid:
                    S.op('act', lambda e, ps=ps: e.copy(out=up[:, 1 + tt * 8:1 + tt * 8 + 8, 1:65], in_=ps[:, :].rearrange("p (r c) -> p r c", c=64)),
                         reads=[pk], writes=[upk])
                else:
                    S.op('act', lambda e, ps=ps: e.copy(out=up[:, :, 1:L + 1], in_=ps[:, :].rearrange("p (s l) -> p s l", s=nseq)),
                         reads=[pk], writes=[upk])
            cw_ = B.fconvT
            if M.grid:
                cvv = cv[:].rearrange("p (r c) -> p r c", c=64)
                taps = [(dr, dc) for dr in range(3) for dc in range(3)]
                view = lambda dr, dc: up[:, dr:dr + 16, dc:dc + 64]
                center = 4
            else:
                cvv = cv[:].rearrange("p (s l) -> p s l", s=nseq)
                taps = [(1, dc) for dc in range(3)]
                view = lambda dr, dc: up[:, :, dc:dc + L]
                center = 1
            ctap = taps[center]
            S.op('act', lambda e: e.activation(out=cvv, in_=view(*ctap), func=AF.Identity, scale=cw_[:, f, 4:5], bias=B.fconvb[:, f:f + 1]),
                 reads=[upk, 'fconvT', 'fconvb'], writes=[cvk])
            for (dr, dc) in taps:
                if (dr, dc) == ctap:
                    continue
                wi = dr * 3 + dc
                S.op('dve', lambda e, dr=dr, dc=dc, wi=wi: e.scalar_tensor_tensor(out=cvv, in0=view(dr, dc), scalar=cw_[:, f, wi:wi + 1], in1=cvv,
                                                                                  op0=ALU.mult, op1=ALU.add),
                     reads=[upk, 'fconvT', cvk], writes=[cvk])
            S.op('act', lambda e: e.activation(out=cv[:], in_=cv[:], func=AF.Gelu_apprx_tanh), reads=[cvk], writes=[cvk])
            for tt in range(T // 512):
                ps, pk = B.psum()
                for kc in range(8):
                    S.op('pe', lambda e, kc=kc, ps=ps: e.matmul(ps[:, :], wub[:, kc, go:go + 128], h2T[:, kc, tt * 512:(tt + 1) * 512],
                                                                start=(kc == 0), stop=(kc == 7)),
                         reads=[wuk] + [('h2T', kc, tb) for tb in range(tt * 4, tt * 4 + 4)], writes=[pk])
                S.op('dve', lambda e, ps=ps, tt=tt: e.tensor_tensor(out=act[:, f, tt * 512:(tt + 1) * 512], in0=ps[:, :], in1=cv[:, tt * 512:(tt + 1) * 512],
                                                                    op=ALU.mult), reads=[pk, cvk], writes=[('act', f)])
        S.barrier()
        esU.close()
        with contextlib.ExitStack() as es2:
            lb = ln_bufs(B, es2, 'rl2')
            ots = [es2.enter_context(B.st('pot', [128, 1024])) for i in range(2)]
            tmps = [es2.enter_context(B.st('rl_tmp2', [128, 1024])) for i in range(2)]
            def tile2(tb):
                i = tb % 2
                ot, ok_ = ots[i], ('pot', i)
                pss, pks = [], []
                for half in range(2):
                    ps, pk = B.psum()
                    for f in range(NFC):
                        S.op('pe', lambda e, f=f, ps=ps, half=half: e.matmul(ps[:, :], act[:, f, tb * 128:(tb + 1) * 128],
                                                                            B.wdown[:, f, half * 512:(half + 1) * 512],
                                                                            start=(f == 0), stop=(f == NFC - 1)),
                             reads=[('act', f), ('wdown', f // 11)], writes=[pk])
                    pss.append(ps); pks.append(pk)
                yield
                yield from resid_ln_gen(B, (tmps[i], ('rl_tmp2', i), lb[i]), pss, pks, x1[:, tb, :], ('x1', tb), 1, ci, 'ln2_g', 'ln2_b', ot[:], ok_, ALPHA)
                S.dma('sp', M.out_dram[tb * 128:(tb + 1) * 128, :], ot[:], reads=[ok_])
            run_window([(lambda tb=tb: tile2(tb)) for tb in range(nblk)], 2)
            S.barrier()

def build_full(dbg=()):
    B = Builder(dbg=dbg)
    nc, S = B.nc, B.S
    phase_mod(B)
    Ms = Mix(); Ms.name = 's'; Ms.T = 1024; Ms.L = 1024; Ms.nseq = 1; Ms.ci = 0; Ms.grid = True
    Mp = Mix(); Mp.name = 'p'; Mp.T = 512; Mp.L = 256; Mp.nseq = 2; Mp.ci = 1; Mp.grid = False
    Ms.x_dram = B.inp('xs', [1024, D]); Mp.x_dram = B.inp('xp', [512, D])
    Ms.out_dram = B.outp('ys', [1024, D]); Mp.out_dram = B.outp('yp', [512, D])
    Ms.yT = B.sb('yT_s', [128, 8, 1024], BF16); Mp.yT = B.sb('yT_p', [128, 8, 512], BF16)
    B.w_in_d = B.inp('w_in', [1024, CIN]).rearrange("(kc p) n -> p kc n", p=128)
    B.convT = B.sb('convT_s', [128, 26, 3]); S.dma('sp', B.convT[:], B.inp('convT', [128, 26, 3]), writes=['convT'])
    B.hbias = B.sb('hbias_s', [128, 4]); S.dma('sp', B.hbias[:], B.inp('hbias', [128, 4]), writes=['hbias'])
    alloc_dft(B)
    start_ada_stream(B)
    phase_filters(B, (1024,))
    mod_part(B, 0)
    phase_filters(B, (256,))
    mod_part(B, 1)
    phase_mix_hyena(B, Mp)
    mod_part(B, 2)
    B.es_aw.close()
    phase_mix_hyena(B, Ms)
    B.es_dft.close()
    with contextlib.ExitStack() as es:
        rwkv_setup(B, es)
        Ms.st0 = es.enter_context(B.st('st0_s', [128, 2, 4, 64]))
        S.dma('sp', Ms.st0[:], B.inp('st0T', [128, 2, 4, 64]), writes=['st0'])
        Ms.ns_out = None
        Mp.st0 = None
        Mp.ns_out = B.outp('ns', [2, 2, 8, 64, 64])
        Mp.nst = es.enter_context(B.st('nst_s', [64, 128]))
        phase_rwkv(B, Mp)
        phase_rwkv(B, Ms)
    with contextlib.ExitStack() as es:
        post_setup(B, es)
        phase_post(B, Ms)
        phase_post(B, Mp)
    S.finish()
    return B


_CACHE = {}


def kernel(**inputs):
    B = build_full()
    inp = {k: np.asarray(v) for k, v in inputs.items()}
    g = shared_inputs(inp)
    in_maps = []
    for core in range(8):
        m = dict(g)
        m.update(core_inputs(inp, core))
        in_maps.append({'i_' + k: np.ascontiguousarray(m[k]) for k in B.din})
    res = run_bass_kernel_spmd(B.nc, in_maps, core_ids=list(range(8)))
    ys = np.stack([np.asarray(r['o_ys'], np.float32) for r in res.results], 0)
    yp = np.concatenate([np.asarray(r['o_yp'], np.float32).reshape(2, 256, D) for r in res.results], 0)
    ns = np.concatenate([np.asarray(r['o_ns'], np.float32) for r in res.results], 0)[:, None]
    return (np.ascontiguousarray(yp), np.ascontiguousarray(ys), np.ascontiguousarray(ns))
```
